# Optimizing a Trainium2 kernel written in Bass

```python
import math
import jax, jax.numpy as jnp
from jax import lax
import numpy as np

D_MODEL = 2048
BATCH = 2
SEQ = 8192
DEPTH = 4

GRID_W = 64
CTX_LEN = 256
HEAD_DIM = 64
GROUP_W = D_MODEL // 4
N_MOD = 9
D_FF = 5632
NORM_EPS = 1e-6
ROPE_BASE = 10000.0
NEG_INF = -1e30
F32 = jnp.float32

NA_HEADS = GROUP_W // HEAD_DIM
NA_WIN_R = 8
NA_WIN_C = 16
NA_QBLOCK = 128
RW_HEADS = GROUP_W // HEAD_DIM
RW_N = HEAD_DIM
RW_DECAY_RANK = 32
RW_ICLR_RANK = 32
RW_GATE_RANK = 96
RW_GN_EPS = 64e-5
WA_HEADS = GROUP_W // HEAD_DIM
WA_KV_HEADS = 2
WA_GROUP = WA_HEADS // WA_KV_HEADS
WA_WINDOW = 128
WA_BLOCK = 128
S5_P = 16
S5_GROUPS = GROUP_W // S5_P
S5_N = 64

IN_SPLITS = (GROUP_W, GROUP_W, GROUP_W,
             GROUP_W, GROUP_W, GROUP_W, RW_DECAY_RANK, RW_ICLR_RANK, RW_GATE_RANK,
             WA_HEADS * HEAD_DIM, WA_KV_HEADS * HEAD_DIM, WA_KV_HEADS * HEAD_DIM,
             GROUP_W)
D_IN = sum(IN_SPLITS)

kernel_name = "hybrid_parallel_group_diffusion_trunk"


def rmsnorm(x, g, eps=NORM_EPS):
    xf = x.astype(F32)
    y = xf * lax.rsqrt(jnp.mean(xf * xf, axis=-1, keepdims=True) + eps)
    return (y * g.astype(F32)).astype(x.dtype)


def modulate(x, g, shift, scale):
    return rmsnorm(x, g) * (1 + scale) + shift


def swiglu(h, wg, wu, wd):
    return (jax.nn.silu(h @ wg) * (h @ wu)) @ wd


def centred_conv3(x, w):
    xp = jnp.pad(x, ((0, 0), (1, 1), (0, 0)))
    return xp[:, :-2] * w[0] + xp[:, 1:-1] * w[1] + xp[:, 2:] * w[2]


def axial_rope_tables(n_tokens):
    t = jnp.arange(n_tokens)
    nf = HEAD_DIM // 4
    inv = 1.0 / (ROPE_BASE ** (jnp.arange(nf, dtype=F32) / nf))

    def ang(p):
        a = p.astype(F32)[:, None] * inv[None, :]
        return jnp.concatenate([a, a], -1)

    a = jnp.concatenate([ang(t // GRID_W), ang(t % GRID_W)], -1)
    return jnp.cos(a), jnp.sin(a)


def _rot_half(t):
    h = t.shape[-1] // 2
    return jnp.concatenate([-t[..., h:], t[..., :h]], -1)


def apply_axial_rope(x, cos, sin):
    half = HEAD_DIM // 2
    cos, sin = cos.astype(x.dtype), sin.astype(x.dtype)
    xr, xc = x[..., :half], x[..., half:]
    return jnp.concatenate([xr * cos[..., :half] + _rot_half(xr) * sin[..., :half],
                            xc * cos[..., half:] + _rot_half(xc) * sin[..., half:]], -1)


def context_attention(q, k, v, sink=None):
    s = jnp.einsum('bqkgd,bckd->bkgqc', q, k).astype(F32) * q.shape[-1] ** -0.5
    if sink is not None:
        s = jnp.concatenate([s, jnp.broadcast_to(sink.astype(F32)[None, :, :, None, None], s.shape[:-1] + (1,))], -1)
    p = jax.nn.softmax(s, axis=-1)[..., :k.shape[1]].astype(v.dtype)
    return jnp.einsum('bkgqc,bckd->bqkgd', p, v)


def neighbourhood_indices(n_tokens):
    rows = n_tokens // GRID_W
    kr = min(NA_WIN_R, rows)
    t = jnp.arange(n_tokens)
    r, c = t // GRID_W, t % GRID_W
    rs = jnp.clip(r - kr // 2, 0, rows - kr)
    cs = jnp.clip(c - NA_WIN_C // 2, 0, GRID_W - NA_WIN_C)
    key_r = rs[:, None, None] + jnp.arange(kr)[None, :, None]
    key_c = cs[:, None, None] + jnp.arange(NA_WIN_C)[None, None, :]
    idx = (key_r * GRID_W + key_c).reshape(n_tokens, kr * NA_WIN_C)
    off_r = key_r - r[:, None, None] + (NA_WIN_R - 1)
    off_c = key_c - c[:, None, None] + (NA_WIN_C - 1)
    bidx = (off_r * (2 * NA_WIN_C - 1) + off_c).reshape(n_tokens, kr * NA_WIN_C)
    return idx, bidx


def neighbourhood_attention(q, k, v, kc, vc, rpb):
    bsz, n, h, dh = q.shape
    idx, bidx = neighbourhood_indices(n)
    nk = idx.shape[1]
    nb = n // NA_QBLOCK
    qb = jnp.moveaxis(q.reshape(bsz, nb, NA_QBLOCK, h, dh), 1, 0)
    idxb = idx.reshape(nb, NA_QBLOCK, nk)
    bidxb = bidx.reshape(nb, NA_QBLOCK, nk)
    scale = dh ** -0.5
    rpb32 = rpb.astype(F32)

    def block(args):
        qq, ii, bi = args
        kk, vv = k[:, ii], v[:, ii]
        s_n = jnp.einsum('bqhd,bqkhd->bhqk', qq, kk).astype(F32) * scale + rpb32[:, bi]
        s_c = jnp.einsum('bqhd,bchd->bhqc', qq, kc).astype(F32) * scale
        p = jax.nn.softmax(jnp.concatenate([s_n, s_c], -1), axis=-1).astype(v.dtype)
        return (jnp.einsum('bhqk,bqkhd->bqhd', p[..., :nk], vv)
                + jnp.einsum('bhqc,bchd->bqhd', p[..., nk:], vc))

    out = lax.map(block, (qb, idxb, bidxb))
    return jnp.moveaxis(out, 0, 1).reshape(bsz, n, h, dh)


def mixer_neighbourhood(pa, pac, q_g, k_g, rpb, want_ctx):
    def qkv(p):
        q, k, v = (t.reshape(*t.shape[:-1], NA_HEADS, HEAD_DIM) for t in p)
        return rmsnorm(q, q_g), rmsnorm(k, k_g), v

    q, k, v = qkv(pa)
    qc, kc, vc = qkv(pac)
    o = neighbourhood_attention(q, k, v, kc, vc, rpb)
    o = o.reshape(*o.shape[:2], GROUP_W)
    oc = context_attention(qc[:, :, :, None], kc, vc).reshape(*qc.shape[:2], GROUP_W) if want_ctx else None
    return o, oc


def _heads(t):
    return t.astype(F32).reshape(*t.shape[:-1], RW_HEADS, RW_N)


def rwkv7_prepare(p, conv_w, w0, w_up, a0, a_up, g_up, k_k, k_a):
    r, k, v, dw, da, dg = p
    r, k, v = jnp.split(centred_conv3(jnp.concatenate([r, k, v], -1), conv_w), 3, axis=-1)
    dwt = jnp.tanh(dw.astype(F32))
    daf = da.astype(F32)
    g = jax.nn.sigmoid(dg.astype(F32)) @ g_up.astype(F32)
    kf = _heads(k)
    kk = kf * _heads(k_k)
    kk = kk / jnp.maximum(jnp.sqrt(jnp.sum(kk * kk, -1, keepdims=True)), 1e-12)
    k_a_h = _heads(k_a)
    dirs = []
    for d in range(2):
        wlog = -jax.nn.softplus(-(w0[d].astype(F32) + dwt @ w_up[d].astype(F32))) - 0.5
        decay = _heads(jnp.exp(-jnp.exp(wlog)))
        a = _heads(jax.nn.sigmoid(a0[d].astype(F32) + daf @ a_up[d].astype(F32)))
        dirs.append((decay, kf * (1 + (a - 1) * k_a_h), a))
    return _heads(r), _heads(v), kk, g, dirs


def rwkv7_scan(r, w, k, v, kk, a, s0, reverse, want_y):
    xs = tuple(jnp.moveaxis(t, 1, 0) for t in (r, w, k, v, kk, a))

    def step(S, inp):
        rt, wt, kt, vt, kkt, at = inp
        sa = jnp.einsum('bhvk,bhk->bhv', S, -kkt)
        S = S * wt[:, :, None, :] + sa[..., None] * (kkt * at)[:, :, None, :] + vt[..., None] * kt[:, :, None, :]
        return S, (jnp.einsum('bhvk,bhk->bhv', S, rt) if want_y else None)

    S, ys = lax.scan(step, s0, xs, reverse=reverse)
    return S, (jnp.moveaxis(ys, 0, 1) if want_y else None)


def rwkv7_readout(r, v, g, k_sum, y, r_k, gn_w, gn_b, dtype):
    y = y + jnp.sum(r * k_sum * r_k.astype(F32), -1, keepdims=True) * v
    mu = jnp.mean(y, -1, keepdims=True)
    var = jnp.mean(jnp.square(y - mu), -1, keepdims=True)
    yn = ((y - mu) * lax.rsqrt(var + RW_GN_EPS)).reshape(*y.shape[:2], GROUP_W)
    return ((yn * gn_w.astype(F32) + gn_b.astype(F32)) * g).astype(dtype)


def mixer_rwkv7(pb, pbc, conv_w, w0, w_up, a0, a_up, g_up, k_k, k_a, r_k, gn_w, gn_b, want_ctx):
    prm = (conv_w, w0, w_up, a0, a_up, g_up, k_k, k_a)
    r, v, kk, g, dirs = rwkv7_prepare(pb, *prm)
    rc, vc, kkc, gc, dirs_c = rwkv7_prepare(pbc, *prm)
    zero = jnp.zeros((r.shape[0], RW_HEADS, RW_N, RW_N), F32)
    ys, ycs = [], []
    for d in range(2):
        wc_, kc_, ac_ = dirs_c[d]
        s_ctx, yc = rwkv7_scan(rc, wc_, kc_, vc, kkc, ac_, zero, d == 1, want_ctx)
        w_, k_, a_ = dirs[d]
        _, y = rwkv7_scan(r, w_, k_, v, kk, a_, s_ctx, d == 1, True)
        ys.append(y)
        ycs.append(yc)
    dtype = pb[0].dtype
    o = rwkv7_readout(r, v, g, dirs[0][1] + dirs[1][1], ys[0] + ys[1], r_k, gn_w, gn_b, dtype)
    oc = (rwkv7_readout(rc, vc, gc, dirs_c[0][1] + dirs_c[1][1], ycs[0] + ycs[1], r_k, gn_w, gn_b, dtype)
          if want_ctx else None)
    return o, oc


def window_attention(q, k, v, kc, vc, sink):
    bsz, n, hk, g, dh = q.shape
    nb = n // WA_BLOCK
    pad = ((0, 0), (WA_BLOCK, WA_BLOCK), (0, 0), (0, 0))

    def bands(t):
        tp = jnp.pad(t, pad)
        return jnp.concatenate([tp[:, i * WA_BLOCK:i * WA_BLOCK + n].reshape(bsz, nb, WA_BLOCK, hk, dh)
                                for i in range(3)], axis=2)

    kb, vb = bands(k), bands(v)
    qb = q.reshape(bsz, nb, WA_BLOCK, hk, g, dh)
    scale = dh ** -0.5
    s_w = jnp.einsum('bnqkgd,bnskd->bnkgqs', qb, kb).astype(F32) * scale
    rel = jnp.arange(3 * WA_BLOCK)[None, :] - WA_BLOCK - jnp.arange(WA_BLOCK)[:, None]
    kpos = jnp.arange(nb)[:, None, None] * WA_BLOCK - WA_BLOCK + jnp.arange(3 * WA_BLOCK)[None, None, :]
    valid = (jnp.abs(rel) <= WA_WINDOW)[None] & (kpos >= 0) & (kpos < n)
    s_w = jnp.where(valid[None, :, None, None], s_w, NEG_INF)
    s_c = jnp.einsum('bnqkgd,bckd->bnkgqc', qb, kc).astype(F32) * scale
    s_s = jnp.broadcast_to(sink.astype(F32)[None, None, :, :, None, None], s_w.shape[:-1] + (1,))
    p = jax.nn.softmax(jnp.concatenate([s_w, s_c, s_s], -1), axis=-1).astype(v.dtype)
    nw, nc = 3 * WA_BLOCK, kc.shape[1]
    o = (jnp.einsum('bnkgqs,bnskd->bnqkgd', p[..., :nw], vb)
         + jnp.einsum('bnkgqc,bckd->bnqkgd', p[..., nw:nw + nc], vc))
    return o.reshape(bsz, n, hk * g * dh)


def mixer_window(pc, pcc, q_g, k_g, sink, cos, sin, want_ctx):
    def qkv(p):
        q, k, v = p
        q = rmsnorm(q.reshape(*q.shape[:-1], WA_KV_HEADS, WA_GROUP, HEAD_DIM), q_g)
        k = rmsnorm(k.reshape(*k.shape[:-1], WA_KV_HEADS, HEAD_DIM), k_g)
        return q, k, v.reshape(*v.shape[:-1], WA_KV_HEADS, HEAD_DIM)

    q, k, v = qkv(pc)
    qc, kc, vc = qkv(pcc)
    q = apply_axial_rope(q, cos[None, :, None, None], sin[None, :, None, None])
    k = apply_axial_rope(k, cos[None, :, None], sin[None, :, None])
    sink2 = sink.reshape(WA_KV_HEADS, WA_GROUP)
    o = window_attention(q, k, v, kc, vc, sink2)
    oc = context_attention(qc, kc, vc, sink2).reshape(*qc.shape[:2], GROUP_W) if want_ctx else None
    return o, oc


def _cmul(ar, ai, br, bi):
    return ar * br - ai * bi, ar * bi + ai * br


def s5_discretise(a_re, a_im, log_dt):
    are, aim = a_re.astype(F32), a_im.astype(F32)
    dt = jnp.exp(log_dt.astype(F32))[:, None]
    mag = jnp.exp(dt * are)
    ang = dt * aim
    ab_re, ab_im = mag * jnp.cos(ang), mag * jnp.sin(ang)
    den = are * are + aim * aim
    nr = ab_re - 1.0
    return ab_re, ab_im, (nr * are + ab_im * aim) / den, (ab_im * are - nr * aim) / den


def s5_scan(bu_re, bu_im, disc, s0, reverse):
    ab_re, ab_im, cf_re, cf_im = disc
    b_re, b_im = _cmul(cf_re, cf_im, bu_re, bu_im)
    if s0 is not None:
        i = -1 if reverse else 0
        i_re, i_im = _cmul(ab_re, ab_im, s0[0], s0[1])
        b_re = b_re.at[:, i].add(i_re)
        b_im = b_im.at[:, i].add(i_im)
    a_re = jnp.broadcast_to(ab_re, b_re.shape)
    a_im = jnp.broadcast_to(ab_im, b_im.shape)

    def combine(e1, e2):
        a1r, a1i, b1r, b1i = e1
        a2r, a2i, b2r, b2i = e2
        ar, ai = _cmul(a2r, a2i, a1r, a1i)
        br, bi = _cmul(a2r, a2i, b1r, b1i)
        return ar, ai, br + b2r, bi + b2i

    _, _, x_re, x_im = lax.associative_scan(combine, (a_re, a_im, b_re, b_im), reverse=reverse, axis=1)
    return x_re, x_im


def mixer_s5(u, uc, a_re, a_im, log_dt, b_re, b_im, c_re, c_im, d_skip, glu_w, glu_b, want_ctx):
    discs = [s5_discretise(a_re[d], a_im[d], log_dt[d]) for d in range(2)]
    bre, bim = b_re.astype(F32), b_im.astype(F32)
    cre, cim = c_re.astype(F32), c_im.astype(F32)
    dsk = d_skip.astype(F32).reshape(S5_GROUPS, S5_P)

    def project(t):
        tg = t.astype(F32).reshape(*t.shape[:-1], S5_GROUPS, S5_P)
        return tg, jnp.einsum('gnp,btgp->btgn', bre, tg), jnp.einsum('gnp,btgp->btgn', bim, tg)

    def readout(tg, x_re, x_im):
        y = (jnp.einsum('gpn,btgn->btgp', cre, x_re) - jnp.einsum('gpn,btgn->btgp', cim, x_im) + dsk * tg)
        y = jax.nn.gelu(y.reshape(*y.shape[:2], GROUP_W)).astype(u.dtype)
        return y * jax.nn.sigmoid(y @ glu_w + glu_b)

    ug, bu_re, bu_im = project(u)
    ucg, buc_re, buc_im = project(uc)
    xf_c = s5_scan(buc_re, buc_im, discs[0], None, False)
    xb_c = s5_scan(buc_re, buc_im, discs[1], None, True)
    xf = s5_scan(bu_re, bu_im, discs[0], (xf_c[0][:, -1], xf_c[1][:, -1]), False)
    xb = s5_scan(bu_re, bu_im, discs[1], (xb_c[0][:, 0], xb_c[1][:, 0]), True)
    o = readout(ug, xf[0] + xb[0], xf[1] + xb[1])
    oc = readout(ucg, xf_c[0] + xb_c[0], xf_c[1] + xb_c[1]) if want_ctx else None
    return o, oc


def setup_inputs(seed: int = 0) -> dict:
    key = jax.random.key(seed)
    ks = iter(jax.random.split(key, 48))
    nrm = lambda shape, s=1.0: s * jax.random.normal(next(ks), shape, F32)
    gain = lambda shape: 1.0 + nrm(shape, 0.05)
    D, L = D_MODEL, DEPTH
    n_idx = jnp.arange(S5_N, dtype=F32)
    conv_base = jnp.array([0.25, 0.5, 0.25], F32)[None, :, None]
    return {
        "x": nrm((BATCH, SEQ, D)),
        "c": nrm((BATCH, D)),
        "ctx": nrm((BATCH, CTX_LEN, D)),
        "c_ctx": nrm((D,)),
        "w_ada": nrm((L, D, N_MOD * D), 0.5 * D ** -0.5),
        "b_ada": nrm((L, N_MOD * D), 0.02),
        "norm_g": gain((L, 3, D)),
        "ffn1_wg": nrm((L, D, D_FF), D ** -0.5),
        "ffn1_wu": nrm((L, D, D_FF), D ** -0.5),
        "ffn1_wd": nrm((L, D_FF, D), D_FF ** -0.5),
        "ffn2_wg": nrm((L, D, D_FF), D ** -0.5),
        "ffn2_wu": nrm((L, D, D_FF), D ** -0.5),
        "ffn2_wd": nrm((L, D_FF, D), D_FF ** -0.5),
        "w_in": nrm((L, D, D_IN), D ** -0.5),
        "w_out": nrm((L, D, D), D ** -0.5),
        "na_q_g": gain((L, HEAD_DIM)),
        "na_k_g": gain((L, HEAD_DIM)),
        "na_rpb": nrm((L, NA_HEADS, (2 * NA_WIN_R - 1) * (2 * NA_WIN_C - 1)), 0.1),
        "rw_conv": conv_base + nrm((L, 3, 3 * GROUP_W), 0.05),
        "rw_w0": jax.random.uniform(next(ks), (L, 2, GROUP_W), F32, -7.0, -1.0),
        "rw_w_up": nrm((L, 2, RW_DECAY_RANK, GROUP_W), 0.1),
        "rw_a0": nrm((L, 2, GROUP_W), 0.1),
        "rw_a_up": nrm((L, 2, RW_ICLR_RANK, GROUP_W), 0.1),
        "rw_g_up": nrm((L, RW_GATE_RANK, GROUP_W), RW_GATE_RANK ** -0.5),
        "rw_k_k": 0.85 + nrm((L, GROUP_W), 0.05),
        "rw_k_a": gain((L, GROUP_W)),
        "rw_r_k": nrm((L, RW_HEADS, RW_N), 0.1),
        "rw_gn_w": gain((L, GROUP_W)),
        "rw_gn_b": nrm((L, GROUP_W), 0.02),
        "wa_q_g": gain((L, HEAD_DIM)),
        "wa_k_g": gain((L, HEAD_DIM)),
        "wa_sink": nrm((L, WA_HEADS), 0.5),
        "s5_a_re": -0.5 + nrm((L, 2, S5_GROUPS, S5_N), 0.01),
        "s5_a_im": math.pi * n_idx + nrm((L, 2, S5_GROUPS, S5_N), 0.01),
        "s5_log_dt": jax.random.uniform(next(ks), (L, 2, S5_GROUPS), F32, math.log(1e-3), math.log(1e-1)),
        "s5_b_re": nrm((L, S5_GROUPS, S5_N, S5_P), (2 * S5_P) ** -0.5),
        "s5_b_im": nrm((L, S5_GROUPS, S5_N, S5_P), (2 * S5_P) ** -0.5),
        "s5_c_re": nrm((L, S5_GROUPS, S5_P, S5_N), S5_N ** -0.5),
        "s5_c_im": nrm((L, S5_GROUPS, S5_P, S5_N), S5_N ** -0.5),
        "s5_d": nrm((L, GROUP_W)),
        "s5_glu_w": nrm((L, GROUP_W, GROUP_W), GROUP_W ** -0.5),
        "s5_glu_b": nrm((L, GROUP_W), 0.02),
    }


def reference(x, c, ctx, c_ctx, w_ada, b_ada, norm_g, ffn1_wg, ffn1_wu, ffn1_wd, ffn2_wg, ffn2_wu, ffn2_wd,
              w_in, w_out, na_q_g, na_k_g, na_rpb, rw_conv, rw_w0, rw_w_up, rw_a0, rw_a_up, rw_g_up,
              rw_k_k, rw_k_a, rw_r_k, rw_gn_w, rw_gn_b, wa_q_g, wa_k_g, wa_sink, s5_a_re, s5_a_im,
              s5_log_dt, s5_b_re, s5_b_im, s5_c_re, s5_c_im, s5_d, s5_glu_w, s5_glu_b):
    cos, sin = axial_rope_tables(x.shape[1])
    cuts = np.cumsum(IN_SPLITS)[:-1].tolist()
    xc = ctx
    for l in range(DEPTH):
        want_ctx = l < DEPTH - 1
        mod = jnp.split((jax.nn.silu(c) @ w_ada[l] + b_ada[l])[:, None, :], N_MOD, axis=-1)
        mod_c = jnp.split((jax.nn.silu(c_ctx) @ w_ada[l] + b_ada[l])[None, None, :], N_MOD, axis=-1)
        x = x + 0.5 * mod[2] * swiglu(modulate(x, norm_g[l, 0], mod[0], mod[1]), ffn1_wg[l], ffn1_wu[l], ffn1_wd[l])
        xc = xc + 0.5 * mod_c[2] * swiglu(modulate(xc, norm_g[l, 0], mod_c[0], mod_c[1]),
                                          ffn1_wg[l], ffn1_wu[l], ffn1_wd[l])
        h = modulate(x, norm_g[l, 1], mod[3], mod[4])
        hc = modulate(xc, norm_g[l, 1], mod_c[3], mod_c[4])
        pa = jnp.split(h @ w_in[l], cuts, axis=-1)
        pc = jnp.split(hc @ w_in[l], cuts, axis=-1)
        o_a, oc_a = mixer_neighbourhood(pa[0:3], pc[0:3], na_q_g[l], na_k_g[l], na_rpb[l], want_ctx)
        o_b, oc_b = mixer_rwkv7(pa[3:9], pc[3:9], rw_conv[l], rw_w0[l], rw_w_up[l], rw_a0[l], rw_a_up[l],
                                rw_g_up[l], rw_k_k[l], rw_k_a[l], rw_r_k[l], rw_gn_w[l], rw_gn_b[l], want_ctx)
        o_c, oc_c = mixer_window(pa[9:12], pc[9:12], wa_q_g[l], wa_k_g[l], wa_sink[l], cos, sin, want_ctx)
        o_d, oc_d = mixer_s5(pa[12], pc[12], s5_a_re[l], s5_a_im[l], s5_log_dt[l], s5_b_re[l], s5_b_im[l],
                             s5_c_re[l], s5_c_im[l], s5_d[l], s5_glu_w[l], s5_glu_b[l], want_ctx)
        x = x + mod[5] * (jnp.concatenate([o_a, o_b, o_c, o_d], axis=-1) @ w_out[l])
        x = x + 0.5 * mod[8] * swiglu(modulate(x, norm_g[l, 2], mod[6], mod[7]), ffn2_wg[l], ffn2_wu[l], ffn2_wd[l])
        if want_ctx:
            xc = xc + mod_c[5] * (jnp.concatenate([oc_a, oc_b, oc_c, oc_d], axis=-1) @ w_out[l])
            xc = xc + 0.5 * mod_c[8] * swiglu(modulate(xc, norm_g[l, 2], mod_c[6], mod_c[7]),
                                              ffn2_wg[l], ffn2_wu[l], ffn2_wd[l])
    return x
```

```python
import numpy as np
from contextlib import ExitStack
import concourse.bass as bass
import concourse.mybir as mybir
from concourse.bass_utils import run_bass_kernel_spmd

F32 = mybir.dt.float32
ALU = mybir.AluOpType
AF = mybir.ActivationFunctionType
AX = mybir.AxisListType

D = 2048
DFF = 5632
DIN = 4512
NMOD = 9
EPS = 1e-6


class _Op:
    __slots__ = ("eng", "fn", "deps", "inc", "seq", "dsem", "dval")

    def __init__(self, eng, fn):
        self.eng, self.fn, self.deps, self.inc, self.seq = eng, fn, [], False, 0
        self.dsem, self.dval = None, 0


class Prog:
    ENG = ("pe", "act", "dve", "pool", "sp")

    def __init__(self):
        self.nc = bass.Bass("TRN2", target_bir_lowering=False)
        self.es = ExitStack()
        self.ops = {e: [] for e in self.ENG}
        self.last_w = {}
        self.readers = {}
        self.dma_sems = {}
        self.dma_cnt = {}
        self.n = 0
        self.out_dmas = []

    def din(self, name, shape):
        return self.nc.dram_tensor(name, list(shape), F32, kind="ExternalInput").ap()

    def dout(self, name, shape):
        return self.nc.dram_tensor(name, list(shape), F32, kind="ExternalOutput").ap()

    def sb(self, shape, dtype=F32):
        self.n += 1
        return self.es.enter_context(self.nc.sbuf_tensor("sb%d" % self.n, list(shape), dtype))

    def ps(self, shape, dtype=F32):
        self.n += 1
        return self.es.enter_context(self.nc.psum_tensor("ps%d" % self.n, list(shape), dtype))

    def op(self, eng, fn, r=(), w=()):
        o = _Op(eng, fn)
        deps = []
        for k in r:
            lw = self.last_w.get(k)
            if lw is not None:
                deps.append(lw)
            if isinstance(k, tuple) and k[0] == "bank":
                for rd in self.readers.get(k, {}).values():
                    if rd.eng != eng:
                        deps.append(rd)
        for k in w:
            lw = self.last_w.get(k)
            if lw is not None:
                deps.append(lw)
            for rd in self.readers.get(k, {}).values():
                deps.append(rd)
        for d in deps:
            if d is o:
                continue
            if d.eng == eng and eng != "sp":
                if not any(self.last_w.get(k) is d for k in r):
                    continue
            o.deps.append(d)
            if d.eng != "sp":
                d.inc = True
        for k in r:
            self.readers.setdefault(k, {})[eng] = o
        for k in w:
            self.last_w[k] = o
            self.readers[k] = {}
        self.ops[eng].append(o)
        return o

    def dma(self, out, in_, semkey, r=(), w=(), is_out=False, eng="sp"):
        o = self.op(eng, lambda e: e.dma_start(out=out, in_=in_), r=r, w=w)
        if semkey not in self.dma_sems:
            self.dma_sems[semkey] = self.es.enter_context(self.nc.semaphore("dq%d" % len(self.dma_sems)))
            self.dma_cnt[semkey] = 0
        self.dma_cnt[semkey] += 16
        o.dsem, o.dval = self.dma_sems[semkey], self.dma_cnt[semkey]
        if is_out:
            self.out_dmas.append(o)
        return o

    def finish(self):
        nc = self.nc
        esem = {e: self.es.enter_context(nc.semaphore("e_" + e)) for e in self.ENG}
        for e in self.ENG:
            c = 0
            for o in self.ops[e]:
                if o.dsem is None and o.inc:
                    c += 1
                    o.seq = c
        ops, out_dmas = self.ops, self.out_dmas

        def emit(ename, e):
            seen = {}
            for o in ops[ename]:
                need = {}
                for d in o.deps:
                    if d.dsem is not None:
                        s, v = d.dsem, d.dval
                    else:
                        s, v = esem[d.eng], d.seq
                    key = id(s)
                    if need.get(key, (None, 0))[1] < v:
                        need[key] = (s, v)
                for key, (s, v) in need.items():
                    if seen.get(key, 0) < v:
                        e.wait_ge(s, v)
                        seen[key] = v
                ins = o.fn(e)
                if o.dsem is not None:
                    ins.then_inc(o.dsem, 16)
                elif o.inc:
                    ins.then_inc(esem[ename], 1)
            if ename == "sp":
                fin = {}
                for o in out_dmas:
                    if fin.get(id(o.dsem), (None, 0))[1] < o.dval:
                        fin[id(o.dsem)] = (o.dsem, o.dval)
                for s, v in fin.values():
                    e.wait_ge(s, v)

        with nc.Block() as block:
            @block.tensor
            def _(e):
                emit("pe", e)

            @block.scalar
            def _(e):
                emit("act", e)

            @block.vector
            def _(e):
                emit("dve", e)

            @block.gpsimd
            def _(e):
                emit("pool", e)

            @block.sync
            def _(e):
                emit("sp", e)
        self.es.close()
        return nc


NT = 512
KC = D // 128


class TL:
    def __init__(self, P):
        self.P = P
        self.x = P.sb([128, KC, NT])
        self.h = P.sb([128, KC, NT])
        self.a = P.sb([128, 22, NT])
        self.wa = [P.sb([128, KC, 128]) for _ in range(2)]
        self.wb = [P.sb([128, KC, 128]) for _ in range(2)]
        self.wd = [P.sb([128, 22, 128]) for _ in range(2)]
        self.sq = [P.sb([128, NT]) for _ in range(2)]
        self.rstd = P.sb([128, NT])
        self.tmp = [P.sb([128, NT]) for _ in range(2)]
        self.ones = P.sb([128, 128])
        self.gs = P.sb([128, KC])
        self.ps_ss = P.ps([128, NT])
        self.ps_g = [P.ps([128, NT]) for _ in range(2)]
        self.ps_u = [P.ps([128, NT]) for _ in range(2)]
        self.ps_o = [P.ps([128, NT]) for _ in range(2)]
        self.cnt = 0
        P.op("pool", lambda e: e.memset(self.ones[:], 1.0), w=["ones"])

    def xkeys(self):
        return [("x", k) for k in range(KC)]

    def hkeys(self):
        return [("h", k) for k in range(KC)]

    def rms_modulate(self, g_ap, scale_ap, shift_ap, vkey):
        P = self.P
        x, h = self.x, self.h
        for k in range(KC):
            s = self.sq[k % 2]
            P.op("act", lambda e, s=s, k=k: e.activation(out=s[:], in_=x[:, k, :], func=AF.Square),
                 r=[("x", k)], w=[("sq", k % 2)])
            P.op("pe", lambda e, s=s, k=k: e.matmul(self.ps_ss[:], lhsT=self.ones[:], rhs=s[:],
                                                     start=(k == 0), stop=(k == KC - 1)),
                 r=["ones", ("sq", k % 2)], w=["ps_ss"])
        t0 = self.tmp[0]
        P.op("act", lambda e: e.activation(out=t0[:], in_=self.ps_ss[:], func=AF.Sqrt, scale=1.0 / D, bias=EPS),
             r=["ps_ss"], w=[("tmp", 0)])
        P.op("dve", lambda e: e.reciprocal(out=self.rstd[:], in_=t0[:]), r=[("tmp", 0)], w=["rstd"])
        P.op("dve", lambda e: e.scalar_tensor_tensor(out=self.gs[:], in0=scale_ap, scalar=1.0, in1=g_ap,
                                                      op0=ALU.add, op1=ALU.mult), r=[vkey], w=["gs"])
        for k in range(KC):
            t = self.tmp[k % 2]
            P.op("dve", lambda e, t=t, k=k: e.tensor_tensor(out=t[:], in0=x[:, k, :], in1=self.rstd[:], op=ALU.mult),
                 r=[("x", k), "rstd"], w=[("tmp", k % 2)])
            P.op("dve", lambda e, t=t, k=k: e.tensor_scalar(out=h[:, k, :], in0=t[:], scalar1=self.gs[:, k:k + 1],
                                                             scalar2=shift_ap[:, k:k + 1], op0=ALU.mult, op1=ALU.add),
                 r=[("tmp", k % 2), "gs", vkey], w=[("h", k)])

    def load_w(self, buf, key, src):
        m = src.shape[1]
        self.P.dma(buf[:, :, 0:m], src.rearrange("(k p) m -> p k m", p=128), semkey=key, w=[key])

    def ffn(self, wg, wu, wd, gate_ap, vkey):
        P = self.P
        x, h, a = self.x, self.h, self.a
        for half in range(2):
            for jj in range(22):
                j = half * 22 + jj
                c = self.cnt
                self.cnt += 1
                wa, wb = self.wa[c % 2], self.wb[c % 2]
                self.load_w(wa, ("wa", c % 2), wg[:, j * 128:(j + 1) * 128])
                self.load_w(wb, ("wb", c % 2), wu[:, j * 128:(j + 1) * 128])
                pg, pu = self.ps_g[c % 2], self.ps_u[c % 2]
                for k in range(KC):
                    P.op("pe", lambda e, k=k, wa=wa, pg=pg: e.matmul(pg[:], lhsT=wa[:, k, :], rhs=h[:, k, :],
                                                                   start=(k == 0), stop=(k == KC - 1)),
                         r=[("wa", c % 2), ("h", k)], w=[("ps_g", c % 2)])
                for k in range(KC):
                    P.op("pe", lambda e, k=k, wb=wb, pu=pu: e.matmul(pu[:], lhsT=wb[:, k, :], rhs=h[:, k, :],
                                                                   start=(k == 0), stop=(k == KC - 1)),
                         r=[("wb", c % 2), ("h", k)], w=[("ps_u", c % 2)])
                s = self.sq[c % 2]
                P.op("act", lambda e, s=s, pg=pg: e.activation(out=s[:], in_=pg[:], func=AF.Silu),
                     r=[("ps_g", c % 2)], w=[("sq", c % 2)])
                P.op("dve", lambda e, s=s, pu=pu, jj=jj: e.tensor_tensor(out=a[:, jj, :], in0=s[:], in1=pu[:], op=ALU.mult),
                     r=[("sq", c % 2), ("ps_u", c % 2)], w=[("a", jj)])
            for m in range(KC):
                c = self.cnt
                self.cnt += 1
                wdb = self.wd[c % 2]
                P.dma(wdb[:, :, :], wd[half * 2816:(half + 1) * 2816, m * 128:(m + 1) * 128].rearrange("(k p) m -> p k m", p=128),
                      semkey=("wd", c % 2), w=[("wd", c % 2)])
                po = self.ps_o[c % 2]
                for jj in range(22):
                    P.op("pe", lambda e, jj=jj, wdb=wdb, po=po: e.matmul(po[:], lhsT=wdb[:, jj, :], rhs=a[:, jj, :],
                                                                       start=(jj == 0), stop=(jj == 21)),
                         r=[("wd", c % 2), ("a", jj)], w=[("ps_o", c % 2)])
                P.op("dve", lambda e, m=m, po=po: e.scalar_tensor_tensor(out=x[:, m, :], in0=po[:], scalar=gate_ap[:, m:m + 1],
                                                                       in1=x[:, m, :], op0=ALU.mult, op1=ALU.add),
                     r=[("ps_o", c % 2), vkey, ("x", m)], w=[("x", m)])

    def proj(self, w, ncols, out_fn):
        P = self.P
        h = self.h
        nch = (ncols + 127) // 128
        for m in range(nch):
            mc = min(128, ncols - m * 128)
            c = self.cnt
            self.cnt += 1
            wa = self.wa[c % 2]
            self.load_w(wa, ("wa", c % 2), w[:, m * 128:m * 128 + mc])
            po = self.ps_o[c % 2]
            for k in range(KC):
                P.op("pe", lambda e, k=k, wa=wa, po=po, mc=mc: e.matmul(po[0:mc, :], lhsT=wa[:, k, 0:mc], rhs=h[:, k, :],
                                                                      start=(k == 0), stop=(k == KC - 1)),
                     r=[("wa", c % 2), ("h", k)], w=[("ps_o", c % 2)])
            out_fn(m, mc, po, ("ps_o", c % 2))


def build_t1(NB):
    P = Prog()
    xT = P.din("xT", [NB, 128, KC, NT])
    mv = P.din("mv", [NB, 128, 5, KC])
    gv = P.din("gv", [128, 2, KC])
    wg = P.din("wg", [D, DFF])
    wu = P.din("wu", [D, DFF])
    wd = P.din("wd", [DFF, D])
    win = P.din("win", [D, DIN])
    xo = P.dout("xo", [NB, 128, KC, NT])
    po_ = P.dout("pT", [NB, DIN, NT])
    T = TL(P)
    mvt = P.sb([128, 5, KC])
    gvt = P.sb([128, 2, KC])
    hg = P.sb([128, KC])
    ot = [P.sb([128, NT]) for _ in range(2)]
    P.dma(gvt[:], gv, semkey="gv", w=["gv"])
    for b in range(NB):
        P.dma(T.x[:], xT[b], semkey="x", w=T.xkeys())
        P.dma(mvt[:], mv[b], semkey="mv", w=["mv"])
        P.op("pool", lambda e: e.tensor_scalar(out=hg[:], in0=mvt[:, 2, :], scalar1=0.5, scalar2=None, op0=ALU.mult),
             r=["mv"], w=["hg"])
        T.rms_modulate(gvt[:, 0, :], mvt[:, 1, :], mvt[:, 0, :], "mv")
        T.ffn(wg, wu, wd, hg, "hg")
        P.dma(xo[b], T.x[:], semkey="xo", r=T.xkeys(), is_out=True)
        T.rms_modulate(gvt[:, 1, :], mvt[:, 4, :], mvt[:, 3, :], "mv")

        def out_fn(m, mc, ps, pskey, b=b):
            o = ot[m % 2]
            P.op("act", lambda e: e.activation(out=o[0:mc, :], in_=ps[0:mc, :], func=AF.Copy),
                 r=[pskey], w=[("ot", m % 2)])
            P.dma(po_[b, m * 128:m * 128 + mc, :], o[0:mc, :], semkey=("ot", m % 2), r=[("ot", m % 2)], is_out=True)
        T.proj(win, DIN, out_fn)
    return P.finish()


def build_t2(NB):
    P = Prog()
    xT = P.din("xT", [NB, 128, KC, NT])
    oT = P.din("oT", [NB, 128, KC, NT])
    mv = P.din("mv", [NB, 128, 4, KC])
    gv = P.din("gv", [128, KC])
    gw = P.din("gw", [512, 512])
    gb = P.din("gb", [128, 4])
    wo = P.din("wo", [D, D])
    wg = P.din("wg", [D, DFF])
    wu = P.din("wu", [D, DFF])
    wd = P.din("wd", [DFF, D])
    xo = P.dout("xo", [NB, 128, KC, NT])
    T = TL(P)
    mvt = P.sb([128, 4, KC])
    gvt = P.sb([128, KC])
    gwt = P.sb([128, 4, 512])
    gbt = P.sb([128, 4])
    hg = P.sb([128, KC])
    P.dma(gvt[:], gv, semkey="gv", w=["gv"])
    P.dma(gwt[:], gw.rearrange("(k p) m -> p k m", p=128), semkey="gw", w=["gw"])
    P.dma(gbt[:], gb, semkey="gb", w=["gb"])
    for b in range(NB):
        P.dma(T.x[:], xT[b], semkey="x", w=T.xkeys())
        P.dma(T.h[:], oT[b], semkey="h", w=T.hkeys())
        P.dma(mvt[:], mv[b], semkey="mv", w=["mv"])
        P.op("pool", lambda e: e.tensor_scalar(out=hg[:], in0=mvt[:, 3, :], scalar1=0.5, scalar2=None, op0=ALU.mult),
             r=["mv"], w=["hg"])
        for m in range(4):
            po = T.ps_g[m % 2]
            for k in range(4):
                P.op("pe", lambda e, m=m, k=k, po=po: e.matmul(po[:], lhsT=gwt[:, k, m * 128:(m + 1) * 128], rhs=T.h[:, 12 + k, :],
                                                             start=(k == 0), stop=(k == 3)),
                     r=["gw", ("h", 12 + k)], w=[("ps_g", m % 2)])
            P.op("act", lambda e, m=m, po=po: e.activation(out=T.a[:, m, :], in_=po[:], func=AF.Sigmoid, bias=gbt[:, m:m + 1]),
                 r=[("ps_g", m % 2), "gb"], w=[("a", m)])
        for m in range(4):
            P.op("dve", lambda e, m=m: e.tensor_tensor(out=T.h[:, 12 + m, :], in0=T.h[:, 12 + m, :], in1=T.a[:, m, :], op=ALU.mult),
                 r=[("h", 12 + m), ("a", m)], w=[("h", 12 + m)])

        def out_fn(m, mc, ps, pskey):
            P.op("dve", lambda e: e.scalar_tensor_tensor(out=T.x[:, m, :], in0=ps[:], scalar=mvt[:, 0, m:m + 1], in1=T.x[:, m, :],
                                                          op0=ALU.mult, op1=ALU.add),
                 r=[pskey, "mv", ("x", m)], w=[("x", m)])
        T.proj(wo, D, out_fn)
        T.rms_modulate(gvt[:, :], mvt[:, 2, :], mvt[:, 1, :], "mv")
        T.ffn(wg, wu, wd, hg, "hg")
        P.dma(xo[b], T.x[:], semkey="xo", r=T.xkeys(), is_out=True)
    return P.finish()


ADA_N = NMOD * D // 8


def build_ada(L):
    P = Prog()
    cT = P.din("cT", [128, KC, 3])
    w = P.din("w", [L, D, ADA_N])
    bb = P.din("b", [L, 3, ADA_N])
    out = P.dout("mod", [L, 3, ADA_N])
    ct = P.sb([128, KC, 3])
    sc = P.sb([128, KC, 3])
    wt = [P.sb([128, KC, 512]) for _ in range(2)]
    bt = P.sb([3, L, ADA_N])
    ot = P.sb([3, L, ADA_N])
    ps = [P.ps([3, 512]) for _ in range(2)]
    P.dma(ct[:], cT, semkey="c", w=["c"])
    P.dma(bt[:], bb.rearrange("l r n -> r l n"), semkey="b", w=["b"])
    P.op("act", lambda e: e.activation(out=sc[:], in_=ct[:], func=AF.Silu), r=["c"], w=["sc"])
    c = 0
    for l in range(L):
        for n0 in range(0, ADA_N, 512):
            nn = min(512, ADA_N - n0)
            wb, pb = wt[c % 2], ps[c % 2]
            P.dma(wb[:, :, 0:nn], w[l, :, n0:n0 + nn].rearrange("(k p) n -> p k n", p=128), semkey=("w", c % 2), w=[("w", c % 2)])
            for k in range(KC):
                P.op("pe", lambda e, k=k, wb=wb, pb=pb, nn=nn: e.matmul(pb[:, 0:nn], lhsT=sc[:, k, :], rhs=wb[:, k, 0:nn],
                                                                      start=(k == 0), stop=(k == KC - 1)),
                     r=["sc", ("w", c % 2)], w=[("ps", c % 2)])
            P.op("dve", lambda e, pb=pb, l=l, n0=n0, nn=nn: e.tensor_tensor(out=ot[:, l, n0:n0 + nn], in0=pb[:, 0:nn],
                                                                           in1=bt[:, l, n0:n0 + nn], op=ALU.add),
                 r=[("ps", c % 2), "b"], w=["ot"])
            c += 1
    P.dma(out.rearrange("l r n -> r l n"), ot[:], semkey="o", r=["ot"], is_out=True)
    return P.finish()


TLAT = 8192
TCTX = 256
TQ = TLAT + TCTX
NQT = TQ // 128
NEG = -1e30


def na_specs():
    types = {0: 0, 1: 1, 62: 3, 63: 4}
    specs = []
    for i in range(64):
        r0 = 2 * i
        lo = min(max(r0 - 4, 0), 120)
        hi = min(max(r0 + 1 - 4, 0), 120) + 7
        nch = (hi - lo + 2) // 2
        ty = types.get(i, 2)
        specs.append(([lo // 2 + c for c in range(nch)], ty))
    return specs


NA_REP = [0, 1, 2, 62, 63]
NA_NCH = [4, 4, 5, 4, 4]


def na_bias_tables(rpb):
    tabs = []
    q = np.arange(128)
    qr, qc = q // 64, q % 64
    for ty, i in enumerate(NA_REP):
        r0 = 2 * i
        lo = min(max(r0 - 4, 0), 120)
        r = r0 + qr
        rs = np.clip(r - 4, 0, 120)
        cs = np.clip(qc - 8, 0, 48)
        for c in range(NA_NCH[ty]):
            kl = np.arange(128)
            kr = lo + 2 * c + kl // 64
            kc = kl % 64
            valid = ((kr[:, None] >= rs[None, :]) & (kr[:, None] <= rs[None, :] + 7) &
                     (kc[:, None] >= cs[None, :]) & (kc[:, None] <= cs[None, :] + 15))
            bidx = (kr[:, None] - r[None, :] + 7) * 31 + (kc[:, None] - qc[None, :] + 15)
            bidx = np.where(valid, bidx, 0)
            t = np.where(valid[None], rpb[:, bidx], np.float32(NEG)).astype(np.float32)
            tabs.append(t)
    return np.stack(tabs, axis=2)


def wa_bias_table():
    j = np.arange(128)[:, None]
    q = np.arange(128)[None, :]
    prev = np.where(j >= q, 0.0, NEG)
    cur = np.zeros((128, 128))
    nxt = np.where(j <= q, 0.0, NEG)
    return np.stack([prev, cur, nxt], axis=1).astype(np.float32)


def rope_tables():
    t = np.arange(TLAT)
    inv = (1.0 / (np.float32(10000.0) ** (np.arange(16, dtype=np.float32) / np.float32(16)))).astype(np.float32)

    def ang(p):
        a = p.astype(np.float32)[:, None] * inv[None, :]
        return np.concatenate([a, a], -1)
    a = np.concatenate([ang(t // 64), ang(t % 64)], -1)
    return np.cos(a).astype(np.float32), np.sin(a).astype(np.float32)


def rot_matrix():
    m = np.zeros((128, 128), np.float32)
    for o in range(128):
        if o % 32 < 16:
            m[o + 16, o] = -1.0
        else:
            m[o - 16, o] = 1.0
    return m


def build_attn(kind):
    rope = kind == "C"
    P = Prog()
    NS = 2 * sum(NA_NCH) if kind == "A" else 3
    qT = P.din("qT", [128, TQ])
    kT = P.din("kT", [128, TQ])
    vtm = P.din("vtm", [TQ, 128])
    gq = P.din("gq", [128, 1])
    gk = P.din("gk", [128, 1])
    btab = P.din("btab", [128, NS, 128])
    if rope:
        cosT = P.din("cosT", [128, TLAT])
        sinT = P.din("sinT", [128, TLAT])
        prot = P.din("prot", [128, 128])
        sink = P.din("sink", [128, 2])
    o_tm = P.dout("o", [TQ, 128])

    qn = P.sb([128, TQ])
    kn = P.sb([128, TQ])
    V1 = P.sb([128, NQT, 2, 65])
    BT = P.sb([128, NS, 128])
    gqt = P.sb([128, 1])
    gkt = P.sb([128, 1])
    bones = P.sb([128, 128])
    raw = [P.sb([128, 512]) for _ in range(2)]
    sq = P.sb([128, 512])
    st = P.sb([128, 512])
    rs = P.sb([128, 512])
    tmpS = [P.sb([128, 640]) for _ in range(2)]
    E = [P.sb([128, 896]) for _ in range(2)]
    ot = [P.sb([128, 128]) for _ in range(2)]
    den = [P.sb([128, 2]) for _ in range(2)]
    banks = [P.ps([128, 512]) for _ in range(8)]

    P.dma(gqt[:], gq, semkey="gq", w=["gq"])
    P.dma(gkt[:], gk, semkey="gk", w=["gk"])
    P.dma(BT[:], btab, semkey="bt", w=["BT"])
    for h in range(2):
        P.dma(V1[:, :, h, 0:64], vtm[:, h * 64:(h + 1) * 64].rearrange("(t p) d -> p t d", p=128), semkey=("v", h), w=["V1"])
    P.op("pool", lambda e: e.memset(V1[:, :, :, 64:65], 1.0), w=["V1"])
    P.op("pool", lambda e: e.memset(bones[:], 0.0), w=["bones"])
    P.op("pool", lambda e: e.memset(bones[0:64, 0:64], 1.0), w=["bones"])
    P.op("pool", lambda e: e.memset(bones[64:128, 64:128], 1.0), w=["bones"])
    if rope:
        prt = P.sb([128, 128])
        cs = [P.sb([128, 512]) for _ in range(2)]
        sn = [P.sb([128, 512]) for _ in range(2)]
        t1 = P.sb([128, 512])
        t2 = P.sb([128, 512])
        skt = P.sb([128, 2])
        ske = P.sb([128, 2])
        P.dma(prt[:], prot, semkey="prot", w=["prot"])
        P.dma(skt[:], sink, semkey="sink", w=["sink"])
        P.op("act", lambda e: e.activation(out=ske[:], in_=skt[:], func=AF.Exp), r=["sink"], w=["ske"])

    c = 0
    for src, dst, g, nm in ((qT, qn, gqt, "qn"), (kT, kn, gkt, "kn")):
        for t0 in range(0, TQ, 512):
            nt = min(512, TQ - t0)
            rw = raw[c % 2]
            P.dma(rw[:, 0:nt], src[:, t0:t0 + nt], semkey=("raw", c % 2), w=[("raw", c % 2)])
            P.op("pool", lambda e, rw=rw, nt=nt: e.tensor_tensor(out=sq[:, 0:nt], in0=rw[:, 0:nt], in1=rw[:, 0:nt], op=ALU.mult),
                 r=[("raw", c % 2)], w=["sq"])
            P.op("pe", lambda e, nt=nt: e.matmul(banks[0][:, 0:nt], lhsT=bones[:], rhs=sq[:, 0:nt], start=True, stop=True),
                 r=["bones", "sq"], w=[("bank", 0)])
            P.op("act", lambda e, nt=nt: e.activation(out=st[:, 0:nt], in_=banks[0][:, 0:nt], func=AF.Sqrt, scale=1.0 / 64, bias=EPS),
                 r=[("bank", 0)], w=["st"])
            P.op("dve", lambda e, nt=nt: e.reciprocal(out=rs[:, 0:nt], in_=st[:, 0:nt]), r=["st"], w=["rs"])
            P.op("dve", lambda e, rw=rw, nt=nt, t0=t0, dst=dst, g=g: e.scalar_tensor_tensor(
                out=dst[:, t0:t0 + nt], in0=rw[:, 0:nt], scalar=g[:, 0:1], in1=rs[:, 0:nt], op0=ALU.mult, op1=ALU.mult),
                r=[("raw", c % 2), "rs", "gq", "gk"], w=[(nm, t0)])
            if rope and t0 < TLAT:
                P.dma(cs[c % 2][:], cosT[:, t0:t0 + 512], semkey=("cs", c % 2), w=[("cs", c % 2)])
                P.dma(sn[c % 2][:], sinT[:, t0:t0 + 512], semkey=("sn", c % 2), w=[("sn", c % 2)])
                P.op("pe", lambda e, t0=t0, dst=dst: e.matmul(banks[1][:], lhsT=prt[:], rhs=dst[:, t0:t0 + 512], start=True, stop=True),
                     r=["prot", (nm, t0)], w=[("bank", 1)])
                P.op("dve", lambda e, t0=t0, dst=dst, cc=cs[c % 2]: e.tensor_tensor(out=t1[:], in0=dst[:, t0:t0 + 512], in1=cc[:], op=ALU.mult),
                     r=[(nm, t0), ("cs", c % 2)], w=["t1"])
                P.op("dve", lambda e, ss=sn[c % 2]: e.tensor_tensor(out=t2[:], in0=banks[1][:], in1=ss[:], op=ALU.mult),
                     r=[("bank", 1), ("sn", c % 2)], w=["t2"])
                P.op("pool", lambda e, t0=t0, dst=dst: e.tensor_tensor(out=dst[:, t0:t0 + 512], in0=t1[:], in1=t2[:], op=ALU.add),
                     r=["t1", "t2"], w=[(nm, t0)])
            c += 1

    def nkeys(nm, tile):
        return [(nm, (tile * 128) // 512 * 512)]

    if kind == "A":
        specs = na_specs()
        base = np.concatenate([[0], np.cumsum(NA_NCH)])[:5]
        ntab = sum(NA_NCH)
    it = 0
    for i in range(NQT):
        if i < 64:
            if kind == "A":
                ktiles, ty = specs[i]
            else:
                ktiles = [k for k in (i - 1, i, i + 1) if 0 <= k < 64]
        else:
            ktiles = []
        nW = len(ktiles)
        par = i % 2
        po = banks[6 + par]
        for h in range(2):
            hs = slice(h * 64, (h + 1) * 64)
            ip = it % 2
            it += 1
            bA, bB = banks[2 + 2 * ip], banks[3 + 2 * ip]
            if kind == "A":
                s0 = (h * ntab + int(base[ty])) if nW else 0
            else:
                s0 = 1 if i == 0 else 0

            def sl(s):
                return (bA, s * 128) if s < 4 else (bB, (s - 4) * 128)
            chunks = [(kt, s) for s, kt in enumerate(ktiles)] + [(64, 5), (65, 6)]
            for kt, s in chunks:
                bk, off = sl(s)
                P.op("pe", lambda e, bk=bk, off=off, kt=kt, hs=hs, i=i: e.matmul(
                    bk[:, off:off + 128], lhsT=kn[hs, kt * 128:(kt + 1) * 128], rhs=qn[hs, i * 128:(i + 1) * 128], start=True, stop=True),
                    r=nkeys("kn", kt) + nkeys("qn", i), w=[("bank", 2 + 2 * ip + (0 if s < 4 else 1))])
            nA = min(nW, 4)
            Eb, tS = E[ip], tmpS[ip]
            if nA:
                P.op("dve", lambda e, nA=nA, s0=s0, tS=tS, bA=bA: e.scalar_tensor_tensor(
                    out=tS[:, 0:nA * 128], in0=bA[:, 0:nA * 128], scalar=0.125, in1=BT[:, s0:s0 + nA, :].rearrange("p s q -> p (s q)"),
                    op0=ALU.mult, op1=ALU.add), r=[("bank", 2 + 2 * ip), "BT"], w=[("tS", ip)])
            if nW == 5:
                P.op("dve", lambda e, s0=s0, tS=tS, bB=bB: e.scalar_tensor_tensor(
                    out=tS[:, 512:640], in0=bB[:, 0:128], scalar=0.125, in1=BT[:, s0 + 4, :],
                    op0=ALU.mult, op1=ALU.add), r=[("bank", 3 + 2 * ip), "BT"], w=[("tS", ip)])
            if nW:
                P.op("act", lambda e, nW=nW, Eb=Eb, tS=tS: e.activation(out=Eb[:, 0:nW * 128], in_=tS[:, 0:nW * 128], func=AF.Exp),
                     r=[("tS", ip)], w=[("E", ip)])
            P.op("act", lambda e, Eb=Eb, bB=bB: e.activation(out=Eb[:, 640:896], in_=bB[:, 128:384], func=AF.Exp, scale=0.125),
                 r=[("bank", 3 + 2 * ip)], w=[("E", ip)])
            for n, (kt, s) in enumerate(chunks):
                eo = s * 128
                P.op("pe", lambda e, n=n, kt=kt, eo=eo, Eb=Eb, h=h, po=po: e.matmul(
                    po[:, h * 65:(h + 1) * 65], lhsT=Eb[:, eo:eo + 128], rhs=V1[:, kt, h, :], start=(n == 0), stop=(n == len(chunks) - 1)),
                    r=[("E", ip), "V1"], w=[("bank", 6 + par)])
        dn, o = den[par], ot[par]
        pov = po[:, 0:130].rearrange("p (h c) -> p h c", h=2)
        if rope:
            P.op("dve", lambda e, dn=dn, pov=pov: e.tensor_tensor(out=dn[:, :].rearrange("p (h o) -> p h o", o=1), in0=pov[:, :, 64:65],
                                                                 in1=ske[:, :].rearrange("p (h o) -> p h o", o=1), op=ALU.add),
                 r=[("bank", 6 + par), "ske"], w=[("den", par)])
            P.op("dve", lambda e, dn=dn: e.reciprocal(out=dn[:], in_=dn[:]), r=[("den", par)], w=[("den", par)])
        else:
            P.op("dve", lambda e, dn=dn, pov=pov: e.reciprocal(out=dn[:, :].rearrange("p (h o) -> p h o", o=1), in_=pov[:, :, 64:65]),
                 r=[("bank", 6 + par)], w=[("den", par)])
        for h in range(2):
            P.op("dve", lambda e, h=h, dn=dn, o=o, po=po: e.tensor_scalar(out=o[:, h * 64:(h + 1) * 64], in0=po[:, h * 65:h * 65 + 64],
                                                                       scalar1=dn[:, h:h + 1], scalar2=None, op0=ALU.mult),
                 r=[("bank", 6 + par), ("den", par)], w=[("ot", par)])
        P.dma(o_tm[i * 128:(i + 1) * 128, :], o[:], semkey=("ot", par), r=[("ot", par)], is_out=True)
    return P.finish()


def attn_inputs(kind, plat, pctx, j, prm):
    def fm(cols):
        return np.ascontiguousarray(np.concatenate([plat[cols], pctx[cols]], axis=1))
    a = np.arange(128)
    if kind == "A":
        qc, kc, vc = j * 128 + a, 512 + j * 128 + a, 1024 + j * 128 + a
        tabs = prm["na_tabs"]
        d = dict(btab=np.ascontiguousarray(np.concatenate([tabs[2 * j], tabs[2 * j + 1]], axis=1)))
        gq, gk = prm["na_q_g"], prm["na_k_g"]
    else:
        kvh = j // 2
        a64 = np.tile(np.arange(64), 2)
        qc, kc, vc = 3232 + j * 128 + a, 3744 + kvh * 64 + a64, 3872 + kvh * 64 + a64
        d = dict(btab=prm["wa_tab"], cosT=prm["cosT"], sinT=prm["sinT"], prot=prm["prot"],
                 sink=np.ascontiguousarray(np.broadcast_to(prm["wa_sink"][2 * j:2 * j + 2][None, :], (128, 2))))
        gq, gk = prm["wa_q_g"], prm["wa_k_g"]
    d.update(qT=fm(qc), kT=fm(kc), vtm=np.ascontiguousarray(fm(vc).T),
             gq=np.ascontiguousarray(np.tile(gq, 2)[:, None]), gk=np.ascontiguousarray(np.tile(gk, 2)[:, None]))
    return d


def attn_consts():
    cos, sin = rope_tables()
    return dict(wa_tab=wa_bias_table(), cosT=np.ascontiguousarray(np.tile(cos.T, (2, 1))),
                sinT=np.ascontiguousarray(np.tile(sin.T, (2, 1))), prot=rot_matrix())


I32 = mybir.dt.int32
S5_BLOCKS = [(0, TCTX)] + [(TCTX + 512 * i, 512) for i in range(16)]
TWO_PI = 2.0 * np.pi


def build_s5():
    P = Prog()
    uTd = P.din("uT", [128, TQ])
    bre = P.din("bre", [128, 4, 128])
    bim = P.din("bim", [128, 4, 128])
    cre = P.din("cre", [128, 4, 128])
    cim = P.din("cim", [128, 4, 128])
    pare = P.din("are", [128, 8])
    paim = P.din("aim", [128, 8])
    pldt = P.din("ldt", [128, 8])
    pdsk = P.din("dsk", [128, 1])
    yo = P.dout("yT", [128, TQ])

    uT = P.sb([128, TQ])
    yacc = P.sb([128, TQ])
    re = [P.sb([128, TQ]) for _ in range(2)]
    nim = P.sb([128, TQ])
    Bre, Bim, Cre, Cim = (P.sb([128, 4, 128]) for _ in range(4))
    small = {}

    def sm(name, shape=(128, 8), dt=F32):
        small[name] = P.sb(list(shape), dt)
        return small[name]
    are, aim, ldt, dsk = sm("are"), sm("aim"), sm("ldt"), sm("dsk", (128, 1))
    NL = 14
    pwr, pwi, npwi = sm("pwr", (128, NL, 8)), sm("pwi", (128, NL, 8)), sm("npwi", (128, NL, 8))
    tmp = [P.sb([128, 512]) for _ in range(2)]
    g1 = [P.sb([128, 512]) for _ in range(2)]
    g2 = [P.sb([128, 512]) for _ in range(2)]
    ob = [P.sb([128, 512]) for _ in range(2)]
    banks = [P.ps([128, 512]) for _ in range(6)]

    P.dma(uT[:], uTd, semkey="u", w=["uT"])
    for t, s, k in ((Bre, bre, "Bre"), (Bim, bim, "Bim"), (Cre, cre, "Cre"), (Cim, cim, "Cim"),
                    (are, pare, "are"), (aim, paim, "aim"), (ldt, pldt, "ldt"), (dsk, pdsk, "dsk")):
        P.dma(t[:], s, semkey=k, w=[k])

    def V(name, fn, r, w):
        P.op("dve", fn, r=r, w=w)

    dt_, dre, mag, ang = sm("dt"), sm("dre"), sm("mag"), sm("ang")
    P.op("act", lambda e: e.activation(out=dt_[:], in_=ldt[:], func=AF.Exp), r=["ldt"], w=["dt"])
    V("", lambda e: e.tensor_tensor(out=dre[:], in0=dt_[:], in1=are[:], op=ALU.mult), ["dt", "are"], ["dre"])
    P.op("act", lambda e: e.activation(out=mag[:], in_=dre[:], func=AF.Exp), r=["dre"], w=["mag"])
    V("", lambda e: e.tensor_tensor(out=ang[:], in0=dt_[:], in1=aim[:], op=ALU.mult), ["dt", "aim"], ["ang"])

    def sin_of(dst, shift, nm):
        z, ki, kf, r_, m_ = sm(nm + "z"), sm(nm + "ki", dt=I32), sm(nm + "kf"), sm(nm + "r"), sm(nm + "m")
        V("", lambda e: e.tensor_scalar(out=z[:], in0=ang[:], scalar1=1.0 / TWO_PI, scalar2=shift / TWO_PI + 0.5, op0=ALU.mult, op1=ALU.add),
          ["ang"], [nm + "z"])
        V("", lambda e: e.tensor_copy(out=ki[:], in_=z[:]), [nm + "z"], [nm + "ki"])
        V("", lambda e: e.tensor_copy(out=kf[:], in_=ki[:]), [nm + "ki"], [nm + "kf"])
        V("", lambda e: e.scalar_tensor_tensor(out=r_[:], in0=kf[:], scalar=-TWO_PI, in1=ang[:], op0=ALU.mult, op1=ALU.add),
          [nm + "kf", "ang"], [nm + "r"])
        if shift:
            V("", lambda e: e.tensor_scalar(out=r_[:], in0=r_[:], scalar1=shift, scalar2=None, op0=ALU.add), [nm + "r"], [nm + "r"])
        V("", lambda e: e.tensor_scalar(out=m_[:], in0=r_[:], scalar1=-np.pi, scalar2=TWO_PI, op0=ALU.is_lt, op1=ALU.mult), [nm + "r"], [nm + "m"])
        V("", lambda e: e.tensor_tensor(out=r_[:], in0=r_[:], in1=m_[:], op=ALU.add), [nm + "r", nm + "m"], [nm + "r"])
        V("", lambda e: e.tensor_scalar(out=m_[:], in0=r_[:], scalar1=np.pi, scalar2=-TWO_PI, op0=ALU.is_gt, op1=ALU.mult), [nm + "r"], [nm + "m"])
        V("", lambda e: e.tensor_tensor(out=r_[:], in0=r_[:], in1=m_[:], op=ALU.add), [nm + "r", nm + "m"], [nm + "r"])
        P.op("act", lambda e: e.activation(out=dst[:], in_=r_[:], func=AF.Sin), r=[nm + "r"], w=[nm + "s"])
    sn_, cs_ = sm("sn"), sm("cs")
    sin_of(sn_, 0.0, "sn")
    sin_of(cs_, np.pi / 2, "cs")
    V("", lambda e: e.tensor_tensor(out=pwr[:, 0, :], in0=mag[:], in1=cs_[:], op=ALU.mult), ["mag", "css"], ["pwr"])
    V("", lambda e: e.tensor_tensor(out=pwi[:, 0, :], in0=mag[:], in1=sn_[:], op=ALU.mult), ["mag", "sns"], ["pwi"])
    den, nr, t1, t2, cfr, cfi, ncfr, ncfi = (sm(n) for n in ("den", "nr", "t1", "t2", "cfr", "cfi", "ncfr", "ncfi"))
    V("", lambda e: e.tensor_tensor(out=den[:], in0=are[:], in1=are[:], op=ALU.mult), ["are"], ["den"])
    V("", lambda e: e.tensor_tensor(out=t1[:], in0=aim[:], in1=aim[:], op=ALU.mult), ["aim"], ["t1"])
    V("", lambda e: e.tensor_tensor(out=den[:], in0=den[:], in1=t1[:], op=ALU.add), ["den", "t1"], ["den"])
    V("", lambda e: e.reciprocal(out=den[:], in_=den[:]), ["den"], ["den"])
    V("", lambda e: e.tensor_scalar(out=nr[:], in0=pwr[:, 0, :], scalar1=-1.0, scalar2=None, op0=ALU.add), ["pwr"], ["nr"])
    V("", lambda e: e.tensor_tensor(out=t1[:], in0=nr[:], in1=are[:], op=ALU.mult), ["nr", "are", "den"], ["t1"])
    V("", lambda e: e.tensor_tensor(out=t2[:], in0=pwi[:, 0, :], in1=aim[:], op=ALU.mult), ["pwi", "aim"], ["t2"])
    V("", lambda e: e.tensor_tensor(out=t1[:], in0=t1[:], in1=t2[:], op=ALU.add), ["t1", "t2"], ["t1"])
    V("", lambda e: e.tensor_tensor(out=cfr[:], in0=t1[:], in1=den[:], op=ALU.mult), ["t1", "den"], ["cfr"])
    V("", lambda e: e.tensor_tensor(out=t1[:], in0=pwi[:, 0, :], in1=are[:], op=ALU.mult), ["pwi", "are", "cfr"], ["t1"])
    V("", lambda e: e.tensor_tensor(out=t2[:], in0=nr[:], in1=aim[:], op=ALU.mult), ["nr", "aim", "cfr"], ["t2"])
    V("", lambda e: e.tensor_tensor(out=t1[:], in0=t1[:], in1=t2[:], op=ALU.subtract), ["t1", "t2"], ["t1"])
    V("", lambda e: e.tensor_tensor(out=cfi[:], in0=t1[:], in1=den[:], op=ALU.mult), ["t1", "den"], ["cfi"])
    V("", lambda e: e.tensor_scalar(out=ncfr[:], in0=cfr[:], scalar1=-1.0, scalar2=None, op0=ALU.mult), ["cfr"], ["ncfr"])
    V("", lambda e: e.tensor_scalar(out=ncfi[:], in0=cfi[:], scalar1=-1.0, scalar2=None, op0=ALU.mult), ["cfi"], ["ncfi"])
    for l in range(1, NL):
        V("", lambda e, l=l: e.tensor_tensor(out=t1[:], in0=pwr[:, l - 1, :], in1=pwr[:, l - 1, :], op=ALU.mult), ["pwr", "cfi", "ncfi"], ["t1"])
        V("", lambda e, l=l: e.tensor_tensor(out=t2[:], in0=pwi[:, l - 1, :], in1=pwi[:, l - 1, :], op=ALU.mult), ["pwi", "cfi", "ncfi"], ["t2"])
        V("", lambda e, l=l: e.tensor_tensor(out=pwi[:, l, :], in0=pwr[:, l - 1, :], in1=pwi[:, l - 1, :], op=ALU.mult), ["pwr", "pwi"], ["pwi"])
        V("", lambda e, l=l: e.tensor_tensor(out=pwr[:, l, :], in0=t1[:], in1=t2[:], op=ALU.subtract), ["t1", "t2"], ["pwr"])
        V("", lambda e, l=l: e.tensor_scalar(out=pwi[:, l, :], in0=pwi[:, l, :], scalar1=2.0, scalar2=None, op0=ALU.mult), ["pwi"], ["pwi"])
    V("", lambda e: e.tensor_scalar(out=npwi[:].rearrange("p l c -> p (l c)"), in0=pwi[:].rearrange("p l c -> p (l c)"), scalar1=-1.0, scalar2=None, op0=ALU.mult),
      ["pwi"], ["npwi"])

    V("", lambda e: e.tensor_scalar(out=yacc[:], in0=uT[:], scalar1=dsk[:, 0:1], scalar2=None, op0=ALU.mult), ["uT", "dsk"], ["yacc"])

    cnt = 0
    for d in range(2):
        for c in range(4):
            dc = d * 4 + c
            cur = 0
            for (t0, nt) in S5_BLOCKS:
                if d == 0:
                    o0 = t0
                else:
                    o0 = TLAT if t0 == 0 else t0 - TCTX
                p1, p2 = banks[(cnt % 2) * 2], banks[(cnt % 2) * 2 + 1]
                k1, k2 = ("bank", (cnt % 2) * 2), ("bank", (cnt % 2) * 2 + 1)
                tb = tmp[cnt % 2]
                cnt += 1
                P.op("pe", lambda e, p1=p1, c=c, t0=t0, nt=nt: e.matmul(p1[:, 0:nt], lhsT=Bre[:, c, :], rhs=uT[:, t0:t0 + nt], start=True, stop=True),
                     r=["Bre", "uT"], w=[k1])
                P.op("pe", lambda e, p2=p2, c=c, t0=t0, nt=nt: e.matmul(p2[:, 0:nt], lhsT=Bim[:, c, :], rhs=uT[:, t0:t0 + nt], start=True, stop=True),
                     r=["Bim", "uT"], w=[k2])
                V("", lambda e, tb=tb, p2=p2, nt=nt, dc=dc: e.tensor_scalar(out=tb[:, 0:nt], in0=p2[:, 0:nt], scalar1=cfi[:, dc:dc + 1], scalar2=None, op0=ALU.mult),
                  [k2, "cfi"], [("tmp", id(tb))])
                V("", lambda e, tb=tb, p1=p1, nt=nt, dc=dc, o0=o0: e.scalar_tensor_tensor(
                    out=re[0][:, o0:o0 + nt], in0=p1[:, 0:nt], scalar=cfr[:, dc:dc + 1], in1=tb[:, 0:nt], op0=ALU.mult, op1=ALU.subtract),
                    [k1, "cfr", ("tmp", id(tb))], ["re0"])
                V("", lambda e, tb=tb, p1=p1, nt=nt, dc=dc: e.tensor_scalar(out=tb[:, 0:nt], in0=p1[:, 0:nt], scalar1=ncfi[:, dc:dc + 1], scalar2=None, op0=ALU.mult),
                  [k1, "ncfi"], [("tmp", id(tb))])
                V("", lambda e, tb=tb, p2=p2, nt=nt, dc=dc, o0=o0: e.scalar_tensor_tensor(
                    out=nim[:, o0:o0 + nt], in0=p2[:, 0:nt], scalar=ncfr[:, dc:dc + 1], in1=tb[:, 0:nt], op0=ALU.mult, op1=ALU.add),
                    [k2, "ncfr", ("tmp", id(tb))], ["nim"])
            for l in range(NL):
                s = 1 << l
                n = TQ - s
                ro, rn = re[cur], re[1 - cur]
                kro, krn = "re%d" % cur, "re%d" % (1 - cur)
                ar, ai, nai = pwr[:, l, dc:dc + 1], pwi[:, l, dc:dc + 1], npwi[:, l, dc:dc + 1]
                if d == 0:
                    dst, src, keep = slice(s, TQ), slice(0, n), slice(0, s)
                else:
                    dst, src, keep = slice(0, n), slice(s, TQ), slice(n, TQ)
                V("", lambda e, ro=ro, rn=rn, ar=ar, dst=dst, src=src: e.scalar_tensor_tensor(
                    out=rn[:, dst], in0=ro[:, src], scalar=ar, in1=ro[:, dst], op0=ALU.mult, op1=ALU.add), [kro, "pwr"], [krn])
                V("", lambda e, rn=rn, ai=ai, dst=dst, src=src: e.scalar_tensor_tensor(
                    out=rn[:, dst], in0=nim[:, src], scalar=ai, in1=rn[:, dst], op0=ALU.mult, op1=ALU.add), [krn, "nim", "pwi"], [krn])
                P.op("pool", lambda e, ro=ro, rn=rn, keep=keep: e.tensor_copy(out=rn[:, keep], in_=ro[:, keep]), r=[kro], w=[krn])
                if d == 0:
                    dsr, ssr = slice(TQ - 1, s - 1, -1), (slice(n - 1, None, -1))
                else:
                    dsr, ssr = dst, src
                V("", lambda e, ar=ar, dsr=dsr, ssr=ssr: e.scalar_tensor_tensor(
                    out=nim[:, dsr], in0=nim[:, ssr], scalar=ar, in1=nim[:, dsr], op0=ALU.mult, op1=ALU.add), ["nim", "pwr"], ["nim"])
                V("", lambda e, ro=ro, nai=nai, dst=dst, src=src: e.scalar_tensor_tensor(
                    out=nim[:, dst], in0=ro[:, src], scalar=nai, in1=nim[:, dst], op0=ALU.mult, op1=ALU.add), ["nim", kro, "npwi"], ["nim"])
                cur = 1 - cur
            xr, kxr = re[cur], "re%d" % cur
            for (t0, nt) in S5_BLOCKS:
                if d == 0:
                    o0 = t0
                else:
                    o0 = TLAT if t0 == 0 else t0 - TCTX
                pb, kb = banks[4 + cnt % 2], ("bank", 4 + cnt % 2)
                cnt += 1
                P.op("pe", lambda e, pb=pb, c=c, o0=o0, nt=nt, xr=xr: e.matmul(pb[:, 0:nt], lhsT=Cre[:, c, :], rhs=xr[:, o0:o0 + nt], start=True, stop=False),
                     r=["Cre", kxr], w=[kb])
                P.op("pe", lambda e, pb=pb, c=c, o0=o0, nt=nt: e.matmul(pb[:, 0:nt], lhsT=Cim[:, c, :], rhs=nim[:, o0:o0 + nt], start=False, stop=True),
                     r=["Cim", "nim"], w=[kb])
                V("", lambda e, pb=pb, t0=t0, nt=nt: e.tensor_tensor(out=yacc[:, t0:t0 + nt], in0=pb[:, 0:nt], in1=yacc[:, t0:t0 + nt], op=ALU.add),
                  [kb, "yacc"], ["yacc"])
            if cur != 0:
                re[0], re[1] = re[1], re[0]

    for n_, (t0, nt) in enumerate(S5_BLOCKS):
        a, b_, o = g1[n_ % 2], g2[n_ % 2], ob[n_ % 2]
        ka, kb2, ko = ("g1", n_ % 2), ("g2", n_ % 2), ("ob", n_ % 2)
        ys = yacc[:, t0:t0 + nt]
        P.op("pool", lambda e, a=a, ys=ys, nt=nt: e.tensor_tensor(out=a[:, 0:nt], in0=ys, in1=ys, op=ALU.mult), r=["yacc"], w=[ka])
        P.op("pool", lambda e, a=a, nt=nt: e.tensor_scalar(out=a[:, 0:nt], in0=a[:, 0:nt], scalar1=0.044715, scalar2=1.0, op0=ALU.mult, op1=ALU.add), r=[ka], w=[ka])
        P.op("pool", lambda e, a=a, ys=ys, nt=nt: e.tensor_tensor(out=a[:, 0:nt], in0=a[:, 0:nt], in1=ys, op=ALU.mult), r=[ka, "yacc"], w=[ka])
        P.op("act", lambda e, a=a, b_=b_, nt=nt: e.activation(out=b_[:, 0:nt], in_=a[:, 0:nt], func=AF.Sigmoid, scale=1.5957691216057308), r=[ka], w=[kb2])
        V("", lambda e, b_=b_, o=o, ys=ys, nt=nt: e.tensor_tensor(out=o[:, 0:nt], in0=b_[:, 0:nt], in1=ys, op=ALU.mult), [kb2, "yacc"], [ko])
        P.dma(yo[:, t0:t0 + nt], o[:, 0:nt], semkey=ko, r=[ko], is_out=True)
    return P.finish()


def s5_inputs(ulat, uctx, j, prm):
    ch = slice(j * 128, (j + 1) * 128)
    G0 = 8 * j
    bre = np.zeros((128, 4, 128), np.float32)
    bim = np.zeros((128, 4, 128), np.float32)
    cre = np.zeros((128, 4, 128), np.float32)
    cim = np.zeros((128, 4, 128), np.float32)
    for c in range(4):
        for g2 in range(2):
            g = 2 * c + g2
            bre[g * 16:(g + 1) * 16, c, g2 * 64:(g2 + 1) * 64] = prm["s5_b_re"][G0 + g].T
            bim[g * 16:(g + 1) * 16, c, g2 * 64:(g2 + 1) * 64] = prm["s5_b_im"][G0 + g].T
            cre[g2 * 64:(g2 + 1) * 64, c, g * 16:(g + 1) * 16] = prm["s5_c_re"][G0 + g].T
            cim[g2 * 64:(g2 + 1) * 64, c, g * 16:(g + 1) * 16] = prm["s5_c_im"][G0 + g].T

    def st(a):
        a = a[:, G0:G0 + 8].reshape(2, 4, 2, 64)
        return np.ascontiguousarray(a.transpose(2, 3, 0, 1).reshape(128, 8))
    ldt = np.broadcast_to(prm["s5_log_dt"][:, :, None], (2, 32, 64))
    return dict(uT=np.ascontiguousarray(np.concatenate([uctx[ch], ulat[ch]], axis=1)), bre=bre, bim=bim, cre=cre, cim=cim,
                are=st(prm["s5_a_re"]), aim=st(prm["s5_a_im"]), ldt=st(ldt),
                dsk=np.ascontiguousarray(prm["s5_d"][ch][:, None]))


RW_T = TCTX + TLAT
RW_NCH = RW_T // 64
RW_BLOCKS = [(0, TCTX, 0)] + [(TCTX + 512 * i, 512, TCTX + 2 + 512 * i) for i in range(16)]
RW_PADT = TCTX + 2 + TLAT + 2
NEG_SQRT_E = -0.6065306597126334


def rw_consts():
    i = np.arange(128) % 64
    jj = np.arange(128) % 64
    strict = (np.arange(128) < 64)[None, :]
    I, J = i[:, None], jj[None, :]
    mf = np.where(strict, I < J, I <= J)
    mb = np.where(strict, I > J, I >= J)
    mask = np.stack([mf, mb], axis=1).astype(np.float32)
    a = np.arange(64)
    nmask = np.stack([a[:, None] > a[None, :], a[:, None] < a[None, :]], axis=1).astype(np.float32)
    m0 = np.ones((128, 512), np.float32)
    m0[:, ::64] = 0.0
    bones = np.zeros((128, 128), np.float32)
    bones[:64, :64] = 1.0
    bones[64:, 64:] = 1.0
    hsel = np.zeros((128, 2), np.float32)
    hsel[:64, 0] = 1.0
    hsel[64:, 1] = 1.0
    ib = np.concatenate([np.eye(64, dtype=np.float32)] * 2, axis=0)
    return dict(mask=mask, nmask=nmask, m0=m0, bones=bones, hsel=hsel, ib=ib, ident=np.eye(128, dtype=np.float32))


def build_rwkv(nblk=17, dbg=99):
    P = Prog()
    din = P.din
    rP, kP, vP = din("rP", [128, RW_PADT]), din("kP", [128, RW_PADT]), din("vP", [128, RW_PADT])
    dwT, daT, dgT = din("dwT", [32, RW_T]), din("daT", [32, RW_T]), din("dgT", [96, RW_T])
    cwD, wupD, aupD, gupD = din("cw", [128, 9]), din("wup", [32, 2, 128]), din("aup", [32, 2, 128]), din("gup", [96, 128])
    w0D, a0D, kkwD, kaD, rkD = din("w0", [128, 2]), din("a0", [128, 2]), din("kkw", [128, 1]), din("ka", [128, 1]), din("rk", [128, 1])
    gnwD, gnbD = din("gnw", [64, 128]), din("gnb", [64, 128])
    maskD, nmaskD, m0D, bonesD, hselD, ibD, identD = (din("mask", [128, 2, 128]), din("nmask", [64, 2, 64]), din("m0", [128, 512]),
                                                      din("bones", [128, 128]), din("hsel", [128, 2]), din("ib", [128, 64]), din("ident", [128, 128]))
    o_tm = P.dout("o", [RW_T, 128])

    def load(shape, src, key):
        t = P.sb(shape)
        P.dma(t[:], src, semkey=key, w=[key])
        return t
    cw, wup, aup, gup = load([128, 9], cwD, "cw"), load([32, 2, 128], wupD, "wup"), load([32, 2, 128], aupD, "aup"), load([96, 128], gupD, "gup")
    w0, a0, kkw, ka, rk = load([128, 2], w0D, "w0"), load([128, 2], a0D, "a0"), load([128, 1], kkwD, "kkw"), load([128, 1], kaD, "ka"), load([128, 1], rkD, "rk")
    gnw, gnb = load([64, 128], gnwD, "gnw"), load([64, 128], gnbD, "gnb")
    MASK, NMASK, mask0 = load([128, 2, 128], maskD, "mask"), load([64, 2, 64], nmaskD, "nmask"), load([128, 512], m0D, "m0")
    bones, HSEL, IB, ident = load([128, 128], bonesD, "bones"), load([128, 2], hselD, "hsel"), load([128, 64], ibD, "ib"), load([128, 128], identD, "ident")

    bk = [P.ps([128, 512]) for _ in range(8)]

    def B(i):
        return ("bank", i)
    tiles = {}

    def T(name, shape=(128, 512)):
        if name not in tiles:
            tiles[name] = P.sb(list(shape))
        return tiles[name]
    R1 = [P.sb([128, 8, 128]) for _ in range(2)]
    L1 = [P.sb([128, 8, 128]) for _ in range(2)]
    Z1 = [P.sb([128, 8, 128]) for _ in range(2)]
    Z2 = [P.sb([128, 8, 128]) for _ in range(2)]
    WL = [P.sb([128, 8]) for _ in range(2)]
    RKK = [P.sb([128, 512]) for _ in range(2)]
    SG = [P.sb([96, 512]) for _ in range(2)]
    XV = [P.sb([128, 2, 128]) for _ in range(2)]
    BK = [P.sb([128, 2, 64]) for _ in range(2)]
    ATs = [P.sb([128, 2, 128]) for _ in range(2)]
    PP0 = P.sb([64, 2, 2, 64])
    PP = [P.sb([64, 2, 2, 64]) for _ in range(2)]
    MTs = P.sb([128, 64])
    RTs = P.sb([128, 64])
    S = [P.sb([128, 64]) for _ in range(2)]
    Yf = P.sb([64, RW_NCH, 128])
    P.op("pool", lambda e: e.memset(S[0][:], 0.0), w=[("S", 0)])

    def v3(ap, nch):
        return ap[:, 0:nch * 64].rearrange("p (c j) -> p c j", j=64)

    def prep(bi, d, par):
        s0, nt, p0 = RW_BLOCKS[bi]
        nch = nt // 64
        raws = {}
        for nm, src in (("r", rP), ("k", kP), ("v", vP)):
            t = T("raw" + nm, (128, 514))
            P.dma(t[:, 0:nt + 2], src[:, p0 - 1 + 0:p0 - 1 + nt + 2] if False else src[:, p0:p0 + nt + 2], semkey="raw" + nm, w=["raw" + nm])
            raws[nm] = t
        dw, da = T("dw", (32, 512)), T("da", (32, 512))
        P.dma(dw[:, 0:nt], dwT[:, s0:s0 + nt], semkey="dw", w=["dw"])
        P.dma(da[:, 0:nt], daT[:, s0:s0 + nt], semkey="da", w=["da"])
        outs = {"r": T("rc"), "k": T("kc"), "v": T("vc")}
        for ai, nm in enumerate(("r", "k", "v")):
            rw, o = raws[nm], outs[nm]
            P.op("dve", lambda e, rw=rw, o=o, ai=ai: e.tensor_scalar(out=o[:, 0:nt], in0=rw[:, 0:nt], scalar1=cw[:, 3 * ai:3 * ai + 1], scalar2=None, op0=ALU.mult),
                 r=["raw" + nm, "cw"], w=[nm + "c"])
            for tap in (1, 2):
                P.op("dve", lambda e, rw=rw, o=o, ai=ai, tap=tap: e.scalar_tensor_tensor(
                    out=o[:, 0:nt], in0=rw[:, tap:tap + nt], scalar=cw[:, 3 * ai + tap:3 * ai + tap + 1], in1=o[:, 0:nt], op0=ALU.mult, op1=ALU.add),
                    r=["raw" + nm, "cw", nm + "c"], w=[nm + "c"])
        rc, kc, vc = outs["r"], outs["k"], outs["v"]
        P.op("pool", lambda e: e.tensor_copy(out=Z1[par][:, 0:nch, 64:128], in_=v3(vc, nch)), r=["vc"], w=[("Z1", par)])
        th = T("th", (32, 512))
        P.op("act", lambda e: e.activation(out=th[:, 0:nt], in_=dw[:, 0:nt], func=AF.Tanh), r=["dw"], w=["th"])
        dirs = [0] if d == 0 else [0, 1]
        aa, kt = {}, {}
        for dd in dirs:
            aa[dd] = T("a%d" % dd)
            P.op("pe", lambda e, dd=dd: e.matmul(bk[0][:, 0:nt], lhsT=aup[:, dd, :], rhs=da[:, 0:nt], start=True, stop=True), r=["aup", "da"], w=[B(0)])
            P.op("act", lambda e, dd=dd: e.activation(out=aa[dd][:, 0:nt], in_=bk[0][:, 0:nt], func=AF.Sigmoid, bias=a0[:, dd:dd + 1]),
                 r=[B(0), "a0"], w=["a%d" % dd])
        sw, lw = T("sw"), T("lw")
        P.op("pe", lambda e: e.matmul(bk[0][:, 0:nt], lhsT=wup[:, d, :], rhs=th[:, 0:nt], start=True, stop=True), r=["wup", "th"], w=[B(0)])
        P.op("act", lambda e: e.activation(out=sw[:, 0:nt], in_=bk[0][:, 0:nt], func=AF.Sigmoid, bias=w0[:, d:d + 1]), r=[B(0), "w0"], w=["sw"])
        P.op("dve", lambda e: e.tensor_scalar(out=lw[:, 0:nt], in0=sw[:, 0:nt], scalar1=NEG_SQRT_E, scalar2=None, op0=ALU.mult), r=["sw"], w=["lw"])
        kkr, sq, nr, kk = T("kkr"), T("sq"), T("nr"), T("kk")
        P.op("dve", lambda e: e.tensor_scalar(out=kkr[:, 0:nt], in0=kc[:, 0:nt], scalar1=kkw[:, 0:1], scalar2=None, op0=ALU.mult), r=["kc", "kkw"], w=["kkr"])
        P.op("pool", lambda e: e.tensor_tensor(out=sq[:, 0:nt], in0=kkr[:, 0:nt], in1=kkr[:, 0:nt], op=ALU.mult), r=["kkr"], w=["sq"])
        P.op("pe", lambda e: e.matmul(bk[0][:, 0:nt], lhsT=bones[:], rhs=sq[:, 0:nt], start=True, stop=True), r=["bones", "sq"], w=[B(0)])
        P.op("act", lambda e: e.activation(out=nr[:, 0:nt], in_=bk[0][:, 0:nt], func=AF.Sqrt), r=[B(0)], w=["nr"])
        P.op("dve", lambda e: e.tensor_scalar(out=nr[:, 0:nt], in0=nr[:, 0:nt], scalar1=1e-12, scalar2=None, op0=ALU.max), r=["nr"], w=["nr"])
        P.op("dve", lambda e: e.reciprocal(out=nr[:, 0:nt], in_=nr[:, 0:nt]), r=["nr"], w=["nr"])
        P.op("dve", lambda e: e.tensor_tensor(out=kk[:, 0:nt], in0=kkr[:, 0:nt], in1=nr[:, 0:nt], op=ALU.mult), r=["kkr", "nr"], w=["kk"])
        for dd in dirs:
            kt[dd] = T("kt%d" % dd)
            tk = T("tk")
            P.op("pool", lambda e, dd=dd: e.tensor_scalar(out=tk[:, 0:nt], in0=aa[dd][:, 0:nt], scalar1=-1.0, scalar2=ka[:, 0:1], op0=ALU.add, op1=ALU.mult),
                 r=["a%d" % dd, "ka"], w=["tk"])
            P.op("dve", lambda e, dd=dd: e.scalar_tensor_tensor(out=kt[dd][:, 0:nt], in0=tk[:, 0:nt], scalar=1.0, in1=kc[:, 0:nt], op0=ALU.add, op1=ALU.mult),
                 r=["tk", "kc"], w=["kt%d" % dd])
        beta = T("beta")
        P.op("pool", lambda e: e.tensor_tensor(out=beta[:, 0:nt], in0=kk[:, 0:nt], in1=aa[d][:, 0:nt], op=ALU.mult), r=["kk", "a%d" % d], w=["beta"])
        lWf, lW, lWex, dl = T("lWf"), T("lW"), T("lWex"), T("dl")
        P.op("dve", lambda e: e.tensor_tensor_scan(out=lWf[:, 0:nt], data0=mask0[:, 0:nt], data1=lw[:, 0:nt], initial=0.0, op0=ALU.mult, op1=ALU.add),
             r=["m0", "lw"], w=["lWf"])
        tot = v3(lWf, nch)[:, :, 63:64]
        totb = tot.broadcast_to([128, nch, 64])
        if d == 0:
            lW = lWf
            klW = "lWf"
        else:
            klW = "lW"
            P.op("dve", lambda e: e.tensor_tensor(out=v3(lW, nch), in0=totb, in1=v3(lWf, nch), op=ALU.subtract), r=["lWf"], w=["lW"])
            P.op("dve", lambda e: e.tensor_tensor(out=lW[:, 0:nt], in0=lW[:, 0:nt], in1=lw[:, 0:nt], op=ALU.add), r=["lW", "lw"], w=["lW"])
        P.op("pool", lambda e: e.tensor_tensor(out=lWex[:, 0:nt], in0=lW[:, 0:nt], in1=lw[:, 0:nt], op=ALU.subtract), r=[klW, "lw"], w=["lWex"])
        P.op("dve", lambda e: e.tensor_tensor(out=v3(dl, nch), in0=totb, in1=v3(lW, nch), op=ALU.subtract), r=["lWf", klW], w=["dl"])
        E1, E2, E3, E4 = T("E1"), T("E2"), T("E3"), T("E4")
        P.op("act", lambda e: e.activation(out=E1[:, 0:nt], in_=lWex[:, 0:nt], func=AF.Exp), r=["lWex"], w=["E1"])
        P.op("act", lambda e: e.activation(out=E2[:, 0:nt], in_=lW[:, 0:nt], func=AF.Exp), r=[klW], w=["E2"])
        P.op("act", lambda e: e.activation(out=E3[:, 0:nt], in_=lW[:, 0:nt], func=AF.Exp, scale=-1.0), r=[klW], w=["E3"])
        P.op("act", lambda e: e.activation(out=E4[:, 0:nt], in_=dl[:, 0:nt], func=AF.Exp), r=["dl"], w=["E4"])
        P.op("act", lambda e: e.activation(out=WL[par][:, 0:nch].rearrange("p (c o) -> p c o", o=1), in_=tot, func=AF.Exp), r=["lWf"], w=[("WL", par)])
        P.op("dve", lambda e: e.scalar_tensor_tensor(out=R1[par][:, 0:nch, 0:64], in0=v3(kk, nch), scalar=-1.0, in1=v3(E1, nch), op0=ALU.mult, op1=ALU.mult),
             r=["kk", "E1"], w=[("R1", par)])
        P.op("pool", lambda e: e.tensor_copy(out=Z1[par][:, 0:nch, 0:64], in_=R1[par][:, 0:nch, 0:64]), r=[("R1", par)], w=[("Z1", par)])
        P.op("dve", lambda e: e.tensor_tensor(out=R1[par][:, 0:nch, 64:128], in0=v3(rc, nch), in1=v3(E2, nch), op=ALU.mult), r=["rc", "E2"], w=[("R1", par)])
        P.op("pool", lambda e: e.tensor_tensor(out=L1[par][:, 0:nch, 0:64], in0=v3(beta, nch), in1=v3(E3, nch), op=ALU.mult), r=["beta", "E3"], w=[("L1", par)])
        P.op("dve", lambda e: e.tensor_tensor(out=L1[par][:, 0:nch, 64:128], in0=v3(kt[d], nch), in1=v3(E3, nch), op=ALU.mult), r=["kt%d" % d, "E3"], w=[("L1", par)])
        P.op("pool", lambda e: e.tensor_tensor(out=Z2[par][:, 0:nch, 0:64], in0=v3(beta, nch), in1=v3(E4, nch), op=ALU.mult), r=["beta", "E4"], w=[("Z2", par)])
        P.op("dve", lambda e: e.tensor_tensor(out=Z2[par][:, 0:nch, 64:128], in0=v3(kt[d], nch), in1=v3(E4, nch), op=ALU.mult), r=["kt%d" % d, "E4"], w=[("Z2", par)])
        if d == 1:
            dg = T("dg", (96, 512))
            P.dma(dg[:, 0:nt], dgT[:, s0:s0 + nt], semkey="dg", w=["dg"])
            P.op("act", lambda e: e.activation(out=SG[par][:, 0:nt], in_=dg[:, 0:nt], func=AF.Sigmoid), r=["dg"], w=[("SG", par)])
            ks = T("ks")
            P.op("pool", lambda e: e.tensor_tensor(out=ks[:, 0:nt], in0=kt[0][:, 0:nt], in1=kt[1][:, 0:nt], op=ALU.add), r=["kt0", "kt1"], w=["ks"])
            P.op("dve", lambda e: e.scalar_tensor_tensor(out=RKK[par][:, 0:nt], in0=rc[:, 0:nt], scalar=rk[:, 0:1], in1=ks[:, 0:nt], op0=ALU.mult, op1=ALU.mult),
                 r=["rc", "rk", "ks"], w=[("RKK", par)])

    state = {"cur": 0, "q": 0}

    def mm(out, lhsT, rhs, r, w, rows, start=True, stop=True):
        P.op("pe", lambda e: e.matmul(out, lhsT=lhsT, rhs=rhs, start=start, stop=stop), r=r, w=w)

    def chunk(bi, c, d, par):
        s0, nt, _ = RW_BLOCKS[bi]
        cg = s0 // 64 + c
        q = state["q"]
        state["q"] = 1 - q
        cur = state["cur"]
        xv, bkq, ats = XV[q], BK[q], ATs[q]
        kxv, kbk, kat = ("XV", q), ("BK", q), ("ATs", q)
        r1, l1, z1, z2 = R1[par], L1[par], Z1[par], Z2[par]
        if dbg < 1:
            return
        mm(bk[0][:, 0:128], z1[:, c, :], ident[:], [("Z1", par), "ident"], [B(0)], 128)
        mm(bk[0][:, 128:256], z2[:, c, :], ident[:], [("Z2", par), "ident"], [B(0)], 128)
        if d == 1:
            mm(bk[0][0:64, 256:384], z1[:, c, 64:128], ident[:], [("Z1", par), "ident"], [B(0)], 128)
        P.op("act", lambda e: e.activation(out=xv[0:64, :, 64:128], in_=bk[0][0:64, 0:128].rearrange("p (h k) -> p h k", h=2), func=AF.Copy),
             r=[B(0)], w=[kxv])
        P.op("act", lambda e: e.activation(out=xv[64:128, :, 0:64], in_=bk[0][64:128, 0:128].rearrange("p (h k) -> p h k", h=2), func=AF.Copy),
             r=[B(0)], w=[kxv])
        P.op("dve", lambda e: e.tensor_copy(out=bkq[:, :, :], in_=bk[0][:, 128:256].rearrange("p (h k) -> p h k", h=2)), r=[B(0)], w=[kbk])
        if d == 1:
            vtm = T("vtm", (64, 128))
            P.op("act", lambda e: e.activation(out=vtm[:, :], in_=bk[0][0:64, 256:384], func=AF.Copy), r=[B(0)], w=["vtm"])
        if dbg < 2:
            return
        for h in range(2):
            hs = slice(64 * h, 64 * h + 64)
            mm(bk[1 + h][:, 0:128], l1[hs, c, :], r1[hs, c, :], [("L1", par), ("R1", par)], [B(1 + h)], 64)
            mm(bk[1 + h][0:64, 128:192], r1[hs, c, 0:64], l1[hs, c, 0:64], [("L1", par), ("R1", par)], [B(1 + h)], 64)
        for h in range(2):
            P.op("dve", lambda e, h=h: e.tensor_tensor(out=ats[:, h, :], in0=bk[1 + h][:, 0:128], in1=MASK[:, d, :], op=ALU.mult), r=[B(1 + h), "mask"], w=[kat])
            P.op("dve", lambda e, h=h: e.tensor_tensor(out=PP0[:, h, 0, :], in0=bk[1 + h][0:64, 128:192], in1=NMASK[:, d, :], op=ALU.mult),
                 r=[B(1 + h), "nmask"], w=["PP0"])
        if dbg < 3:
            return
        for h in range(2):
            mm(bk[4][0:64, 64 * h:64 * h + 64], ats[64:128, h, 0:64], xv[64:128, h, 0:64], [kat, kxv], [B(4)], 64)
        P.op("act", lambda e: e.activation(out=xv[0:64, :, 0:64], in_=bk[4][0:64, 0:128].rearrange("p (h k) -> p h k", h=2), func=AF.Copy), r=[B(4)], w=[kxv])
        if dbg < 4:
            return
        ppc, kpp = PP0, "PP0"
        for l in range(6):
            for h in range(2):
                pt = ats[0:64, h, 0:64] if l == 0 else ppc[:, h, 1, :]
                mm(bk[3][0:64, 128 * h:128 * h + 128], pt, xv[0:64, h, :], [kat, kpp, kxv], [B(3)], 64)
            if l < 5:
                for h in range(2):
                    pt = ats[0:64, h, 0:64] if l == 0 else ppc[:, h, 1, :]
                    pm = ppc[:, h, 0, :]
                    mm(bk[5][0:64, 128 * h:128 * h + 64], pt, pm, [kat, kpp], [B(5)], 64)
                    mm(bk[5][0:64, 128 * h + 64:128 * h + 128], pm, pt, [kat, kpp], [B(5)], 64)
            P.op("dve", lambda e: e.tensor_tensor(out=xv[0:64, :, :], in0=bk[3][0:64, 0:256].rearrange("p (h k) -> p h k", h=2), in1=xv[0:64, :, :], op=ALU.add),
                 r=[B(3), kxv], w=[kxv])
            if l < 5:
                nx = PP[l % 2]
                P.op("act", lambda e, nx=nx: e.activation(out=nx[:, :, :, :].rearrange("p h t k -> p (h t k)"), in_=bk[5][0:64, 0:256], func=AF.Copy),
                     r=[B(5)], w=[("PP", l % 2)])
                ppc, kpp = nx, ("PP", l % 2)
        if dbg < 5:
            return
        for h in range(2):
            hs = slice(64 * h, 64 * h + 64)
            mm(bk[6][hs, 0:64], xv[0:64, h, 64:128], bkq[0:64, h, :], [kxv, kbk], [B(6)], 64)
            mm(bk[6][hs, 64:128], xv[0:64, h, 64:128], ats[0:64, h, 64:128], [kxv, kat], [B(6)], 64)
        P.op("dve", lambda e: e.scalar_tensor_tensor(out=MTs[:, :], in0=IB[:, :], scalar=WL[par][:, c:c + 1], in1=bk[6][:, 0:64], op0=ALU.mult, op1=ALU.add),
             r=["ib", ("WL", par), B(6)], w=["MTs"])
        P.op("dve", lambda e: e.tensor_tensor(out=RTs[:, :], in0=bk[6][:, 64:128], in1=r1[:, c, 64:128], op=ALU.add), r=[B(6), ("R1", par)], w=["RTs"])
        if dbg < 6:
            return
        for h in range(2):
            hs = slice(64 * h, 64 * h + 64)
            mm(bk[7][0:64, 64 * h:64 * h + 64], ats[:, h, 64:128], xv[:, h, 0:64], [kat, kxv], [B(7)], 128, True, False)
            mm(bk[7][0:64, 64 * h:64 * h + 64], RTs[hs, :], S[cur][hs, :], ["RTs", ("S", cur)], [B(7)], 64, False, True)
        for h in range(2):
            hs = slice(64 * h, 64 * h + 64)
            mm(bk[6][hs, 128:192], bkq[:, h, :], xv[:, h, 0:64], [kbk, kxv], [B(6)], 128, True, False)
            mm(bk[6][hs, 128:192], MTs[hs, :], S[cur][hs, :], ["MTs", ("S", cur)], [B(6)], 64, False, True)
        P.op("act", lambda e: e.activation(out=S[1 - cur][:, :], in_=bk[6][:, 128:192], func=AF.Copy), r=[B(6)], w=[("S", 1 - cur)])
        state["cur"] = 1 - cur
        if d == 0:
            P.op("act", lambda e: e.activation(out=Yf[:, cg, :], in_=bk[7][0:64, 0:128], func=AF.Copy), r=[B(7)], w=["Yf"])
            return
        if dbg < 7:
            return
        y, yc, sq, o = T("y", (64, 128)), T("yc", (64, 128)), T("ysq", (64, 128)), T("o%d" % (cg % 2), (64, 128))
        mu, var, rstd = T("mu", (64, 2)), T("var", (64, 2)), T("rstd", (64, 2))
        tl = slice(64 * c, 64 * c + 64)
        P.op("dve", lambda e: e.tensor_tensor(out=y[:, :], in0=bk[7][0:64, 0:128], in1=Yf[:, cg, :], op=ALU.add), r=[B(7), "Yf"], w=["y"])
        mm(bk[0][0:64, 384:512], SG[par][:, tl], gup[:, :], [("SG", par), "gup"], [B(0)], 96)
        mm(bk[4][0:64, 128:130], RKK[par][:, tl], HSEL[:, :], [("RKK", par), "hsel"], [B(4)], 128)
        for h in range(2):
            hc = slice(64 * h, 64 * h + 64)
            P.op("dve", lambda e, h=h, hc=hc: e.scalar_tensor_tensor(out=y[:, hc], in0=vtm[:, hc], scalar=bk[4][0:64, 128 + h:129 + h], in1=y[:, hc],
                                                                   op0=ALU.mult, op1=ALU.add), r=["vtm", B(4), "y"], w=["y"])
        P.op("dve", lambda e: e.tensor_reduce(out=mu[:, :], in_=y[:, :].rearrange("p (h k) -> p h k", h=2), axis=AX.X, op=ALU.add), r=["y"], w=["mu"])
        P.op("dve", lambda e: e.tensor_scalar(out=mu[:, :], in0=mu[:, :], scalar1=-1.0 / 64, scalar2=None, op0=ALU.mult), r=["mu"], w=["mu"])
        for h in range(2):
            hc = slice(64 * h, 64 * h + 64)
            P.op("dve", lambda e, h=h, hc=hc: e.tensor_scalar(out=yc[:, hc], in0=y[:, hc], scalar1=mu[:, h:h + 1], scalar2=None, op0=ALU.add), r=["y", "mu"], w=["yc"])
        P.op("pool", lambda e: e.tensor_tensor(out=sq[:, :], in0=yc[:, :], in1=yc[:, :], op=ALU.mult), r=["yc"], w=["ysq"])
        P.op("dve", lambda e: e.tensor_reduce(out=var[:, :], in_=sq[:, :].rearrange("p (h k) -> p h k", h=2), axis=AX.X, op=ALU.add), r=["ysq"], w=["var"])
        P.op("act", lambda e: e.activation(out=rstd[:, :], in_=var[:, :], func=AF.Sqrt, scale=1.0 / 64, bias=64e-5), r=["var"], w=["rstd"])
        P.op("dve", lambda e: e.reciprocal(out=rstd[:, :], in_=rstd[:, :]), r=["rstd"], w=["rstd"])
        for h in range(2):
            hc = slice(64 * h, 64 * h + 64)
            P.op("dve", lambda e, h=h, hc=hc: e.scalar_tensor_tensor(out=yc[:, hc], in0=yc[:, hc], scalar=rstd[:, h:h + 1], in1=gnw[:, hc], op0=ALU.mult, op1=ALU.mult),
                 r=["yc", "rstd", "gnw"], w=["yc"])
        P.op("pool", lambda e: e.tensor_tensor(out=yc[:, :], in0=yc[:, :], in1=gnb[:, :], op=ALU.add), r=["yc", "gnb"], w=["yc"])
        P.op("dve", lambda e: e.tensor_tensor(out=o[:, :], in0=bk[0][0:64, 384:512], in1=yc[:, :], op=ALU.mult), r=[B(0), "yc"], w=[("o", cg % 2)])
        P.dma(o_tm[64 * cg:64 * cg + 64, :], o[:, :], semkey=("o", cg % 2), r=[("o", cg % 2)], is_out=True)

    nb = 0
    for d in range(2):
        order = list(range(17)) if d == 0 else [0] + list(range(16, 0, -1))
        for bi in order[:nblk]:
            par = nb % 2
            nb += 1
            prep(bi, d, par)
            nch = RW_BLOCKS[bi][1] // 64
            for c in (range(nch) if d == 0 else range(nch - 1, -1, -1)):
                chunk(bi, c, d, par)
        if d == 0:
            P.op("pool", lambda e: e.memset(S[state["cur"]][:], 0.0), w=[("S", state["cur"])])
    return P.finish()


def rwkv_inputs(plat, pctx, j, prm, cst):
    a = np.arange(128)
    cols = j * 128 + a

    def pad(c0):
        z = np.zeros((128, 1), np.float32)
        return np.ascontiguousarray(np.concatenate([z, pctx[c0 + cols], z, z, plat[c0 + cols], z], axis=1))

    def sfm(c0, n):
        return np.ascontiguousarray(np.concatenate([pctx[c0:c0 + n], plat[c0:c0 + n]], axis=1))
    cw = np.stack([prm["rw_conv"][tap, ai * 512 + cols] for ai in range(3) for tap in range(3)], axis=1)
    d = dict(rP=pad(1536), kP=pad(2048), vP=pad(2560), dwT=sfm(3072, 32), daT=sfm(3104, 32), dgT=sfm(3136, 96),
             cw=np.ascontiguousarray(cw), wup=np.ascontiguousarray(prm["rw_w_up"].transpose(1, 0, 2)[:, :, cols]),
             aup=np.ascontiguousarray(prm["rw_a_up"].transpose(1, 0, 2)[:, :, cols]), gup=np.ascontiguousarray(prm["rw_g_up"][:, cols]),
             w0=np.ascontiguousarray(prm["rw_w0"][:, cols].T), a0=np.ascontiguousarray(prm["rw_a0"][:, cols].T),
             kkw=np.ascontiguousarray(prm["rw_k_k"][cols][:, None]), ka=np.ascontiguousarray(prm["rw_k_a"][cols][:, None]),
             rk=np.ascontiguousarray(prm["rw_r_k"].reshape(512)[cols][:, None]),
             gnw=np.ascontiguousarray(np.broadcast_to(prm["rw_gn_w"][cols][None, :], (64, 128))),
             gnb=np.ascontiguousarray(np.broadcast_to(prm["rw_gn_b"][cols][None, :], (64, 128))))
    d.update(cst)
    return d


TC = 3
NBLK = 33
_PROGS = {}


def _prog(name, fn):
    if name not in _PROGS:
        _PROGS[name] = fn()
    return _PROGS[name]


def _fm(v):
    return np.swapaxes(v.reshape(v.shape[:-1] + (KC, 128)), -1, -2)


def _tok_to_blocks(t):
    return np.ascontiguousarray(t.reshape(NBLK, NT, KC, 128).transpose(0, 3, 2, 1))


def _blocks_to_tok(b):
    return b.transpose(0, 3, 2, 1).reshape(NBLK * NT, D)


def _run(nc, ims):
    return run_bass_kernel_spmd(nc, ims, core_ids=list(range(len(ims)))).results


def kernel(x, c, ctx, c_ctx, w_ada, b_ada, norm_g, ffn1_wg, ffn1_wu, ffn1_wd, ffn2_wg, ffn2_wu, ffn2_wd,
           w_in, w_out, na_q_g, na_k_g, na_rpb, rw_conv, rw_w0, rw_w_up, rw_a0, rw_a_up, rw_g_up,
           rw_k_k, rw_k_a, rw_r_k, rw_gn_w, rw_gn_b, wa_q_g, wa_k_g, wa_sink, s5_a_re, s5_a_im,
           s5_log_dt, s5_b_re, s5_b_im, s5_c_re, s5_c_im, s5_d, s5_glu_w, s5_glu_b):
    f = lambda a: np.ascontiguousarray(np.asarray(a, dtype=np.float32))
    x, c, ctx, c_ctx = f(x), f(c), f(ctx), f(c_ctx)
    L = w_ada.shape[0]
    NB = NBLK // TC
    c3 = np.stack([c[0], c[1], c_ctx])
    cT = np.ascontiguousarray(c3.reshape(3, KC, 128).transpose(2, 1, 0))
    w_ada, b_ada = np.asarray(w_ada), np.asarray(b_ada)
    ims = [dict(cT=cT, w=f(w_ada[:, :, i * ADA_N:(i + 1) * ADA_N]),
                b=f(np.broadcast_to(b_ada[:, None, i * ADA_N:(i + 1) * ADA_N], (L, 3, ADA_N)))) for i in range(8)]
    res = _run(_prog("ada", lambda: build_ada(L)), ims)
    mods = np.concatenate([r["mod"] for r in res], axis=-1).reshape(L, 3, NMOD, D)
    rows = np.array([0] * 16 + [1] * 16 + [2])

    toks = np.concatenate([x[0], x[1], ctx[0], ctx[1]], axis=0)
    xb = _tok_to_blocks(toks)
    cst_at = attn_consts()
    cst_rw = rw_consts()
    for l in range(L):
        ml = _fm(mods[l])
        mvb = ml[rows]
        mv1 = np.ascontiguousarray(mvb[:, 0:5].transpose(0, 2, 1, 3))
        mv2 = np.ascontiguousarray(mvb[:, 5:9].transpose(0, 2, 1, 3))
        gl = _fm(f(norm_g[l]))
        wg, wu, wd, win = f(ffn1_wg[l]), f(ffn1_wu[l]), f(ffn1_wd[l]), f(w_in[l])
        gv1 = np.ascontiguousarray(gl[0:2].transpose(1, 0, 2))
        ims = [dict(xT=np.ascontiguousarray(xb[i * NB:(i + 1) * NB]), mv=np.ascontiguousarray(mv1[i * NB:(i + 1) * NB]),
                    gv=gv1, wg=wg, wu=wu, wd=wd, win=win) for i in range(TC)]
        res = _run(_prog("t1", lambda: build_t1(NB)), ims)
        xb = np.concatenate([r["xo"] for r in res], axis=0)
        pT = np.concatenate([r["pT"] for r in res], axis=0)
        del wg, wu, wd, win, ims, res
        plat = [np.ascontiguousarray(pT[b * 16:(b + 1) * 16].transpose(1, 0, 2).reshape(DIN, TLAT)) for b in range(2)]
        pctx = [np.ascontiguousarray(pT[32][:, b * TCTX:(b + 1) * TCTX]) for b in range(2)]
        prm = dict(cst_at)
        prm.update(na_q_g=f(na_q_g[l]), na_k_g=f(na_k_g[l]), wa_q_g=f(wa_q_g[l]), wa_k_g=f(wa_k_g[l]), wa_sink=f(wa_sink[l]),
                   na_tabs=na_bias_tables(f(na_rpb[l])))
        lp = dict(rw_conv=f(rw_conv[l]), rw_w0=f(rw_w0[l]), rw_w_up=f(rw_w_up[l]), rw_a0=f(rw_a0[l]), rw_a_up=f(rw_a_up[l]),
                  rw_g_up=f(rw_g_up[l]), rw_k_k=f(rw_k_k[l]), rw_k_a=f(rw_k_a[l]), rw_r_k=f(rw_r_k[l]), rw_gn_w=f(rw_gn_w[l]),
                  rw_gn_b=f(rw_gn_b[l]), s5_a_re=f(s5_a_re[l]), s5_a_im=f(s5_a_im[l]), s5_log_dt=f(s5_log_dt[l]),
                  s5_b_re=f(s5_b_re[l]), s5_b_im=f(s5_b_im[l]), s5_c_re=f(s5_c_re[l]), s5_c_im=f(s5_c_im[l]), s5_d=f(s5_d[l]))
        cores = [(cid // 4, cid % 4) for cid in range(8)]
        rA = _run(_prog("A", lambda: build_attn("A")), [attn_inputs("A", plat[b], pctx[b], j, prm) for b, j in cores])
        rC = _run(_prog("C", lambda: build_attn("C")), [attn_inputs("C", plat[b], pctx[b], j, prm) for b, j in cores])
        rB = _run(_prog("rw", build_rwkv), [rwkv_inputs(plat[b], pctx[b], j, lp, cst_rw) for b, j in cores])
        rD = _run(_prog("s5", build_s5), [s5_inputs(plat[b][4000:4512], pctx[b][4000:4512], j, lp) for b, j in cores])
        olat = np.empty((2, TLAT, D), np.float32)
        octx = np.empty((2, TCTX, D), np.float32)
        for cid, (b, j) in enumerate(cores):
            cs = slice(j * 128, (j + 1) * 128)
            oa, oc_, ob, od = rA[cid]["o"], rC[cid]["o"], rB[cid]["o"], rD[cid]["yT"].T
            olat[b][:, 0:512][:, cs] = oa[:TLAT]
            octx[b][:, 0:512][:, cs] = oa[TLAT:]
            olat[b][:, 512:1024][:, cs] = ob[TCTX:]
            octx[b][:, 512:1024][:, cs] = ob[:TCTX]
            olat[b][:, 1024:1536][:, cs] = oc_[:TLAT]
            octx[b][:, 1024:1536][:, cs] = oc_[TLAT:]
            olat[b][:, 1536:2048][:, cs] = od[TCTX:]
            octx[b][:, 1536:2048][:, cs] = od[:TCTX]
        ob_ = _tok_to_blocks(np.concatenate([olat[0], olat[1], octx[0], octx[1]], axis=0))
        del pT, plat, pctx, rA, rB, rC, rD, olat, octx
        wg, wu, wd, wo = f(ffn2_wg[l]), f(ffn2_wu[l]), f(ffn2_wd[l]), f(w_out[l])
        gw = f(s5_glu_w[l])
        gb = np.ascontiguousarray(f(s5_glu_b[l]).reshape(4, 128).T)
        gv2 = np.ascontiguousarray(gl[2])
        ims = [dict(xT=np.ascontiguousarray(xb[i * NB:(i + 1) * NB]), oT=np.ascontiguousarray(ob_[i * NB:(i + 1) * NB]),
                    mv=np.ascontiguousarray(mv2[i * NB:(i + 1) * NB]), gv=gv2, gw=gw, gb=gb, wo=wo, wg=wg, wu=wu, wd=wd) for i in range(TC)]
        res = _run(_prog("t2", lambda: build_t2(NB)), ims)
        xb = np.concatenate([r["xo"] for r in res], axis=0)
        del wg, wu, wd, wo, ims, res, ob_
    out = _blocks_to_tok(xb)[:2 * TLAT].reshape(2, TLAT, D)
    return np.ascontiguousarray(out.astype(np.float32))
```

```python
import numpy as np
from contextlib import ExitStack
import concourse.bass as bass
import concourse.mybir as mybir
from concourse.bass_utils import run_bass_kernel_spmd

F32 = mybir.dt.float32
ALU = mybir.AluOpType
AF = mybir.ActivationFunctionType
AX = mybir.AxisListType

D = 2048
DFF = 5632
DIN = 4512
NMOD = 9
EPS = 1e-6


class _Op:
    __slots__ = ("eng", "fn", "deps", "inc", "seq", "dsem", "dval", "dinc", "flushed", "last")

    def __init__(self, eng, fn):
        self.eng, self.fn, self.deps, self.inc, self.seq = eng, fn, [], False, 0
        self.dsem, self.dval, self.dinc = None, 0, 16
        self.flushed, self.last = False, None


class Prog:
    ENG = ("pe", "act", "dve", "pool", "sp")

    def __init__(self, fused=False):
        self.nc = bass.Bass("TRN2", target_bir_lowering=False)
        self.es = ExitStack()
        self.esp = ExitStack()
        self.ops = {e: [] for e in self.ENG}
        self.last_w = {}
        self.readers = {}
        self.dma_sems = {}
        self.dma_cnt = {}
        self.n = 0
        self.out_dmas = []
        self.fused = fused
        self.bind = {}
        self.prefix = ""
        self.ext_inputs = {}
        self.fence_ops = []
        self.fenced = set(self.ENG)
        self.esem = None
        self.seqc = {e: 0 for e in self.ENG}
        self.seen = {e: {} for e in self.ENG}
        self.touched = set()

    def din(self, name, shape, dtype=F32):
        if name in self.bind:
            return self.bind[name]
        full = self.prefix + name
        self.ext_inputs[full] = tuple(shape)
        return self.nc.dram_tensor(full, list(shape), dtype, kind="ExternalInput").ap()

    def dout(self, name, shape):
        if name in self.bind:
            return self.bind[name]
        return self.nc.dram_tensor(self.prefix + name, list(shape), F32, kind="ExternalOutput").ap()

    def scratch(self, name, shape):
        return self.nc.dram_tensor(name, list(shape), F32)

    def sb(self, shape, dtype=F32, persist=False):
        self.n += 1
        return (self.esp if persist else self.es).enter_context(self.nc.sbuf_tensor("sb%d" % self.n, list(shape), dtype))

    def ps(self, shape, dtype=F32):
        self.n += 1
        return self.es.enter_context(self.nc.psum_tensor("ps%d" % self.n, list(shape), dtype))

    def op(self, eng, fn, r=(), w=()):
        o = _Op(eng, fn)
        deps = []
        if eng not in self.fenced:
            deps.extend(self.fence_ops)
            self.fenced.add(eng)
        for k in r:
            lw = self.last_w.get(k)
            if lw is not None:
                deps.append(lw)
            if isinstance(k, tuple) and k[0] == "bank":
                for rd in self.readers.get(k, {}).values():
                    if rd.eng != eng:
                        deps.append(rd)
        for k in w:
            lw = self.last_w.get(k)
            if lw is not None:
                deps.append(lw)
            for rd in self.readers.get(k, {}).values():
                deps.append(rd)
        for d in deps:
            if d is o:
                continue
            if d.flushed and d.dsem is None and not d.inc:
                d = d.last
            if d.eng == eng and eng != "sp" and d.dsem is None and not d.flushed:
                if not any(self.last_w.get(k) is d for k in r):
                    continue
            o.deps.append(d)
            if d.dsem is None:
                d.inc = True
        for k in r:
            self.readers.setdefault(k, {})[eng] = o
        for k in w:
            self.last_w[k] = o
            self.readers[k] = {}
        self.ops[eng].append(o)
        return o

    def _async(self, o, semkey, inc):
        if semkey not in self.dma_sems:
            self.dma_sems[semkey] = self.esp.enter_context(self.nc.semaphore("dq%d" % len(self.dma_sems)))
            self.dma_cnt[semkey] = 0
        self.dma_cnt[semkey] += inc
        o.dsem, o.dval, o.dinc = self.dma_sems[semkey], self.dma_cnt[semkey], inc
        self.touched.add(semkey)
        return o

    def dma(self, out, in_, semkey, r=(), w=(), is_out=False, eng="sp"):
        o = self.op(eng, lambda e: e.dma_start(out=out, in_=in_), r=r, w=w)
        self._async(o, semkey, 16)
        if is_out:
            self.out_dmas.append(o)
        return o

    def coll(self, kind, src, dst, groups, semkey, r=(), w=()):
        o = self.op("pool", lambda e: e.collective_compute(kind, ALU.bypass, replica_groups=groups, ins=[src.opt()], outs=[dst.opt()]), r=r, w=w)
        return self._async(o, semkey, 1)

    def flush(self, final=False):
        nc = self.nc
        if self.esem is None:
            self.esem = {e: self.esp.enter_context(nc.semaphore("e_" + e)) for e in self.ENG}
        esem = self.esem
        lasts = {}
        for e in self.ENG:
            comp = [o for o in self.ops[e] if o.dsem is None]
            if comp:
                comp[-1].inc = True
                lasts[e] = comp[-1]
            c = self.seqc[e]
            for o in self.ops[e]:
                if o.dsem is None and o.inc:
                    c += 1
                    o.seq = c
            self.seqc[e] = c
        ops, out_dmas = self.ops, self.out_dmas

        def emit(ename, e):
            seen = self.seen[ename]
            for o in ops[ename]:
                need = {}
                for d in o.deps:
                    if d.dsem is not None:
                        s, v = d.dsem, d.dval
                    else:
                        s, v = esem[d.eng], d.seq
                    key = id(s)
                    if need.get(key, (None, 0))[1] < v:
                        need[key] = (s, v)
                for key, (s, v) in need.items():
                    if seen.get(key, 0) < v:
                        e.wait_ge(s, v)
                        seen[key] = v
                ins = o.fn(e)
                if o.dsem is not None:
                    if o.dinc == 16:
                        ins.then_inc(o.dsem, 16)
                    else:
                        ins.then_inc(o.dsem)
                elif o.inc:
                    ins.then_inc(esem[ename], 1)
            if ename == "sp" and final:
                fin = {}
                for o in out_dmas:
                    if fin.get(id(o.dsem), (None, 0))[1] < o.dval:
                        fin[id(o.dsem)] = (o.dsem, o.dval)
                for s, v in fin.values():
                    e.wait_ge(s, v)

        with nc.Block() as block:
            if ops["pe"]:
                @block.tensor
                def _(e):
                    emit("pe", e)
            if ops["act"]:
                @block.scalar
                def _(e):
                    emit("act", e)
            if ops["dve"]:
                @block.vector
                def _(e):
                    emit("dve", e)
            if ops["pool"]:
                @block.gpsimd
                def _(e):
                    emit("pool", e)
            if ops["sp"] or final:
                @block.sync
                def _(e):
                    emit("sp", e)
        fence = list(lasts.values())
        for k in self.touched:
            f = _Op("sp", None)
            f.dsem, f.dval, f.flushed = self.dma_sems[k], self.dma_cnt[k], True
            fence.append(f)
        self.touched = set()
        for e in self.ENG:
            for o in self.ops[e]:
                o.flushed = True
                o.last = lasts.get(e, o)
                o.fn = None
            self.ops[e] = []
        self.fence_ops = fence
        self.fenced = set()
        self.es.close()
        self.es = ExitStack()

    def finish(self):
        if self.fused:
            self.flush()
            return self
        self.flush(final=True)
        self.esp.close()
        return self.nc

    def close(self):
        self.flush(final=True)
        self.esp.close()
        return self.nc


NT = 512
KC = D // 128
F32R = mybir.dt.float32r
FAST = True


WDT_ = F32


def RR(ap):
    return ap.bitcast(F32R) if FAST else ap


class TL:
    def __init__(self, P):
        self.P = P
        self.x = P.sb([128, KC, NT])
        self.h = P.sb([128, KC, NT])
        self.a = P.sb([128, 22, NT])
        WDT = F32R if FAST else F32
        self.wa = [P.sb([128, KC, 128], WDT) for _ in range(2)]
        self.wb = [P.sb([128, KC, 128], WDT) for _ in range(2)]
        self.wd = [P.sb([128, 22, 128], WDT) for _ in range(2)]
        self.sq = [P.sb([128, NT]) for _ in range(2)]
        self.rstd = P.sb([128, NT])
        self.tmp = [P.sb([128, NT]) for _ in range(2)]
        self.ones = P.sb([128, 128])
        self.gs = P.sb([128, KC])
        self.ps_ss = P.ps([128, NT])
        self.ps_g = [P.ps([128, NT]) for _ in range(2)]
        self.ps_u = [P.ps([128, NT]) for _ in range(2)]
        self.ps_o = [P.ps([128, NT]) for _ in range(2)]
        self.cnt = 0
        P.op("pool", lambda e: e.memset(self.ones[:], 1.0), w=["ones"])

    def xkeys(self):
        return [("x", k) for k in range(KC)]

    def hkeys(self):
        return [("h", k) for k in range(KC)]

    def rms_modulate(self, g_ap, scale_ap, shift_ap, vkey):
        P = self.P
        x, h = self.x, self.h
        for k in range(KC):
            s = self.sq[k % 2]
            P.op("act", lambda e, s=s, k=k: e.activation(out=s[:], in_=x[:, k, :], func=AF.Square),
                 r=[("x", k)], w=[("sq", k % 2)])
            P.op("pe", lambda e, s=s, k=k: e.matmul(self.ps_ss[:], lhsT=self.ones[:], rhs=s[:],
                                                     start=(k == 0), stop=(k == KC - 1)),
                 r=["ones", ("sq", k % 2)], w=["ps_ss"])
        t0 = self.tmp[0]
        P.op("act", lambda e: e.activation(out=t0[:], in_=self.ps_ss[:], func=AF.Sqrt, scale=1.0 / D, bias=EPS),
             r=["ps_ss"], w=[("tmp", 0)])
        P.op("dve", lambda e: e.reciprocal(out=self.rstd[:], in_=t0[:]), r=[("tmp", 0)], w=["rstd"])
        P.op("dve", lambda e: e.scalar_tensor_tensor(out=self.gs[:], in0=scale_ap, scalar=1.0, in1=g_ap,
                                                      op0=ALU.add, op1=ALU.mult), r=[vkey], w=["gs"])
        for k in range(KC):
            t = self.tmp[k % 2]
            P.op("dve", lambda e, t=t, k=k: e.tensor_tensor(out=t[:], in0=x[:, k, :], in1=self.rstd[:], op=ALU.mult),
                 r=[("x", k), "rstd"], w=[("tmp", k % 2)])
            P.op("dve", lambda e, t=t, k=k: e.tensor_scalar(out=RR(h[:, k, :]), in0=t[:], scalar1=self.gs[:, k:k + 1],
                                                             scalar2=shift_ap[:, k:k + 1], op0=ALU.mult, op1=ALU.add),
                 r=[("tmp", k % 2), "gs", vkey], w=[("h", k)])

    def load_w(self, buf, key, src):
        m = src.shape[1]
        self.P.dma(buf[:, :, 0:m], src.rearrange("(k p) m -> p k m", p=128), semkey=key, w=[key], eng=("pool" if FAST else "sp"))

    def round(self, ap, key):
        self.rcnt = getattr(self, "rcnt", 0) + 1
        if self.rcnt % 2:
            self.P.op("pool", lambda e: e.tensor_copy(out=RR(ap), in_=ap), r=[key], w=[key])
        else:
            self.P.op("act", lambda e: e.activation(out=RR(ap), in_=ap, func=AF.Copy), r=[key], w=[key])

    def ffn(self, wg, wu, wd, gate_ap, vkey):
        P = self.P
        x, h, a = self.x, self.h, self.a
        for half in range(2):
            for jj in range(22):
                j = half * 22 + jj
                c = self.cnt
                self.cnt += 1
                wa, wb = self.wa[c % 2], self.wb[c % 2]
                self.load_w(wa, ("wa", c % 2), wg[:, j * 128:(j + 1) * 128])
                self.load_w(wb, ("wb", c % 2), wu[:, j * 128:(j + 1) * 128])
                pg, pu = self.ps_g[c % 2], self.ps_u[c % 2]
                for k in range(KC):
                    P.op("pe", lambda e, k=k, wa=wa, pg=pg: e.matmul(pg[:], lhsT=RR(wa[:, k, :]), rhs=RR(h[:, k, :]),
                                                                   start=(k == 0), stop=(k == KC - 1)),
                         r=[("wa", c % 2), ("h", k)], w=[("ps_g", c % 2)])
                for k in range(KC):
                    P.op("pe", lambda e, k=k, wb=wb, pu=pu: e.matmul(pu[:], lhsT=RR(wb[:, k, :]), rhs=RR(h[:, k, :]),
                                                                   start=(k == 0), stop=(k == KC - 1)),
                         r=[("wb", c % 2), ("h", k)], w=[("ps_u", c % 2)])
                s = self.sq[c % 2]
                P.op("act", lambda e, s=s, pg=pg: e.activation(out=s[:], in_=pg[:], func=AF.Silu),
                     r=[("ps_g", c % 2)], w=[("sq", c % 2)])
                P.op("dve", lambda e, s=s, pu=pu, jj=jj: e.tensor_tensor(out=RR(a[:, jj, :]), in0=s[:], in1=pu[:], op=ALU.mult),
                     r=[("sq", c % 2), ("ps_u", c % 2)], w=[("a", jj)])
            for m in range(KC):
                c = self.cnt
                self.cnt += 1
                wdb = self.wd[c % 2]
                P.dma(wdb[:, :, :], wd[half * 2816:(half + 1) * 2816, m * 128:(m + 1) * 128].rearrange("(k p) m -> p k m", p=128),
                      semkey=("wd", c % 2), w=[("wd", c % 2)], eng=("pool" if FAST else "sp"))
                po = self.ps_o[c % 2]
                for jj in range(22):
                    P.op("pe", lambda e, jj=jj, wdb=wdb, po=po: e.matmul(po[:], lhsT=RR(wdb[:, jj, :]), rhs=RR(a[:, jj, :]),
                                                                       start=(jj == 0), stop=(jj == 21)),
                         r=[("wd", c % 2), ("a", jj)], w=[("ps_o", c % 2)])
                P.op("dve", lambda e, m=m, po=po: e.scalar_tensor_tensor(out=x[:, m, :], in0=po[:], scalar=gate_ap[:, m:m + 1],
                                                                       in1=x[:, m, :], op0=ALU.mult, op1=ALU.add),
                     r=[("ps_o", c % 2), vkey, ("x", m)], w=[("x", m)])

    def proj(self, w, ncols, out_fn):
        P = self.P
        h = self.h
        nch = (ncols + 127) // 128
        for m in range(nch):
            mc = min(128, ncols - m * 128)
            c = self.cnt
            self.cnt += 1
            wa = self.wa[c % 2]
            self.load_w(wa, ("wa", c % 2), w[:, m * 128:m * 128 + mc])
            po = self.ps_o[c % 2]
            for k in range(KC):
                P.op("pe", lambda e, k=k, wa=wa, po=po, mc=mc: e.matmul(po[0:mc, :], lhsT=(RR(wa[:, k, 0:mc]) if mc == 128 else wa[:, k, 0:mc].bitcast(F32)), rhs=(RR(h[:, k, :]) if mc == 128 else h[:, k, :]),
                                                                      start=(k == 0), stop=(k == KC - 1)),
                     r=[("wa", c % 2), ("h", k)], w=[("ps_o", c % 2)])
            out_fn(m, mc, po, ("ps_o", c % 2))


def build_t1(NB):
    P = Prog()
    xT = P.din("xT", [NB, 128, KC, NT])
    mv = P.din("mv", [NB, 128, 5, KC])
    gv = P.din("gv", [128, 2, KC])
    wg = P.din("wg", [D, DFF], WDT_)
    wu = P.din("wu", [D, DFF], WDT_)
    wd = P.din("wd", [DFF, D], WDT_)
    win = P.din("win", [D, DIN], WDT_)
    xo = P.dout("xo", [NB, 128, KC, NT])
    po_ = P.dout("pT", [NB, DIN, NT])
    T = TL(P)
    mvt = P.sb([128, 5, KC])
    gvt = P.sb([128, 2, KC])
    hg = P.sb([128, KC])
    ot = [P.sb([128, NT]) for _ in range(2)]
    P.dma(gvt[:], gv, semkey="gv", w=["gv"])
    for b in range(NB):
        P.dma(T.x[:], xT[b], semkey="x", w=T.xkeys())
        P.dma(mvt[:], mv[b], semkey="mv", w=["mv"])
        P.op("pool", lambda e: e.tensor_scalar(out=hg[:], in0=mvt[:, 2, :], scalar1=0.5, scalar2=None, op0=ALU.mult),
             r=["mv"], w=["hg"])
        T.rms_modulate(gvt[:, 0, :], mvt[:, 1, :], mvt[:, 0, :], "mv")
        T.ffn(wg, wu, wd, hg, "hg")
        P.dma(xo[b], T.x[:], semkey="xo", r=T.xkeys(), is_out=True)
        T.rms_modulate(gvt[:, 1, :], mvt[:, 4, :], mvt[:, 3, :], "mv")

        def out_fn(m, mc, ps, pskey, b=b):
            o = ot[m % 2]
            P.op("act", lambda e: e.activation(out=o[0:mc, :], in_=ps[0:mc, :], func=AF.Copy),
                 r=[pskey], w=[("ot", m % 2)])
            P.dma(po_[b, m * 128:m * 128 + mc, :], o[0:mc, :], semkey=("ot", m % 2), r=[("ot", m % 2)], is_out=True)
        T.proj(win, DIN, out_fn)
    return P.finish()


def build_t2(NB):
    P = Prog()
    xT = P.din("xT", [NB, 128, KC, NT])
    oT = P.din("oT", [NB, 128, KC, NT])
    mv = P.din("mv", [NB, 128, 4, KC])
    gv = P.din("gv", [128, KC])
    gw = P.din("gw", [512, 512])
    gb = P.din("gb", [128, 4])
    wo = P.din("wo", [D, D], WDT_)
    wg = P.din("wg", [D, DFF], WDT_)
    wu = P.din("wu", [D, DFF], WDT_)
    wd = P.din("wd", [DFF, D], WDT_)
    xo = P.dout("xo", [NB, 128, KC, NT])
    T = TL(P)
    mvt = P.sb([128, 4, KC])
    gvt = P.sb([128, KC])
    gwt = P.sb([128, 4, 512])
    gbt = P.sb([128, 4])
    hg = P.sb([128, KC])
    glu = P.sb([128, 4, NT])
    P.dma(gvt[:], gv, semkey="gv", w=["gv"])
    P.dma(gwt[:], gw.rearrange("(k p) m -> p k m", p=128), semkey="gw", w=["gw"])
    P.dma(gbt[:], gb, semkey="gb", w=["gb"])
    for b in range(NB):
        P.dma(T.x[:], xT[b], semkey="x", w=T.xkeys())
        P.dma(RR(T.h[:]), oT[b], semkey="h", w=T.hkeys(), eng=("pool" if FAST else "sp"))
        P.dma(mvt[:], mv[b], semkey="mv", w=["mv"])
        P.op("pool", lambda e: e.tensor_scalar(out=hg[:], in0=mvt[:, 3, :], scalar1=0.5, scalar2=None, op0=ALU.mult),
             r=["mv"], w=["hg"])
        for m in range(4):
            po = T.ps_g[m % 2]
            for k in range(4):
                P.op("pe", lambda e, m=m, k=k, po=po: e.matmul(po[:], lhsT=gwt[:, k, m * 128:(m + 1) * 128], rhs=T.h[:, 12 + k, :],
                                                             start=(k == 0), stop=(k == 3)),
                     r=["gw", ("h", 12 + k)], w=[("ps_g", m % 2)])
            P.op("act", lambda e, m=m, po=po: e.activation(out=glu[:, m, :], in_=po[:], func=AF.Sigmoid, bias=gbt[:, m:m + 1]),
                 r=[("ps_g", m % 2), "gb"], w=[("glu", m)])
        for m in range(4):
            P.op("dve", lambda e, m=m: e.tensor_tensor(out=RR(T.h[:, 12 + m, :]), in0=T.h[:, 12 + m, :], in1=glu[:, m, :], op=ALU.mult),
                 r=[("h", 12 + m), ("glu", m)], w=[("h", 12 + m)])

        def out_fn(m, mc, ps, pskey):
            P.op("dve", lambda e: e.scalar_tensor_tensor(out=T.x[:, m, :], in0=ps[:], scalar=mvt[:, 0, m:m + 1], in1=T.x[:, m, :],
                                                          op0=ALU.mult, op1=ALU.add),
                 r=[pskey, "mv", ("x", m)], w=[("x", m)])
        T.proj(wo, D, out_fn)
        T.rms_modulate(gvt[:, :], mvt[:, 2, :], mvt[:, 1, :], "mv")
        T.ffn(wg, wu, wd, hg, "hg")
        P.dma(xo[b], T.x[:], semkey="xo", r=T.xkeys(), is_out=True)
    return P.finish()


ADA_N = NMOD * D // 8


def build_ada(L):
    P = Prog()
    cT = P.din("cT", [128, KC, 3])
    w = P.din("w", [L, D, ADA_N])
    bb = P.din("b", [L, 3, ADA_N])
    out = P.dout("mod", [L, 3, ADA_N])
    ct = P.sb([128, KC, 3])
    sc = P.sb([128, KC, 3])
    wt = [P.sb([128, KC, 512]) for _ in range(2)]
    bt = P.sb([3, L, ADA_N])
    ot = P.sb([3, L, ADA_N])
    ps = [P.ps([3, 512]) for _ in range(2)]
    P.dma(ct[:], cT, semkey="c", w=["c"])
    P.dma(bt[:], bb.rearrange("l r n -> r l n"), semkey="b", w=["b"])
    P.op("act", lambda e: e.activation(out=sc[:], in_=ct[:], func=AF.Silu), r=["c"], w=["sc"])
    c = 0
    for l in range(L):
        for n0 in range(0, ADA_N, 512):
            nn = min(512, ADA_N - n0)
            wb, pb = wt[c % 2], ps[c % 2]
            P.dma(wb[:, :, 0:nn], w[l, :, n0:n0 + nn].rearrange("(k p) n -> p k n", p=128), semkey=("w", c % 2), w=[("w", c % 2)])
            for k in range(KC):
                P.op("pe", lambda e, k=k, wb=wb, pb=pb, nn=nn: e.matmul(pb[:, 0:nn], lhsT=sc[:, k, :], rhs=wb[:, k, 0:nn],
                                                                      start=(k == 0), stop=(k == KC - 1)),
                     r=["sc", ("w", c % 2)], w=[("ps", c % 2)])
            P.op("dve", lambda e, pb=pb, l=l, n0=n0, nn=nn: e.tensor_tensor(out=ot[:, l, n0:n0 + nn], in0=pb[:, 0:nn],
                                                                           in1=bt[:, l, n0:n0 + nn], op=ALU.add),
                 r=[("ps", c % 2), "b"], w=["ot"])
            c += 1
    P.dma(out.rearrange("l r n -> r l n"), ot[:], semkey="o", r=["ot"], is_out=True)
    return P.finish()


TLAT = 8192
TCTX = 256
TQ = TLAT + TCTX
NQT = TQ // 128
NEG = -1e30


def na_specs():
    types = {0: 0, 1: 1, 62: 3, 63: 4}
    specs = []
    for i in range(64):
        r0 = 2 * i
        lo = min(max(r0 - 4, 0), 120)
        hi = min(max(r0 + 1 - 4, 0), 120) + 7
        nch = (hi - lo + 2) // 2
        ty = types.get(i, 2)
        specs.append(([lo // 2 + c for c in range(nch)], ty))
    return specs


NA_REP = [0, 1, 2, 62, 63]
NA_NCH = [4, 4, 5, 4, 4]


def na_bias_tables(rpb):
    tabs = []
    q = np.arange(128)
    qr, qc = q // 64, q % 64
    for ty, i in enumerate(NA_REP):
        r0 = 2 * i
        lo = min(max(r0 - 4, 0), 120)
        r = r0 + qr
        rs = np.clip(r - 4, 0, 120)
        cs = np.clip(qc - 8, 0, 48)
        for c in range(NA_NCH[ty]):
            kl = np.arange(128)
            kr = lo + 2 * c + kl // 64
            kc = kl % 64
            valid = ((kr[:, None] >= rs[None, :]) & (kr[:, None] <= rs[None, :] + 7) &
                     (kc[:, None] >= cs[None, :]) & (kc[:, None] <= cs[None, :] + 15))
            bidx = (kr[:, None] - r[None, :] + 7) * 31 + (kc[:, None] - qc[None, :] + 15)
            bidx = np.where(valid, bidx, 0)
            t = np.where(valid[None], rpb[:, bidx], np.float32(NEG)).astype(np.float32)
            tabs.append(t)
    return np.stack(tabs, axis=2)


def wa_bias_table():
    j = np.arange(128)[:, None]
    q = np.arange(128)[None, :]
    prev = np.where(j >= q, 0.0, NEG)
    cur = np.zeros((128, 128))
    nxt = np.where(j <= q, 0.0, NEG)
    return np.stack([prev, cur, nxt], axis=1).astype(np.float32)


def rope_tables():
    t = np.arange(TLAT)
    inv = (1.0 / (np.float32(10000.0) ** (np.arange(16, dtype=np.float32) / np.float32(16)))).astype(np.float32)

    def ang(p):
        a = p.astype(np.float32)[:, None] * inv[None, :]
        return np.concatenate([a, a], -1)
    a = np.concatenate([ang(t // 64), ang(t % 64)], -1)
    return np.cos(a).astype(np.float32), np.sin(a).astype(np.float32)


def rot_matrix():
    m = np.zeros((128, 128), np.float32)
    for o in range(128):
        if o % 32 < 16:
            m[o + 16, o] = -1.0
        else:
            m[o - 16, o] = 1.0
    return m


def build_attn(kind):
    rope = kind == "C"
    P = Prog()
    NS = 2 * sum(NA_NCH) if kind == "A" else 3
    qT = P.din("qT", [128, TQ])
    kT = P.din("kT", [128, TQ])
    vtm = P.din("vtm", [TQ, 128])
    gq = P.din("gq", [128, 1])
    gk = P.din("gk", [128, 1])
    btab = P.din("btab", [128, NS, 128])
    if rope:
        cosT = P.din("cosT", [128, TLAT])
        sinT = P.din("sinT", [128, TLAT])
        prot = P.din("prot", [128, 128])
        sink = P.din("sink", [128, 2])
    o_tm = P.dout("o", [TQ, 128])

    qn = P.sb([128, TQ])
    kn = P.sb([128, TQ])
    V1 = P.sb([128, NQT, 2, 65])
    BT = P.sb([128, NS, 128])
    gqt = P.sb([128, 1])
    gkt = P.sb([128, 1])
    bones = P.sb([128, 128])
    raw = [P.sb([128, 512]) for _ in range(2)]
    sq = P.sb([128, 512])
    st = P.sb([128, 512])
    rs = P.sb([128, 512])
    tmpS = [P.sb([128, 640]) for _ in range(2)]
    E = [P.sb([128, 896]) for _ in range(2)]
    ot = [P.sb([128, 128]) for _ in range(2)]
    den = [P.sb([128, 2]) for _ in range(2)]
    banks = [P.ps([128, 512]) for _ in range(8)]

    P.dma(gqt[:], gq, semkey="gq", w=["gq"])
    P.dma(gkt[:], gk, semkey="gk", w=["gk"])
    P.dma(BT[:], btab, semkey="bt", w=["BT"])
    for h in range(2):
        P.dma(V1[:, :, h, 0:64], vtm[:, h * 64:(h + 1) * 64].rearrange("(t p) d -> p t d", p=128), semkey=("v", h), w=["V1"])
    P.op("pool", lambda e: e.memset(V1[:, :, :, 64:65], 1.0), w=["V1"])
    P.op("pool", lambda e: e.memset(bones[:], 0.0), w=["bones"])
    P.op("pool", lambda e: e.memset(bones[0:64, 0:64], 1.0), w=["bones"])
    P.op("pool", lambda e: e.memset(bones[64:128, 64:128], 1.0), w=["bones"])
    if rope:
        prt = P.sb([128, 128])
        cs = [P.sb([128, 512]) for _ in range(2)]
        sn = [P.sb([128, 512]) for _ in range(2)]
        t1 = P.sb([128, 512])
        t2 = P.sb([128, 512])
        skt = P.sb([128, 2])
        ske = P.sb([128, 2])
        P.dma(prt[:], prot, semkey="prot", w=["prot"])
        P.dma(skt[:], sink, semkey="sink", w=["sink"])
        P.op("act", lambda e: e.activation(out=ske[:], in_=skt[:], func=AF.Exp), r=["sink"], w=["ske"])

    c = 0
    for src, dst, g, nm in ((qT, qn, gqt, "qn"), (kT, kn, gkt, "kn")):
        for t0 in range(0, TQ, 512):
            nt = min(512, TQ - t0)
            rw = raw[c % 2]
            P.dma(rw[:, 0:nt], src[:, t0:t0 + nt], semkey=("raw", c % 2), w=[("raw", c % 2)])
            P.op("pool", lambda e, rw=rw, nt=nt: e.tensor_tensor(out=sq[:, 0:nt], in0=rw[:, 0:nt], in1=rw[:, 0:nt], op=ALU.mult),
                 r=[("raw", c % 2)], w=["sq"])
            P.op("pe", lambda e, nt=nt: e.matmul(banks[0][:, 0:nt], lhsT=bones[:], rhs=sq[:, 0:nt], start=True, stop=True),
                 r=["bones", "sq"], w=[("bank", 0)])
            P.op("act", lambda e, nt=nt: e.activation(out=st[:, 0:nt], in_=banks[0][:, 0:nt], func=AF.Sqrt, scale=1.0 / 64, bias=EPS),
                 r=[("bank", 0)], w=["st"])
            P.op("dve", lambda e, nt=nt: e.reciprocal(out=rs[:, 0:nt], in_=st[:, 0:nt]), r=["st"], w=["rs"])
            P.op("dve", lambda e, rw=rw, nt=nt, t0=t0, dst=dst, g=g: e.scalar_tensor_tensor(
                out=dst[:, t0:t0 + nt], in0=rw[:, 0:nt], scalar=g[:, 0:1], in1=rs[:, 0:nt], op0=ALU.mult, op1=ALU.mult),
                r=[("raw", c % 2), "rs", "gq", "gk"], w=[(nm, t0)])
            if rope and t0 < TLAT:
                P.dma(cs[c % 2][:], cosT[:, t0:t0 + 512], semkey=("cs", c % 2), w=[("cs", c % 2)])
                P.dma(sn[c % 2][:], sinT[:, t0:t0 + 512], semkey=("sn", c % 2), w=[("sn", c % 2)])
                P.op("pe", lambda e, t0=t0, dst=dst: e.matmul(banks[1][:], lhsT=prt[:], rhs=dst[:, t0:t0 + 512], start=True, stop=True),
                     r=["prot", (nm, t0)], w=[("bank", 1)])
                P.op("dve", lambda e, t0=t0, dst=dst, cc=cs[c % 2]: e.tensor_tensor(out=t1[:], in0=dst[:, t0:t0 + 512], in1=cc[:], op=ALU.mult),
                     r=[(nm, t0), ("cs", c % 2)], w=["t1"])
                P.op("dve", lambda e, ss=sn[c % 2]: e.tensor_tensor(out=t2[:], in0=banks[1][:], in1=ss[:], op=ALU.mult),
                     r=[("bank", 1), ("sn", c % 2)], w=["t2"])
                P.op("pool", lambda e, t0=t0, dst=dst: e.tensor_tensor(out=dst[:, t0:t0 + 512], in0=t1[:], in1=t2[:], op=ALU.add),
                     r=["t1", "t2"], w=[(nm, t0)])
            c += 1

    def nkeys(nm, tile):
        return [(nm, (tile * 128) // 512 * 512)]

    if kind == "A":
        specs = na_specs()
        base = np.concatenate([[0], np.cumsum(NA_NCH)])[:5]
        ntab = sum(NA_NCH)
    it = 0
    for i in range(NQT):
        if i < 64:
            if kind == "A":
                ktiles, ty = specs[i]
            else:
                ktiles = [k for k in (i - 1, i, i + 1) if 0 <= k < 64]
        else:
            ktiles = []
        nW = len(ktiles)
        par = i % 2
        po = banks[6 + par]
        for h in range(2):
            hs = slice(h * 64, (h + 1) * 64)
            ip = it % 2
            it += 1
            bA, bB = banks[2 + 2 * ip], banks[3 + 2 * ip]
            if kind == "A":
                s0 = (h * ntab + int(base[ty])) if nW else 0
            else:
                s0 = 1 if i == 0 else 0

            def sl(s):
                return (bA, s * 128) if s < 4 else (bB, (s - 4) * 128)
            chunks = [(kt, s) for s, kt in enumerate(ktiles)] + [(64, 5), (65, 6)]
            for kt, s in chunks:
                bk, off = sl(s)
                P.op("pe", lambda e, bk=bk, off=off, kt=kt, hs=hs, i=i: e.matmul(
                    bk[:, off:off + 128], lhsT=kn[hs, kt * 128:(kt + 1) * 128], rhs=qn[hs, i * 128:(i + 1) * 128], start=True, stop=True),
                    r=nkeys("kn", kt) + nkeys("qn", i), w=[("bank", 2 + 2 * ip + (0 if s < 4 else 1))])
            nA = min(nW, 4)
            Eb, tS = E[ip], tmpS[ip]
            if nA:
                P.op("dve", lambda e, nA=nA, s0=s0, tS=tS, bA=bA: e.scalar_tensor_tensor(
                    out=tS[:, 0:nA * 128], in0=bA[:, 0:nA * 128], scalar=0.125, in1=BT[:, s0:s0 + nA, :].rearrange("p s q -> p (s q)"),
                    op0=ALU.mult, op1=ALU.add), r=[("bank", 2 + 2 * ip), "BT"], w=[("tS", ip)])
            if nW == 5:
                P.op("dve", lambda e, s0=s0, tS=tS, bB=bB: e.scalar_tensor_tensor(
                    out=tS[:, 512:640], in0=bB[:, 0:128], scalar=0.125, in1=BT[:, s0 + 4, :],
                    op0=ALU.mult, op1=ALU.add), r=[("bank", 3 + 2 * ip), "BT"], w=[("tS", ip)])
            if nW:
                P.op("act", lambda e, nW=nW, Eb=Eb, tS=tS: e.activation(out=Eb[:, 0:nW * 128], in_=tS[:, 0:nW * 128], func=AF.Exp),
                     r=[("tS", ip)], w=[("E", ip)])
            P.op("act", lambda e, Eb=Eb, bB=bB: e.activation(out=Eb[:, 640:896], in_=bB[:, 128:384], func=AF.Exp, scale=0.125),
                 r=[("bank", 3 + 2 * ip)], w=[("E", ip)])
            for n, (kt, s) in enumerate(chunks):
                eo = s * 128
                P.op("pe", lambda e, n=n, kt=kt, eo=eo, Eb=Eb, h=h, po=po: e.matmul(
                    po[:, h * 65:(h + 1) * 65], lhsT=Eb[:, eo:eo + 128], rhs=V1[:, kt, h, :], start=(n == 0), stop=(n == len(chunks) - 1)),
                    r=[("E", ip), "V1"], w=[("bank", 6 + par)])
        dn, o = den[par], ot[par]
        pov = po[:, 0:130].rearrange("p (h c) -> p h c", h=2)
        if rope:
            P.op("dve", lambda e, dn=dn, pov=pov: e.tensor_tensor(out=dn[:, :].rearrange("p (h o) -> p h o", o=1), in0=pov[:, :, 64:65],
                                                                 in1=ske[:, :].rearrange("p (h o) -> p h o", o=1), op=ALU.add),
                 r=[("bank", 6 + par), "ske"], w=[("den", par)])
            P.op("dve", lambda e, dn=dn: e.reciprocal(out=dn[:], in_=dn[:]), r=[("den", par)], w=[("den", par)])
        else:
            P.op("dve", lambda e, dn=dn, pov=pov: e.reciprocal(out=dn[:, :].rearrange("p (h o) -> p h o", o=1), in_=pov[:, :, 64:65]),
                 r=[("bank", 6 + par)], w=[("den", par)])
        for h in range(2):
            P.op("dve", lambda e, h=h, dn=dn, o=o, po=po: e.tensor_scalar(out=o[:, h * 64:(h + 1) * 64], in0=po[:, h * 65:h * 65 + 64],
                                                                       scalar1=dn[:, h:h + 1], scalar2=None, op0=ALU.mult),
                 r=[("bank", 6 + par), ("den", par)], w=[("ot", par)])
        P.dma(o_tm[i * 128:(i + 1) * 128, :], o[:], semkey=("ot", par), r=[("ot", par)], is_out=True)
    return P.finish()


def attn_inputs(kind, plat, pctx, j, prm):
    def fm(cols):
        return np.ascontiguousarray(np.concatenate([plat[cols], pctx[cols]], axis=1))
    a = np.arange(128)
    if kind == "A":
        qc, kc, vc = j * 128 + a, 512 + j * 128 + a, 1024 + j * 128 + a
        tabs = prm["na_tabs"]
        d = dict(btab=np.ascontiguousarray(np.concatenate([tabs[2 * j], tabs[2 * j + 1]], axis=1)))
        gq, gk = prm["na_q_g"], prm["na_k_g"]
    else:
        kvh = j // 2
        a64 = np.tile(np.arange(64), 2)
        qc, kc, vc = 3232 + j * 128 + a, 3744 + kvh * 64 + a64, 3872 + kvh * 64 + a64
        d = dict(btab=prm["wa_tab"], cosT=prm["cosT"], sinT=prm["sinT"], prot=prm["prot"],
                 sink=np.ascontiguousarray(np.broadcast_to(prm["wa_sink"][2 * j:2 * j + 2][None, :], (128, 2))))
        gq, gk = prm["wa_q_g"], prm["wa_k_g"]
    d.update(qT=fm(qc), kT=fm(kc), vtm=np.ascontiguousarray(fm(vc).T),
             gq=np.ascontiguousarray(np.tile(gq, 2)[:, None]), gk=np.ascontiguousarray(np.tile(gk, 2)[:, None]))
    return d


def attn_consts():
    cos, sin = rope_tables()
    return dict(wa_tab=wa_bias_table(), cosT=np.ascontiguousarray(np.tile(cos.T, (2, 1))),
                sinT=np.ascontiguousarray(np.tile(sin.T, (2, 1))), prot=rot_matrix())


I32 = mybir.dt.int32
S5_BLOCKS = [(0, TCTX)] + [(TCTX + 512 * i, 512) for i in range(16)]
TWO_PI = 2.0 * np.pi


def build_s5():
    P = Prog()
    uTd = P.din("uT", [128, TQ])
    bre = P.din("bre", [128, 4, 128])
    bim = P.din("bim", [128, 4, 128])
    cre = P.din("cre", [128, 4, 128])
    cim = P.din("cim", [128, 4, 128])
    pare = P.din("are", [128, 8])
    paim = P.din("aim", [128, 8])
    pldt = P.din("ldt", [128, 8])
    pdsk = P.din("dsk", [128, 1])
    yo = P.dout("yT", [128, TQ])

    uT = P.sb([128, TQ])
    yacc = P.sb([128, TQ])
    re = [P.sb([128, TQ]) for _ in range(2)]
    nim = P.sb([128, TQ])
    Bre, Bim, Cre, Cim = (P.sb([128, 4, 128]) for _ in range(4))
    small = {}

    def sm(name, shape=(128, 8), dt=F32):
        small[name] = P.sb(list(shape), dt)
        return small[name]
    are, aim, ldt, dsk = sm("are"), sm("aim"), sm("ldt"), sm("dsk", (128, 1))
    NL = 14
    pwr, pwi, npwi = sm("pwr", (128, NL, 8)), sm("pwi", (128, NL, 8)), sm("npwi", (128, NL, 8))
    tmp = [P.sb([128, 512]) for _ in range(2)]
    g1 = [P.sb([128, 512]) for _ in range(2)]
    g2 = [P.sb([128, 512]) for _ in range(2)]
    ob = [P.sb([128, 512]) for _ in range(2)]
    banks = [P.ps([128, 512]) for _ in range(6)]

    P.dma(uT[:], uTd, semkey="u", w=["uT"])
    for t, s, k in ((Bre, bre, "Bre"), (Bim, bim, "Bim"), (Cre, cre, "Cre"), (Cim, cim, "Cim"),
                    (are, pare, "are"), (aim, paim, "aim"), (ldt, pldt, "ldt"), (dsk, pdsk, "dsk")):
        P.dma(t[:], s, semkey=k, w=[k])

    def V(name, fn, r, w):
        P.op("dve", fn, r=r, w=w)

    dt_, dre, mag, ang = sm("dt"), sm("dre"), sm("mag"), sm("ang")
    P.op("act", lambda e: e.activation(out=dt_[:], in_=ldt[:], func=AF.Exp), r=["ldt"], w=["dt"])
    V("", lambda e: e.tensor_tensor(out=dre[:], in0=dt_[:], in1=are[:], op=ALU.mult), ["dt", "are"], ["dre"])
    P.op("act", lambda e: e.activation(out=mag[:], in_=dre[:], func=AF.Exp), r=["dre"], w=["mag"])
    V("", lambda e: e.tensor_tensor(out=ang[:], in0=dt_[:], in1=aim[:], op=ALU.mult), ["dt", "aim"], ["ang"])

    def sin_of(dst, shift, nm):
        z, ki, kf, r_, m_ = sm(nm + "z"), sm(nm + "ki", dt=I32), sm(nm + "kf"), sm(nm + "r"), sm(nm + "m")
        V("", lambda e: e.tensor_scalar(out=z[:], in0=ang[:], scalar1=1.0 / TWO_PI, scalar2=shift / TWO_PI + 0.5, op0=ALU.mult, op1=ALU.add),
          ["ang"], [nm + "z"])
        V("", lambda e: e.tensor_copy(out=ki[:], in_=z[:]), [nm + "z"], [nm + "ki"])
        V("", lambda e: e.tensor_copy(out=kf[:], in_=ki[:]), [nm + "ki"], [nm + "kf"])
        V("", lambda e: e.scalar_tensor_tensor(out=r_[:], in0=kf[:], scalar=-TWO_PI, in1=ang[:], op0=ALU.mult, op1=ALU.add),
          [nm + "kf", "ang"], [nm + "r"])
        if shift:
            V("", lambda e: e.tensor_scalar(out=r_[:], in0=r_[:], scalar1=shift, scalar2=None, op0=ALU.add), [nm + "r"], [nm + "r"])
        V("", lambda e: e.tensor_scalar(out=m_[:], in0=r_[:], scalar1=-np.pi, scalar2=TWO_PI, op0=ALU.is_lt, op1=ALU.mult), [nm + "r"], [nm + "m"])
        V("", lambda e: e.tensor_tensor(out=r_[:], in0=r_[:], in1=m_[:], op=ALU.add), [nm + "r", nm + "m"], [nm + "r"])
        V("", lambda e: e.tensor_scalar(out=m_[:], in0=r_[:], scalar1=np.pi, scalar2=-TWO_PI, op0=ALU.is_gt, op1=ALU.mult), [nm + "r"], [nm + "m"])
        V("", lambda e: e.tensor_tensor(out=r_[:], in0=r_[:], in1=m_[:], op=ALU.add), [nm + "r", nm + "m"], [nm + "r"])
        P.op("act", lambda e: e.activation(out=dst[:], in_=r_[:], func=AF.Sin), r=[nm + "r"], w=[nm + "s"])
    sn_, cs_ = sm("sn"), sm("cs")
    sin_of(sn_, 0.0, "sn")
    sin_of(cs_, np.pi / 2, "cs")
    V("", lambda e: e.tensor_tensor(out=pwr[:, 0, :], in0=mag[:], in1=cs_[:], op=ALU.mult), ["mag", "css"], ["pwr"])
    V("", lambda e: e.tensor_tensor(out=pwi[:, 0, :], in0=mag[:], in1=sn_[:], op=ALU.mult), ["mag", "sns"], ["pwi"])
    den, nr, t1, t2, cfr, cfi, ncfr, ncfi = (sm(n) for n in ("den", "nr", "t1", "t2", "cfr", "cfi", "ncfr", "ncfi"))
    V("", lambda e: e.tensor_tensor(out=den[:], in0=are[:], in1=are[:], op=ALU.mult), ["are"], ["den"])
    V("", lambda e: e.tensor_tensor(out=t1[:], in0=aim[:], in1=aim[:], op=ALU.mult), ["aim"], ["t1"])
    V("", lambda e: e.tensor_tensor(out=den[:], in0=den[:], in1=t1[:], op=ALU.add), ["den", "t1"], ["den"])
    V("", lambda e: e.reciprocal(out=den[:], in_=den[:]), ["den"], ["den"])
    V("", lambda e: e.tensor_scalar(out=nr[:], in0=pwr[:, 0, :], scalar1=-1.0, scalar2=None, op0=ALU.add), ["pwr"], ["nr"])
    V("", lambda e: e.tensor_tensor(out=t1[:], in0=nr[:], in1=are[:], op=ALU.mult), ["nr", "are", "den"], ["t1"])
    V("", lambda e: e.tensor_tensor(out=t2[:], in0=pwi[:, 0, :], in1=aim[:], op=ALU.mult), ["pwi", "aim"], ["t2"])
    V("", lambda e: e.tensor_tensor(out=t1[:], in0=t1[:], in1=t2[:], op=ALU.add), ["t1", "t2"], ["t1"])
    V("", lambda e: e.tensor_tensor(out=cfr[:], in0=t1[:], in1=den[:], op=ALU.mult), ["t1", "den"], ["cfr"])
    V("", lambda e: e.tensor_tensor(out=t1[:], in0=pwi[:, 0, :], in1=are[:], op=ALU.mult), ["pwi", "are", "cfr"], ["t1"])
    V("", lambda e: e.tensor_tensor(out=t2[:], in0=nr[:], in1=aim[:], op=ALU.mult), ["nr", "aim", "cfr"], ["t2"])
    V("", lambda e: e.tensor_tensor(out=t1[:], in0=t1[:], in1=t2[:], op=ALU.subtract), ["t1", "t2"], ["t1"])
    V("", lambda e: e.tensor_tensor(out=cfi[:], in0=t1[:], in1=den[:], op=ALU.mult), ["t1", "den"], ["cfi"])
    V("", lambda e: e.tensor_scalar(out=ncfr[:], in0=cfr[:], scalar1=-1.0, scalar2=None, op0=ALU.mult), ["cfr"], ["ncfr"])
    V("", lambda e: e.tensor_scalar(out=ncfi[:], in0=cfi[:], scalar1=-1.0, scalar2=None, op0=ALU.mult), ["cfi"], ["ncfi"])
    for l in range(1, NL):
        V("", lambda e, l=l: e.tensor_tensor(out=t1[:], in0=pwr[:, l - 1, :], in1=pwr[:, l - 1, :], op=ALU.mult), ["pwr", "cfi", "ncfi"], ["t1"])
        V("", lambda e, l=l: e.tensor_tensor(out=t2[:], in0=pwi[:, l - 1, :], in1=pwi[:, l - 1, :], op=ALU.mult), ["pwi", "cfi", "ncfi"], ["t2"])
        V("", lambda e, l=l: e.tensor_tensor(out=pwi[:, l, :], in0=pwr[:, l - 1, :], in1=pwi[:, l - 1, :], op=ALU.mult), ["pwr", "pwi"], ["pwi"])
        V("", lambda e, l=l: e.tensor_tensor(out=pwr[:, l, :], in0=t1[:], in1=t2[:], op=ALU.subtract), ["t1", "t2"], ["pwr"])
        V("", lambda e, l=l: e.tensor_scalar(out=pwi[:, l, :], in0=pwi[:, l, :], scalar1=2.0, scalar2=None, op0=ALU.mult), ["pwi"], ["pwi"])
    V("", lambda e: e.tensor_scalar(out=npwi[:].rearrange("p l c -> p (l c)"), in0=pwi[:].rearrange("p l c -> p (l c)"), scalar1=-1.0, scalar2=None, op0=ALU.mult),
      ["pwi"], ["npwi"])

    V("", lambda e: e.tensor_scalar(out=yacc[:], in0=uT[:], scalar1=dsk[:, 0:1], scalar2=None, op0=ALU.mult), ["uT", "dsk"], ["yacc"])

    cnt = 0
    for d in range(2):
        for c in range(4):
            dc = d * 4 + c
            cur = 0
            for (t0, nt) in S5_BLOCKS:
                if d == 0:
                    o0 = t0
                else:
                    o0 = TLAT if t0 == 0 else t0 - TCTX
                p1, p2 = banks[(cnt % 2) * 2], banks[(cnt % 2) * 2 + 1]
                k1, k2 = ("bank", (cnt % 2) * 2), ("bank", (cnt % 2) * 2 + 1)
                tb = tmp[cnt % 2]
                cnt += 1
                P.op("pe", lambda e, p1=p1, c=c, t0=t0, nt=nt: e.matmul(p1[:, 0:nt], lhsT=Bre[:, c, :], rhs=uT[:, t0:t0 + nt], start=True, stop=True),
                     r=["Bre", "uT"], w=[k1])
                P.op("pe", lambda e, p2=p2, c=c, t0=t0, nt=nt: e.matmul(p2[:, 0:nt], lhsT=Bim[:, c, :], rhs=uT[:, t0:t0 + nt], start=True, stop=True),
                     r=["Bim", "uT"], w=[k2])
                V("", lambda e, tb=tb, p2=p2, nt=nt, dc=dc: e.tensor_scalar(out=tb[:, 0:nt], in0=p2[:, 0:nt], scalar1=cfi[:, dc:dc + 1], scalar2=None, op0=ALU.mult),
                  [k2, "cfi"], [("tmp", id(tb))])
                V("", lambda e, tb=tb, p1=p1, nt=nt, dc=dc, o0=o0: e.scalar_tensor_tensor(
                    out=re[0][:, o0:o0 + nt], in0=p1[:, 0:nt], scalar=cfr[:, dc:dc + 1], in1=tb[:, 0:nt], op0=ALU.mult, op1=ALU.subtract),
                    [k1, "cfr", ("tmp", id(tb))], ["re0"])
                V("", lambda e, tb=tb, p1=p1, nt=nt, dc=dc: e.tensor_scalar(out=tb[:, 0:nt], in0=p1[:, 0:nt], scalar1=ncfi[:, dc:dc + 1], scalar2=None, op0=ALU.mult),
                  [k1, "ncfi"], [("tmp", id(tb))])
                V("", lambda e, tb=tb, p2=p2, nt=nt, dc=dc, o0=o0: e.scalar_tensor_tensor(
                    out=nim[:, o0:o0 + nt], in0=p2[:, 0:nt], scalar=ncfr[:, dc:dc + 1], in1=tb[:, 0:nt], op0=ALU.mult, op1=ALU.add),
                    [k2, "ncfr", ("tmp", id(tb))], ["nim"])
            for l in range(NL):
                s = 1 << l
                n = TQ - s
                ro, rn = re[cur], re[1 - cur]
                kro, krn = "re%d" % cur, "re%d" % (1 - cur)
                ar, ai, nai = pwr[:, l, dc:dc + 1], pwi[:, l, dc:dc + 1], npwi[:, l, dc:dc + 1]
                if d == 0:
                    dst, src, keep = slice(s, TQ), slice(0, n), slice(0, s)
                else:
                    dst, src, keep = slice(0, n), slice(s, TQ), slice(n, TQ)
                V("", lambda e, ro=ro, rn=rn, ar=ar, dst=dst, src=src: e.scalar_tensor_tensor(
                    out=rn[:, dst], in0=ro[:, src], scalar=ar, in1=ro[:, dst], op0=ALU.mult, op1=ALU.add), [kro, "pwr"], [krn])
                V("", lambda e, rn=rn, ai=ai, dst=dst, src=src: e.scalar_tensor_tensor(
                    out=rn[:, dst], in0=nim[:, src], scalar=ai, in1=rn[:, dst], op0=ALU.mult, op1=ALU.add), [krn, "nim", "pwi"], [krn])
                P.op("pool", lambda e, ro=ro, rn=rn, keep=keep: e.tensor_copy(out=rn[:, keep], in_=ro[:, keep]), r=[kro], w=[krn])
                if d == 0:
                    dsr, ssr = slice(TQ - 1, s - 1, -1), (slice(n - 1, None, -1))
                else:
                    dsr, ssr = dst, src
                V("", lambda e, ar=ar, dsr=dsr, ssr=ssr: e.scalar_tensor_tensor(
                    out=nim[:, dsr], in0=nim[:, ssr], scalar=ar, in1=nim[:, dsr], op0=ALU.mult, op1=ALU.add), ["nim", "pwr"], ["nim"])
                V("", lambda e, ro=ro, nai=nai, dst=dst, src=src: e.scalar_tensor_tensor(
                    out=nim[:, dst], in0=ro[:, src], scalar=nai, in1=nim[:, dst], op0=ALU.mult, op1=ALU.add), ["nim", kro, "npwi"], ["nim"])
                cur = 1 - cur
            xr, kxr = re[cur], "re%d" % cur
            for (t0, nt) in S5_BLOCKS:
                if d == 0:
                    o0 = t0
                else:
                    o0 = TLAT if t0 == 0 else t0 - TCTX
                pb, kb = banks[4 + cnt % 2], ("bank", 4 + cnt % 2)
                cnt += 1
                P.op("pe", lambda e, pb=pb, c=c, o0=o0, nt=nt, xr=xr: e.matmul(pb[:, 0:nt], lhsT=Cre[:, c, :], rhs=xr[:, o0:o0 + nt], start=True, stop=False),
                     r=["Cre", kxr], w=[kb])
                P.op("pe", lambda e, pb=pb, c=c, o0=o0, nt=nt: e.matmul(pb[:, 0:nt], lhsT=Cim[:, c, :], rhs=nim[:, o0:o0 + nt], start=False, stop=True),
                     r=["Cim", "nim"], w=[kb])
                V("", lambda e, pb=pb, t0=t0, nt=nt: e.tensor_tensor(out=yacc[:, t0:t0 + nt], in0=pb[:, 0:nt], in1=yacc[:, t0:t0 + nt], op=ALU.add),
                  [kb, "yacc"], ["yacc"])
            if cur != 0:
                re[0], re[1] = re[1], re[0]

    for n_, (t0, nt) in enumerate(S5_BLOCKS):
        a, b_, o = g1[n_ % 2], g2[n_ % 2], ob[n_ % 2]
        ka, kb2, ko = ("g1", n_ % 2), ("g2", n_ % 2), ("ob", n_ % 2)
        ys = yacc[:, t0:t0 + nt]
        P.op("pool", lambda e, a=a, ys=ys, nt=nt: e.tensor_tensor(out=a[:, 0:nt], in0=ys, in1=ys, op=ALU.mult), r=["yacc"], w=[ka])
        P.op("pool", lambda e, a=a, nt=nt: e.tensor_scalar(out=a[:, 0:nt], in0=a[:, 0:nt], scalar1=0.044715, scalar2=1.0, op0=ALU.mult, op1=ALU.add), r=[ka], w=[ka])
        P.op("pool", lambda e, a=a, ys=ys, nt=nt: e.tensor_tensor(out=a[:, 0:nt], in0=a[:, 0:nt], in1=ys, op=ALU.mult), r=[ka, "yacc"], w=[ka])
        P.op("act", lambda e, a=a, b_=b_, nt=nt: e.activation(out=b_[:, 0:nt], in_=a[:, 0:nt], func=AF.Sigmoid, scale=1.5957691216057308), r=[ka], w=[kb2])
        V("", lambda e, b_=b_, o=o, ys=ys, nt=nt: e.tensor_tensor(out=o[:, 0:nt], in0=b_[:, 0:nt], in1=ys, op=ALU.mult), [kb2, "yacc"], [ko])
        P.dma(yo[:, t0:t0 + nt], o[:, 0:nt], semkey=ko, r=[ko], is_out=True)
    return P.finish()


def s5_inputs(ulat, uctx, j, prm):
    ch = slice(j * 128, (j + 1) * 128)
    G0 = 8 * j
    bre = np.zeros((128, 4, 128), np.float32)
    bim = np.zeros((128, 4, 128), np.float32)
    cre = np.zeros((128, 4, 128), np.float32)
    cim = np.zeros((128, 4, 128), np.float32)
    for c in range(4):
        for g2 in range(2):
            g = 2 * c + g2
            bre[g * 16:(g + 1) * 16, c, g2 * 64:(g2 + 1) * 64] = prm["s5_b_re"][G0 + g].T
            bim[g * 16:(g + 1) * 16, c, g2 * 64:(g2 + 1) * 64] = prm["s5_b_im"][G0 + g].T
            cre[g2 * 64:(g2 + 1) * 64, c, g * 16:(g + 1) * 16] = prm["s5_c_re"][G0 + g].T
            cim[g2 * 64:(g2 + 1) * 64, c, g * 16:(g + 1) * 16] = prm["s5_c_im"][G0 + g].T

    def st(a):
        a = a[:, G0:G0 + 8].reshape(2, 4, 2, 64)
        return np.ascontiguousarray(a.transpose(2, 3, 0, 1).reshape(128, 8))
    ldt = np.broadcast_to(prm["s5_log_dt"][:, :, None], (2, 32, 64))
    return dict(uT=np.ascontiguousarray(np.concatenate([uctx[ch], ulat[ch]], axis=1)), bre=bre, bim=bim, cre=cre, cim=cim,
                are=st(prm["s5_a_re"]), aim=st(prm["s5_a_im"]), ldt=st(ldt),
                dsk=np.ascontiguousarray(prm["s5_d"][ch][:, None]))


RW_T = TCTX + TLAT
RW_NCH = RW_T // 64
RW_BLOCKS = [(0, TCTX, 0)] + [(TCTX + 512 * i, 512, TCTX + 2 + 512 * i) for i in range(16)]
RW_PADT = TCTX + 2 + TLAT + 2
NEG_SQRT_E = -0.6065306597126334


def rw_consts():
    i = np.arange(128) % 64
    jj = np.arange(128) % 64
    strict = (np.arange(128) < 64)[None, :]
    I, J = i[:, None], jj[None, :]
    mf = np.where(strict, I < J, I <= J)
    mb = np.where(strict, I > J, I >= J)
    mask = np.stack([mf, mb], axis=1).astype(np.float32)
    a = np.arange(64)
    nmask = np.stack([a[:, None] > a[None, :], a[:, None] < a[None, :]], axis=1).astype(np.float32)
    m0 = np.ones((128, 512), np.float32)
    m0[:, ::64] = 0.0
    bones = np.zeros((128, 128), np.float32)
    bones[:64, :64] = 1.0
    bones[64:, 64:] = 1.0
    hsel = np.zeros((128, 2), np.float32)
    hsel[:64, 0] = 1.0
    hsel[64:, 1] = 1.0
    ib = np.concatenate([np.eye(64, dtype=np.float32)] * 2, axis=0)
    return dict(mask=mask, nmask=nmask, m0=m0, bones=bones, hsel=hsel, ib=ib, ident=np.eye(128, dtype=np.float32))


def build_rwkv(nblk=17, dbg=99):
    P = Prog()
    din = P.din
    rP, kP, vP = din("rP", [128, RW_PADT]), din("kP", [128, RW_PADT]), din("vP", [128, RW_PADT])
    dwT, daT, dgT = din("dwT", [32, RW_T]), din("daT", [32, RW_T]), din("dgT", [96, RW_T])
    cwD, wupD, aupD, gupD = din("cw", [128, 9]), din("wup", [32, 2, 128]), din("aup", [32, 2, 128]), din("gup", [96, 128])
    w0D, a0D, kkwD, kaD, rkD = din("w0", [128, 2]), din("a0", [128, 2]), din("kkw", [128, 1]), din("ka", [128, 1]), din("rk", [128, 1])
    gnwD, gnbD = din("gnw", [64, 128]), din("gnb", [64, 128])
    maskD, nmaskD, m0D, bonesD, hselD, ibD, identD = (din("mask", [128, 2, 128]), din("nmask", [64, 2, 64]), din("m0", [128, 512]),
                                                      din("bones", [128, 128]), din("hsel", [128, 2]), din("ib", [128, 64]), din("ident", [128, 128]))
    o_tm = P.dout("o", [RW_T, 128])

    def load(shape, src, key):
        t = P.sb(shape)
        P.dma(t[:], src, semkey=key, w=[key])
        return t
    cw, wup, aup, gup = load([128, 9], cwD, "cw"), load([32, 2, 128], wupD, "wup"), load([32, 2, 128], aupD, "aup"), load([96, 128], gupD, "gup")
    w0, a0, kkw, ka, rk = load([128, 2], w0D, "w0"), load([128, 2], a0D, "a0"), load([128, 1], kkwD, "kkw"), load([128, 1], kaD, "ka"), load([128, 1], rkD, "rk")
    gnw, gnb = load([64, 128], gnwD, "gnw"), load([64, 128], gnbD, "gnb")
    MASK, NMASK, mask0 = load([128, 2, 128], maskD, "mask"), load([64, 2, 64], nmaskD, "nmask"), load([128, 512], m0D, "m0")
    bones, HSEL, IB, ident = load([128, 128], bonesD, "bones"), load([128, 2], hselD, "hsel"), load([128, 64], ibD, "ib"), load([128, 128], identD, "ident")

    bk = [P.ps([128, 512]) for _ in range(8)]

    def B(i):
        return ("bank", i)
    tiles = {}

    def T(name, shape=(128, 512)):
        if name not in tiles:
            tiles[name] = P.sb(list(shape))
        return tiles[name]
    R1 = [P.sb([128, 8, 128]) for _ in range(2)]
    L1 = [P.sb([128, 8, 128]) for _ in range(2)]
    Z1 = [P.sb([128, 8, 128]) for _ in range(2)]
    Z2 = [P.sb([128, 8, 128]) for _ in range(2)]
    WL = [P.sb([128, 8]) for _ in range(2)]
    RKK = [P.sb([128, 512]) for _ in range(2)]
    SG = [P.sb([96, 512]) for _ in range(2)]
    XV = [P.sb([128, 2, 128]) for _ in range(2)]
    BK = [P.sb([128, 2, 64]) for _ in range(2)]
    ATs = [P.sb([128, 2, 128]) for _ in range(2)]
    PP0s = [P.sb([64, 2, 2, 64]) for _ in range(2)]
    PPs = [[P.sb([64, 2, 2, 64]) for _ in range(2)] for _ in range(2)]
    MTss = [P.sb([128, 64]) for _ in range(2)]
    RTss = [P.sb([128, 64]) for _ in range(2)]
    S = [P.sb([128, 64]) for _ in range(2)]
    Yf = P.sb([64, RW_NCH, 128])
    P.op("pool", lambda e: e.memset(S[0][:], 0.0), w=[("S", 0)])

    def v3(ap, nch):
        return ap[:, 0:nch * 64].rearrange("p (c j) -> p c j", j=64)

    def prep(bi, d, par):
        s0, nt, p0 = RW_BLOCKS[bi]
        nch = nt // 64
        raws = {}
        for nm, src in (("r", rP), ("k", kP), ("v", vP)):
            t = T("raw" + nm, (128, 514))
            P.dma(t[:, 0:nt + 2], src[:, p0 - 1 + 0:p0 - 1 + nt + 2] if False else src[:, p0:p0 + nt + 2], semkey="raw" + nm, w=["raw" + nm])
            raws[nm] = t
        dw, da = T("dw", (32, 512)), T("da", (32, 512))
        P.dma(dw[:, 0:nt], dwT[:, s0:s0 + nt], semkey="dw", w=["dw"])
        P.dma(da[:, 0:nt], daT[:, s0:s0 + nt], semkey="da", w=["da"])
        outs = {"r": T("rc"), "k": T("kc"), "v": T("vc")}
        for ai, nm in enumerate(("r", "k", "v")):
            rw, o = raws[nm], outs[nm]
            P.op("dve", lambda e, rw=rw, o=o, ai=ai: e.tensor_scalar(out=o[:, 0:nt], in0=rw[:, 0:nt], scalar1=cw[:, 3 * ai:3 * ai + 1], scalar2=None, op0=ALU.mult),
                 r=["raw" + nm, "cw"], w=[nm + "c"])
            for tap in (1, 2):
                P.op("dve", lambda e, rw=rw, o=o, ai=ai, tap=tap: e.scalar_tensor_tensor(
                    out=o[:, 0:nt], in0=rw[:, tap:tap + nt], scalar=cw[:, 3 * ai + tap:3 * ai + tap + 1], in1=o[:, 0:nt], op0=ALU.mult, op1=ALU.add),
                    r=["raw" + nm, "cw", nm + "c"], w=[nm + "c"])
        rc, kc, vc = outs["r"], outs["k"], outs["v"]
        P.op("pool", lambda e: e.tensor_copy(out=Z1[par][:, 0:nch, 64:128], in_=v3(vc, nch)), r=["vc"], w=[("Z1", par)])
        th = T("th", (32, 512))
        P.op("act", lambda e: e.activation(out=th[:, 0:nt], in_=dw[:, 0:nt], func=AF.Tanh), r=["dw"], w=["th"])
        dirs = [0] if d == 0 else [0, 1]
        aa, kt = {}, {}
        for dd in dirs:
            aa[dd] = T("a%d" % dd)
            P.op("pe", lambda e, dd=dd: e.matmul(bk[0][:, 0:nt], lhsT=aup[:, dd, :], rhs=da[:, 0:nt], start=True, stop=True), r=["aup", "da"], w=[B(0)])
            P.op("act", lambda e, dd=dd: e.activation(out=aa[dd][:, 0:nt], in_=bk[0][:, 0:nt], func=AF.Sigmoid, bias=a0[:, dd:dd + 1]),
                 r=[B(0), "a0"], w=["a%d" % dd])
        sw, lw = T("sw"), T("lw")
        P.op("pe", lambda e: e.matmul(bk[0][:, 0:nt], lhsT=wup[:, d, :], rhs=th[:, 0:nt], start=True, stop=True), r=["wup", "th"], w=[B(0)])
        P.op("act", lambda e: e.activation(out=sw[:, 0:nt], in_=bk[0][:, 0:nt], func=AF.Sigmoid, bias=w0[:, d:d + 1]), r=[B(0), "w0"], w=["sw"])
        P.op("dve", lambda e: e.tensor_scalar(out=lw[:, 0:nt], in0=sw[:, 0:nt], scalar1=NEG_SQRT_E, scalar2=None, op0=ALU.mult), r=["sw"], w=["lw"])
        kkr, sq, nr, kk = T("kkr"), T("sq"), T("nr"), T("kk")
        P.op("dve", lambda e: e.tensor_scalar(out=kkr[:, 0:nt], in0=kc[:, 0:nt], scalar1=kkw[:, 0:1], scalar2=None, op0=ALU.mult), r=["kc", "kkw"], w=["kkr"])
        P.op("pool", lambda e: e.tensor_tensor(out=sq[:, 0:nt], in0=kkr[:, 0:nt], in1=kkr[:, 0:nt], op=ALU.mult), r=["kkr"], w=["sq"])
        P.op("pe", lambda e: e.matmul(bk[0][:, 0:nt], lhsT=bones[:], rhs=sq[:, 0:nt], start=True, stop=True), r=["bones", "sq"], w=[B(0)])
        P.op("act", lambda e: e.activation(out=nr[:, 0:nt], in_=bk[0][:, 0:nt], func=AF.Sqrt), r=[B(0)], w=["nr"])
        P.op("dve", lambda e: e.tensor_scalar(out=nr[:, 0:nt], in0=nr[:, 0:nt], scalar1=1e-12, scalar2=None, op0=ALU.max), r=["nr"], w=["nr"])
        P.op("dve", lambda e: e.reciprocal(out=nr[:, 0:nt], in_=nr[:, 0:nt]), r=["nr"], w=["nr"])
        P.op("dve", lambda e: e.tensor_tensor(out=kk[:, 0:nt], in0=kkr[:, 0:nt], in1=nr[:, 0:nt], op=ALU.mult), r=["kkr", "nr"], w=["kk"])
        for dd in dirs:
            kt[dd] = T("kt%d" % dd)
            tk = T("tk")
            P.op("pool", lambda e, dd=dd: e.tensor_scalar(out=tk[:, 0:nt], in0=aa[dd][:, 0:nt], scalar1=-1.0, scalar2=ka[:, 0:1], op0=ALU.add, op1=ALU.mult),
                 r=["a%d" % dd, "ka"], w=["tk"])
            P.op("dve", lambda e, dd=dd: e.scalar_tensor_tensor(out=kt[dd][:, 0:nt], in0=tk[:, 0:nt], scalar=1.0, in1=kc[:, 0:nt], op0=ALU.add, op1=ALU.mult),
                 r=["tk", "kc"], w=["kt%d" % dd])
        beta = T("beta")
        P.op("pool", lambda e: e.tensor_tensor(out=beta[:, 0:nt], in0=kk[:, 0:nt], in1=aa[d][:, 0:nt], op=ALU.mult), r=["kk", "a%d" % d], w=["beta"])
        lWf, lW, lWex, dl = T("lWf"), T("lW"), T("lWex"), T("dl")
        P.op("dve", lambda e: e.tensor_tensor_scan(out=lWf[:, 0:nt], data0=mask0[:, 0:nt], data1=lw[:, 0:nt], initial=0.0, op0=ALU.mult, op1=ALU.add),
             r=["m0", "lw"], w=["lWf"])
        tot = v3(lWf, nch)[:, :, 63:64]
        totb = tot.broadcast_to([128, nch, 64])
        if d == 0:
            lW = lWf
            klW = "lWf"
        else:
            klW = "lW"
            P.op("dve", lambda e: e.tensor_tensor(out=v3(lW, nch), in0=totb, in1=v3(lWf, nch), op=ALU.subtract), r=["lWf"], w=["lW"])
            P.op("dve", lambda e: e.tensor_tensor(out=lW[:, 0:nt], in0=lW[:, 0:nt], in1=lw[:, 0:nt], op=ALU.add), r=["lW", "lw"], w=["lW"])
        P.op("pool", lambda e: e.tensor_tensor(out=lWex[:, 0:nt], in0=lW[:, 0:nt], in1=lw[:, 0:nt], op=ALU.subtract), r=[klW, "lw"], w=["lWex"])
        P.op("dve", lambda e: e.tensor_tensor(out=v3(dl, nch), in0=totb, in1=v3(lW, nch), op=ALU.subtract), r=["lWf", klW], w=["dl"])
        E1, E2, E3, E4 = T("E1"), T("E2"), T("E3"), T("E4")
        P.op("act", lambda e: e.activation(out=E1[:, 0:nt], in_=lWex[:, 0:nt], func=AF.Exp), r=["lWex"], w=["E1"])
        P.op("act", lambda e: e.activation(out=E2[:, 0:nt], in_=lW[:, 0:nt], func=AF.Exp), r=[klW], w=["E2"])
        P.op("act", lambda e: e.activation(out=E3[:, 0:nt], in_=lW[:, 0:nt], func=AF.Exp, scale=-1.0), r=[klW], w=["E3"])
        P.op("act", lambda e: e.activation(out=E4[:, 0:nt], in_=dl[:, 0:nt], func=AF.Exp), r=["dl"], w=["E4"])
        P.op("act", lambda e: e.activation(out=WL[par][:, 0:nch].rearrange("p (c o) -> p c o", o=1), in_=tot, func=AF.Exp), r=["lWf"], w=[("WL", par)])
        P.op("dve", lambda e: e.scalar_tensor_tensor(out=R1[par][:, 0:nch, 0:64], in0=v3(kk, nch), scalar=-1.0, in1=v3(E1, nch), op0=ALU.mult, op1=ALU.mult),
             r=["kk", "E1"], w=[("R1", par)])
        P.op("pool", lambda e: e.tensor_copy(out=Z1[par][:, 0:nch, 0:64], in_=R1[par][:, 0:nch, 0:64]), r=[("R1", par)], w=[("Z1", par)])
        P.op("dve", lambda e: e.tensor_tensor(out=R1[par][:, 0:nch, 64:128], in0=v3(rc, nch), in1=v3(E2, nch), op=ALU.mult), r=["rc", "E2"], w=[("R1", par)])
        P.op("pool", lambda e: e.tensor_tensor(out=L1[par][:, 0:nch, 0:64], in0=v3(beta, nch), in1=v3(E3, nch), op=ALU.mult), r=["beta", "E3"], w=[("L1", par)])
        P.op("dve", lambda e: e.tensor_tensor(out=L1[par][:, 0:nch, 64:128], in0=v3(kt[d], nch), in1=v3(E3, nch), op=ALU.mult), r=["kt%d" % d, "E3"], w=[("L1", par)])
        P.op("pool", lambda e: e.tensor_tensor(out=Z2[par][:, 0:nch, 0:64], in0=v3(beta, nch), in1=v3(E4, nch), op=ALU.mult), r=["beta", "E4"], w=[("Z2", par)])
        P.op("dve", lambda e: e.tensor_tensor(out=Z2[par][:, 0:nch, 64:128], in0=v3(kt[d], nch), in1=v3(E4, nch), op=ALU.mult), r=["kt%d" % d, "E4"], w=[("Z2", par)])
        if d == 1:
            dg = T("dg", (96, 512))
            P.dma(dg[:, 0:nt], dgT[:, s0:s0 + nt], semkey="dg", w=["dg"])
            P.op("act", lambda e: e.activation(out=SG[par][:, 0:nt], in_=dg[:, 0:nt], func=AF.Sigmoid), r=["dg"], w=[("SG", par)])
            ks = T("ks")
            P.op("pool", lambda e: e.tensor_tensor(out=ks[:, 0:nt], in0=kt[0][:, 0:nt], in1=kt[1][:, 0:nt], op=ALU.add), r=["kt0", "kt1"], w=["ks"])
            P.op("dve", lambda e: e.scalar_tensor_tensor(out=RKK[par][:, 0:nt], in0=rc[:, 0:nt], scalar=rk[:, 0:1], in1=ks[:, 0:nt], op0=ALU.mult, op1=ALU.mult),
                 r=["rc", "rk", "ks"], w=[("RKK", par)])

    state = {"cur": 0, "q": 0}

    def mm(out, lhsT, rhs, r, w, rows, start=True, stop=True):
        P.op("pe", lambda e: e.matmul(out, lhsT=lhsT, rhs=rhs, start=start, stop=stop), r=r, w=w)

    def chunk(bi, c, d, par):
        s0, nt, _ = RW_BLOCKS[bi]
        cg = s0 // 64 + c
        q = state["q"]
        state["q"] = 1 - q
        xv, bkq, ats = XV[q], BK[q], ATs[q]
        kxv, kbk, kat = ("XV", q), ("BK", q), ("ATs", q)
        PP0, PP, MTs, RTs = PP0s[q], PPs[q], MTss[q], RTss[q]
        kPP0, kMT, kRT = ("PP0", q), ("MTs", q), ("RTs", q)
        r1, l1, z1, z2 = R1[par], L1[par], Z1[par], Z2[par]
        if dbg < 1:
            return
        mm(bk[0][:, 0:128], z1[:, c, :], ident[:], [("Z1", par), "ident"], [B(0)], 128)
        mm(bk[0][:, 128:256], z2[:, c, :], ident[:], [("Z2", par), "ident"], [B(0)], 128)
        if d == 1:
            mm(bk[0][0:64, 256:384], z1[:, c, 64:128], ident[:], [("Z1", par), "ident"], [B(0)], 128)
        P.op("act", lambda e: e.activation(out=xv[0:64, :, 64:128], in_=bk[0][0:64, 0:128].rearrange("p (h k) -> p h k", h=2), func=AF.Copy),
             r=[B(0)], w=[kxv])
        P.op("act", lambda e: e.activation(out=xv[64:128, :, 0:64], in_=bk[0][64:128, 0:128].rearrange("p (h k) -> p h k", h=2), func=AF.Copy),
             r=[B(0)], w=[kxv])
        P.op("dve", lambda e: e.tensor_copy(out=bkq[:, :, :], in_=bk[0][:, 128:256].rearrange("p (h k) -> p h k", h=2)), r=[B(0)], w=[kbk])
        if d == 1:
            vtm = T("vtm%d" % q, (64, 128))
            P.op("act", lambda e: e.activation(out=vtm[:, :], in_=bk[0][0:64, 256:384], func=AF.Copy), r=[B(0)], w=[("vtm", q)])
        yield
        if dbg < 2:
            return
        for h in range(2):
            hs = slice(64 * h, 64 * h + 64)
            mm(bk[1 + h][:, 0:128], l1[hs, c, :], r1[hs, c, :], [("L1", par), ("R1", par)], [B(1 + h)], 64)
            mm(bk[1 + h][0:64, 128:192], r1[hs, c, 0:64], l1[hs, c, 0:64], [("L1", par), ("R1", par)], [B(1 + h)], 64)
        for h in range(2):
            P.op("dve", lambda e, h=h: e.tensor_tensor(out=ats[:, h, :], in0=bk[1 + h][:, 0:128], in1=MASK[:, d, :], op=ALU.mult), r=[B(1 + h), "mask"], w=[kat])
            P.op("dve", lambda e, h=h: e.tensor_tensor(out=PP0[:, h, 0, :], in0=bk[1 + h][0:64, 128:192], in1=NMASK[:, d, :], op=ALU.mult),
                 r=[B(1 + h), "nmask"], w=[kPP0])
        yield
        if dbg < 3:
            return
        for h in range(2):
            mm(bk[4][0:64, 64 * h:64 * h + 64], ats[64:128, h, 0:64], xv[64:128, h, 0:64], [kat, kxv], [B(4)], 64)
        P.op("act", lambda e: e.activation(out=xv[0:64, :, 0:64], in_=bk[4][0:64, 0:128].rearrange("p (h k) -> p h k", h=2), func=AF.Copy), r=[B(4)], w=[kxv])
        yield
        if dbg < 4:
            return
        ppc, kpp = PP0, kPP0
        for l in range(6):
            for h in range(2):
                pt = ats[0:64, h, 0:64] if l == 0 else ppc[:, h, 1, :]
                mm(bk[3][0:64, 128 * h:128 * h + 128], pt, xv[0:64, h, :], [kat, kpp, kxv], [B(3)], 64)
            if l < 5:
                for h in range(2):
                    pt = ats[0:64, h, 0:64] if l == 0 else ppc[:, h, 1, :]
                    pm = ppc[:, h, 0, :]
                    mm(bk[5][0:64, 128 * h:128 * h + 64], pt, pm, [kat, kpp], [B(5)], 64)
                    mm(bk[5][0:64, 128 * h + 64:128 * h + 128], pm, pt, [kat, kpp], [B(5)], 64)
            P.op("dve", lambda e: e.tensor_tensor(out=xv[0:64, :, :], in0=bk[3][0:64, 0:256].rearrange("p (h k) -> p h k", h=2), in1=xv[0:64, :, :], op=ALU.add),
                 r=[B(3), kxv], w=[kxv])
            if l < 5:
                nx = PP[l % 2]
                P.op("act", lambda e, nx=nx: e.activation(out=nx[:, :, :, :].rearrange("p h t k -> p (h t k)"), in_=bk[5][0:64, 0:256], func=AF.Copy),
                     r=[B(5)], w=[("PP", q, l % 2)])
                ppc, kpp = nx, ("PP", q, l % 2)
            yield
        yield
        if dbg < 5:
            return
        for h in range(2):
            hs = slice(64 * h, 64 * h + 64)
            mm(bk[6][hs, 0:64], xv[0:64, h, 64:128], bkq[0:64, h, :], [kxv, kbk], [B(6)], 64)
            mm(bk[6][hs, 64:128], xv[0:64, h, 64:128], ats[0:64, h, 64:128], [kxv, kat], [B(6)], 64)
        P.op("dve", lambda e: e.scalar_tensor_tensor(out=MTs[:, :], in0=IB[:, :], scalar=WL[par][:, c:c + 1], in1=bk[6][:, 0:64], op0=ALU.mult, op1=ALU.add),
             r=["ib", ("WL", par), B(6)], w=[kMT])
        P.op("dve", lambda e: e.tensor_tensor(out=RTs[:, :], in0=bk[6][:, 64:128], in1=r1[:, c, 64:128], op=ALU.add), r=[B(6), ("R1", par)], w=[kRT])
        if dbg < 6:
            return
        yield
        cur = state["cur"]
        for h in range(2):
            hs = slice(64 * h, 64 * h + 64)
            mm(bk[7][0:64, 64 * h:64 * h + 64], ats[:, h, 64:128], xv[:, h, 0:64], [kat, kxv], [B(7)], 128, True, False)
            mm(bk[7][0:64, 64 * h:64 * h + 64], RTs[hs, :], S[cur][hs, :], [kRT, ("S", cur)], [B(7)], 64, False, True)
        for h in range(2):
            hs = slice(64 * h, 64 * h + 64)
            mm(bk[6][hs, 128:192], bkq[:, h, :], xv[:, h, 0:64], [kbk, kxv], [B(6)], 128, True, False)
            mm(bk[6][hs, 128:192], MTs[hs, :], S[cur][hs, :], [kMT, ("S", cur)], [B(6)], 64, False, True)
        P.op("act", lambda e: e.activation(out=S[1 - cur][:, :], in_=bk[6][:, 128:192], func=AF.Copy), r=[B(6)], w=[("S", 1 - cur)])
        state["cur"] = 1 - cur
        if d == 0:
            P.op("act", lambda e: e.activation(out=Yf[:, cg, :], in_=bk[7][0:64, 0:128], func=AF.Copy), r=[B(7)], w=["Yf"])
            return
        if dbg < 7:
            return
        y, yc, sq, o = T("y%d" % q, (64, 128)), T("yc%d" % q, (64, 128)), T("ysq%d" % q, (64, 128)), T("o%d" % (cg % 2), (64, 128))
        mu, var, rstd = T("mu%d" % q, (64, 2)), T("var%d" % q, (64, 2)), T("rstd%d" % q, (64, 2))
        tl = slice(64 * c, 64 * c + 64)
        P.op("dve", lambda e: e.tensor_tensor(out=y[:, :], in0=bk[7][0:64, 0:128], in1=Yf[:, cg, :], op=ALU.add), r=[B(7), "Yf"], w=[("y", q)])
        yield
        mm(bk[0][0:64, 384:512], SG[par][:, tl], gup[:, :], [("SG", par), "gup"], [B(0)], 96)
        mm(bk[4][0:64, 128:130], RKK[par][:, tl], HSEL[:, :], [("RKK", par), "hsel"], [B(4)], 128)
        for h in range(2):
            hc = slice(64 * h, 64 * h + 64)
            P.op("dve", lambda e, h=h, hc=hc: e.scalar_tensor_tensor(out=y[:, hc], in0=vtm[:, hc], scalar=bk[4][0:64, 128 + h:129 + h], in1=y[:, hc],
                                                                   op0=ALU.mult, op1=ALU.add), r=[("vtm", q), B(4), ("y", q)], w=[("y", q)])
        P.op("dve", lambda e: e.tensor_reduce(out=mu[:, :], in_=y[:, :].rearrange("p (h k) -> p h k", h=2), axis=AX.X, op=ALU.add), r=[("y", q)], w=[("mu", q)])
        P.op("dve", lambda e: e.tensor_scalar(out=mu[:, :], in0=mu[:, :], scalar1=-1.0 / 64, scalar2=None, op0=ALU.mult), r=[("mu", q)], w=[("mu", q)])
        for h in range(2):
            hc = slice(64 * h, 64 * h + 64)
            P.op("dve", lambda e, h=h, hc=hc: e.tensor_scalar(out=yc[:, hc], in0=y[:, hc], scalar1=mu[:, h:h + 1], scalar2=None, op0=ALU.add), r=[("y", q), ("mu", q)], w=[("yc", q)])
        P.op("pool", lambda e: e.tensor_tensor(out=sq[:, :], in0=yc[:, :], in1=yc[:, :], op=ALU.mult), r=[("yc", q)], w=[("ysq", q)])
        P.op("dve", lambda e: e.tensor_reduce(out=var[:, :], in_=sq[:, :].rearrange("p (h k) -> p h k", h=2), axis=AX.X, op=ALU.add), r=[("ysq", q)], w=[("var", q)])
        P.op("act", lambda e: e.activation(out=rstd[:, :], in_=var[:, :], func=AF.Sqrt, scale=1.0 / 64, bias=64e-5), r=[("var", q)], w=[("rstd", q)])
        P.op("dve", lambda e: e.reciprocal(out=rstd[:, :], in_=rstd[:, :]), r=[("rstd", q)], w=[("rstd", q)])
        for h in range(2):
            hc = slice(64 * h, 64 * h + 64)
            P.op("dve", lambda e, h=h, hc=hc: e.scalar_tensor_tensor(out=yc[:, hc], in0=yc[:, hc], scalar=rstd[:, h:h + 1], in1=gnw[:, hc], op0=ALU.mult, op1=ALU.mult),
                 r=[("yc", q), ("rstd", q), "gnw"], w=[("yc", q)])
        P.op("pool", lambda e: e.tensor_tensor(out=yc[:, :], in0=yc[:, :], in1=gnb[:, :], op=ALU.add), r=[("yc", q), "gnb"], w=[("yc", q)])
        P.op("dve", lambda e: e.tensor_tensor(out=o[:, :], in0=bk[0][0:64, 384:512], in1=yc[:, :], op=ALU.mult), r=[B(0), ("yc", q)], w=[("o", cg % 2)])
        P.dma(o_tm[64 * cg:64 * cg + 64, :], o[:, :], semkey=("o", cg % 2), r=[("o", cg % 2)], is_out=True)

    nb = 0
    for d in range(2):
        order = list(range(17)) if d == 0 else [0] + list(range(16, 0, -1))
        for bi in order[:nblk]:
            par = nb % 2
            nb += 1
            prep(bi, d, par)
            nch = RW_BLOCKS[bi][1] // 64
            cl = list(range(nch) if d == 0 else range(nch - 1, -1, -1))
            for i0 in range(0, len(cl), 2):
                gens = [chunk(bi, c, d, par) for c in cl[i0:i0 + 2]]
                while gens:
                    for g in list(gens):
                        try:
                            next(g)
                        except StopIteration:
                            gens.remove(g)
        if d == 0:
            P.op("pool", lambda e: e.memset(S[state["cur"]][:], 0.0), w=[("S", state["cur"])])
    return P.finish()


def rwkv_inputs(plat, pctx, j, prm, cst):
    a = np.arange(128)
    cols = j * 128 + a

    def pad(c0):
        z = np.zeros((128, 1), np.float32)
        return np.ascontiguousarray(np.concatenate([z, pctx[c0 + cols], z, z, plat[c0 + cols], z], axis=1))

    def sfm(c0, n):
        return np.ascontiguousarray(np.concatenate([pctx[c0:c0 + n], plat[c0:c0 + n]], axis=1))
    cw = np.stack([prm["rw_conv"][tap, ai * 512 + cols] for ai in range(3) for tap in range(3)], axis=1)
    d = dict(rP=pad(1536), kP=pad(2048), vP=pad(2560), dwT=sfm(3072, 32), daT=sfm(3104, 32), dgT=sfm(3136, 96),
             cw=np.ascontiguousarray(cw), wup=np.ascontiguousarray(prm["rw_w_up"].transpose(1, 0, 2)[:, :, cols]),
             aup=np.ascontiguousarray(prm["rw_a_up"].transpose(1, 0, 2)[:, :, cols]), gup=np.ascontiguousarray(prm["rw_g_up"][:, cols]),
             w0=np.ascontiguousarray(prm["rw_w0"][:, cols].T), a0=np.ascontiguousarray(prm["rw_a0"][:, cols].T),
             kkw=np.ascontiguousarray(prm["rw_k_k"][cols][:, None]), ka=np.ascontiguousarray(prm["rw_k_a"][cols][:, None]),
             rk=np.ascontiguousarray(prm["rw_r_k"].reshape(512)[cols][:, None]),
             gnw=np.ascontiguousarray(np.broadcast_to(prm["rw_gn_w"][cols][None, :], (64, 128))),
             gnb=np.ascontiguousarray(np.broadcast_to(prm["rw_gn_b"][cols][None, :], (64, 128))))
    d.update(cst)
    return d


TC = 3
NBLK = 33
_PROGS = {}


def _prog(name, fn):
    if name not in _PROGS:
        _PROGS[name] = fn()
    return _PROGS[name]


def _fm(v):
    return np.swapaxes(v.reshape(v.shape[:-1] + (KC, 128)), -1, -2)


def _tok_to_blocks(t):
    return np.ascontiguousarray(t.reshape(NBLK, NT, KC, 128).transpose(0, 3, 2, 1))


def _blocks_to_tok(b):
    return b.transpose(0, 3, 2, 1).reshape(NBLK * NT, D)


def _run(nc, ims):
    return run_bass_kernel_spmd(nc, ims, core_ids=list(range(len(ims)))).results


def kernel(x, c, ctx, c_ctx, w_ada, b_ada, norm_g, ffn1_wg, ffn1_wu, ffn1_wd, ffn2_wg, ffn2_wu, ffn2_wd,
           w_in, w_out, na_q_g, na_k_g, na_rpb, rw_conv, rw_w0, rw_w_up, rw_a0, rw_a_up, rw_g_up,
           rw_k_k, rw_k_a, rw_r_k, rw_gn_w, rw_gn_b, wa_q_g, wa_k_g, wa_sink, s5_a_re, s5_a_im,
           s5_log_dt, s5_b_re, s5_b_im, s5_c_re, s5_c_im, s5_d, s5_glu_w, s5_glu_b):
    f = lambda a: np.ascontiguousarray(np.asarray(a, dtype=np.float32))
    x, c, ctx, c_ctx = f(x), f(c), f(ctx), f(c_ctx)
    L = w_ada.shape[0]
    NB = NBLK // TC
    c3 = np.stack([c[0], c[1], c_ctx])
    cT = np.ascontiguousarray(c3.reshape(3, KC, 128).transpose(2, 1, 0))
    w_ada, b_ada = np.asarray(w_ada), np.asarray(b_ada)
    ims = [dict(cT=cT, w=f(w_ada[:, :, i * ADA_N:(i + 1) * ADA_N]),
                b=f(np.broadcast_to(b_ada[:, None, i * ADA_N:(i + 1) * ADA_N], (L, 3, ADA_N)))) for i in range(8)]
    res = _run(_prog("ada", lambda: build_ada(L)), ims)
    mods = np.concatenate([r["mod"] for r in res], axis=-1).reshape(L, 3, NMOD, D)
    rows = np.array([0] * 16 + [1] * 16 + [2])

    toks = np.concatenate([x[0], x[1], ctx[0], ctx[1]], axis=0)
    xb = _tok_to_blocks(toks)
    cst_at = attn_consts()
    cst_rw = rw_consts()
    for l in range(L):
        ml = _fm(mods[l])
        mvb = ml[rows]
        mv1 = np.ascontiguousarray(mvb[:, 0:5].transpose(0, 2, 1, 3))
        mv2 = np.ascontiguousarray(mvb[:, 5:9].transpose(0, 2, 1, 3))
        gl = _fm(f(norm_g[l]))
        wg, wu, wd, win = f(ffn1_wg[l]), f(ffn1_wu[l]), f(ffn1_wd[l]), f(w_in[l])
        gv1 = np.ascontiguousarray(gl[0:2].transpose(1, 0, 2))
        ims = [dict(xT=np.ascontiguousarray(xb[i * NB:(i + 1) * NB]), mv=np.ascontiguousarray(mv1[i * NB:(i + 1) * NB]),
                    gv=gv1, wg=wg, wu=wu, wd=wd, win=win) for i in range(TC)]
        res = _run(_prog("t1", lambda: build_t1(NB)), ims)
        xb = np.concatenate([r["xo"] for r in res], axis=0)
        pT = np.concatenate([r["pT"] for r in res], axis=0)
        del wg, wu, wd, win, ims, res
        plat = [np.ascontiguousarray(pT[b * 16:(b + 1) * 16].transpose(1, 0, 2).reshape(DIN, TLAT)) for b in range(2)]
        pctx = [np.ascontiguousarray(pT[32][:, b * TCTX:(b + 1) * TCTX]) for b in range(2)]
        prm = dict(cst_at)
        prm.update(na_q_g=f(na_q_g[l]), na_k_g=f(na_k_g[l]), wa_q_g=f(wa_q_g[l]), wa_k_g=f(wa_k_g[l]), wa_sink=f(wa_sink[l]),
                   na_tabs=na_bias_tables(f(na_rpb[l])))
        lp = dict(rw_conv=f(rw_conv[l]), rw_w0=f(rw_w0[l]), rw_w_up=f(rw_w_up[l]), rw_a0=f(rw_a0[l]), rw_a_up=f(rw_a_up[l]),
                  rw_g_up=f(rw_g_up[l]), rw_k_k=f(rw_k_k[l]), rw_k_a=f(rw_k_a[l]), rw_r_k=f(rw_r_k[l]), rw_gn_w=f(rw_gn_w[l]),
                  rw_gn_b=f(rw_gn_b[l]), s5_a_re=f(s5_a_re[l]), s5_a_im=f(s5_a_im[l]), s5_log_dt=f(s5_log_dt[l]),
                  s5_b_re=f(s5_b_re[l]), s5_b_im=f(s5_b_im[l]), s5_c_re=f(s5_c_re[l]), s5_c_im=f(s5_c_im[l]), s5_d=f(s5_d[l]))
        cores = [(cid // 4, cid % 4) for cid in range(8)]
        rA = _run(_prog("A", lambda: build_attn("A")), [attn_inputs("A", plat[b], pctx[b], j, prm) for b, j in cores])
        rC = _run(_prog("C", lambda: build_attn("C")), [attn_inputs("C", plat[b], pctx[b], j, prm) for b, j in cores])
        rB = _run(_prog("rw", build_rwkv), [rwkv_inputs(plat[b], pctx[b], j, lp, cst_rw) for b, j in cores])
        rD = _run(_prog("s5", build_s5), [s5_inputs(plat[b][4000:4512], pctx[b][4000:4512], j, lp) for b, j in cores])
        olat = np.empty((2, TLAT, D), np.float32)
        octx = np.empty((2, TCTX, D), np.float32)
        for cid, (b, j) in enumerate(cores):
            cs = slice(j * 128, (j + 1) * 128)
            oa, oc_, ob, od = rA[cid]["o"], rC[cid]["o"], rB[cid]["o"], rD[cid]["yT"].T
            olat[b][:, 0:512][:, cs] = oa[:TLAT]
            octx[b][:, 0:512][:, cs] = oa[TLAT:]
            olat[b][:, 512:1024][:, cs] = ob[TCTX:]
            octx[b][:, 512:1024][:, cs] = ob[:TCTX]
            olat[b][:, 1024:1536][:, cs] = oc_[:TLAT]
            octx[b][:, 1024:1536][:, cs] = oc_[TLAT:]
            olat[b][:, 1536:2048][:, cs] = od[TCTX:]
            octx[b][:, 1536:2048][:, cs] = od[:TCTX]
        ob_ = _tok_to_blocks(np.concatenate([olat[0], olat[1], octx[0], octx[1]], axis=0))
        del pT, plat, pctx, rA, rB, rC, rD, olat, octx
        wg, wu, wd, wo = f(ffn2_wg[l]), f(ffn2_wu[l]), f(ffn2_wd[l]), f(w_out[l])
        gw = f(s5_glu_w[l])
        gb = np.ascontiguousarray(f(s5_glu_b[l]).reshape(4, 128).T)
        gv2 = np.ascontiguousarray(gl[2])
        ims = [dict(xT=np.ascontiguousarray(xb[i * NB:(i + 1) * NB]), oT=np.ascontiguousarray(ob_[i * NB:(i + 1) * NB]),
                    mv=np.ascontiguousarray(mv2[i * NB:(i + 1) * NB]), gv=gv2, gw=gw, gb=gb, wo=wo, wg=wg, wu=wu, wd=wd) for i in range(TC)]
        res = _run(_prog("t2", lambda: build_t2(NB)), ims)
        xb = np.concatenate([r["xo"] for r in res], axis=0)
        del wg, wu, wd, wo, ims, res, ob_
    out = _blocks_to_tok(xb)[:2 * TLAT].reshape(2, TLAT, D)
    return np.ascontiguousarray(out.astype(np.float32))
```

```python
import numpy as np
from contextlib import ExitStack
import concourse.bass as bass
import concourse.mybir as mybir
from concourse.bass_utils import run_bass_kernel_spmd

F32 = mybir.dt.float32
ALU = mybir.AluOpType
AF = mybir.ActivationFunctionType
AX = mybir.AxisListType

D = 2048
DFF = 5632
DIN = 4512
NMOD = 9
EPS = 1e-6


class _Op:
    __slots__ = ("eng", "fn", "deps", "inc", "seq", "dsem", "dval", "dinc", "flushed", "last")

    def __init__(self, eng, fn):
        self.eng, self.fn, self.deps, self.inc, self.seq = eng, fn, [], False, 0
        self.dsem, self.dval, self.dinc = None, 0, 16
        self.flushed, self.last = False, None


class Prog:
    ENG = ("pe", "act", "dve", "pool", "sp")

    def __init__(self, fused=False):
        self.nc = bass.Bass("TRN2", target_bir_lowering=False)
        self.es = ExitStack()
        self.esp = ExitStack()
        self.ops = {e: [] for e in self.ENG}
        self.last_w = {}
        self.readers = {}
        self.dma_sems = {}
        self.dma_cnt = {}
        self.n = 0
        self.out_dmas = []
        self.fused = fused
        self.bind = {}
        self.prefix = ""
        self.ext_inputs = {}
        self.fence_ops = []
        self.fenced = set(self.ENG)
        self.esem = None
        self.seqc = {e: 0 for e in self.ENG}
        self.seen = {e: {} for e in self.ENG}
        self.touched = set()

    def din(self, name, shape, dtype=F32):
        if name in self.bind:
            return self.bind[name]
        full = self.prefix + name
        self.ext_inputs[full] = tuple(shape)
        return self.nc.dram_tensor(full, list(shape), dtype, kind="ExternalInput").ap()

    def dout(self, name, shape):
        if name in self.bind:
            return self.bind[name]
        return self.nc.dram_tensor(self.prefix + name, list(shape), F32, kind="ExternalOutput").ap()

    def scratch(self, name, shape):
        return self.nc.dram_tensor(name, list(shape), F32)

    def sb(self, shape, dtype=F32, persist=False):
        self.n += 1
        return (self.esp if persist else self.es).enter_context(self.nc.sbuf_tensor("sb%d" % self.n, list(shape), dtype))

    def ps(self, shape, dtype=F32):
        self.n += 1
        return self.es.enter_context(self.nc.psum_tensor("ps%d" % self.n, list(shape), dtype))

    def op(self, eng, fn, r=(), w=()):
        o = _Op(eng, fn)
        deps = []
        if eng not in self.fenced:
            deps.extend(self.fence_ops)
            self.fenced.add(eng)
        for k in r:
            lw = self.last_w.get(k)
            if lw is not None:
                deps.append(lw)
            if isinstance(k, tuple) and k[0] == "bank":
                for rd in self.readers.get(k, {}).values():
                    if rd.eng != eng:
                        deps.append(rd)
        for k in w:
            lw = self.last_w.get(k)
            if lw is not None:
                deps.append(lw)
            for rd in self.readers.get(k, {}).values():
                deps.append(rd)
        for d in deps:
            if d is o:
                continue
            if d.flushed and d.dsem is None and not d.inc:
                d = d.last
            if d.eng == eng and eng != "sp" and d.dsem is None and not d.flushed:
                if not any(self.last_w.get(k) is d for k in r):
                    continue
            o.deps.append(d)
            if d.dsem is None:
                d.inc = True
        for k in r:
            self.readers.setdefault(k, {})[eng] = o
        for k in w:
            self.last_w[k] = o
            self.readers[k] = {}
        self.ops[eng].append(o)
        return o

    def _async(self, o, semkey, inc):
        if semkey not in self.dma_sems:
            self.dma_sems[semkey] = self.esp.enter_context(self.nc.semaphore("dq%d" % len(self.dma_sems)))
            self.dma_cnt[semkey] = 0
        self.dma_cnt[semkey] += inc
        o.dsem, o.dval, o.dinc = self.dma_sems[semkey], self.dma_cnt[semkey], inc
        self.touched.add(semkey)
        return o

    def dma(self, out, in_, semkey, r=(), w=(), is_out=False, eng="sp"):
        o = self.op(eng, lambda e: e.dma_start(out=out, in_=in_), r=r, w=w)
        self._async(o, semkey, 16)
        if is_out:
            self.out_dmas.append(o)
        return o

    def coll(self, kind, src, dst, groups, semkey, r=(), w=()):
        o = self.op("pool", lambda e: e.collective_compute(kind, ALU.bypass, replica_groups=groups, ins=[src.opt()], outs=[dst.opt()]), r=r, w=w)
        return self._async(o, semkey, 1)

    def flush(self, final=False):
        nc = self.nc
        if self.esem is None:
            self.esem = {e: self.esp.enter_context(nc.semaphore("e_" + e)) for e in self.ENG}
        esem = self.esem
        lasts = {}
        for e in self.ENG:
            comp = [o for o in self.ops[e] if o.dsem is None]
            if comp:
                comp[-1].inc = True
                lasts[e] = comp[-1]
            c = self.seqc[e]
            for o in self.ops[e]:
                if o.dsem is None and o.inc:
                    c += 1
                    o.seq = c
            self.seqc[e] = c
        ops, out_dmas = self.ops, self.out_dmas

        def emit(ename, e):
            seen = self.seen[ename]
            for o in ops[ename]:
                need = {}
                for d in o.deps:
                    if d.dsem is not None:
                        s, v = d.dsem, d.dval
                    else:
                        s, v = esem[d.eng], d.seq
                    key = id(s)
                    if need.get(key, (None, 0))[1] < v:
                        need[key] = (s, v)
                for key, (s, v) in need.items():
                    if seen.get(key, 0) < v:
                        e.wait_ge(s, v)
                        seen[key] = v
                ins = o.fn(e)
                if o.dsem is not None:
                    if o.dinc == 16:
                        ins.then_inc(o.dsem, 16)
                    else:
                        ins.then_inc(o.dsem)
                elif o.inc:
                    ins.then_inc(esem[ename], 1)
            if ename == "sp" and final:
                fin = {}
                for o in out_dmas:
                    if fin.get(id(o.dsem), (None, 0))[1] < o.dval:
                        fin[id(o.dsem)] = (o.dsem, o.dval)
                for s, v in fin.values():
                    e.wait_ge(s, v)

        with nc.Block() as block:
            if ops["pe"]:
                @block.tensor
                def _(e):
                    emit("pe", e)
            if ops["act"]:
                @block.scalar
                def _(e):
                    emit("act", e)
            if ops["dve"]:
                @block.vector
                def _(e):
                    emit("dve", e)
            if ops["pool"]:
                @block.gpsimd
                def _(e):
                    emit("pool", e)
            if ops["sp"] or final:
                @block.sync
                def _(e):
                    emit("sp", e)
        fence = list(lasts.values())
        for k in self.touched:
            f = _Op("sp", None)
            f.dsem, f.dval, f.flushed = self.dma_sems[k], self.dma_cnt[k], True
            fence.append(f)
        self.touched = set()
        for e in self.ENG:
            for o in self.ops[e]:
                o.flushed = True
                o.last = lasts.get(e, o)
                o.fn = None
            self.ops[e] = []
        self.fence_ops = fence
        self.fenced = set()
        self.es.close()
        self.es = ExitStack()

    def finish(self):
        if self.fused:
            self.flush()
            return self
        self.flush(final=True)
        self.esp.close()
        return self.nc

    def close(self):
        self.flush(final=True)
        self.esp.close()
        return self.nc


NT = 512
KC = D // 128
F32R = mybir.dt.float32r
FAST = True


WDT_ = F32


def RR(ap):
    return ap.bitcast(F32R) if FAST else ap


class TL:
    def __init__(self, P):
        self.P = P
        self.x = P.sb([128, KC, NT])
        self.h = P.sb([128, KC, NT])
        self.a = P.sb([128, 22, NT])
        WDT = F32R if FAST else F32
        self.wa = [P.sb([128, KC, 128], WDT) for _ in range(2)]
        self.wb = [P.sb([128, KC, 128], WDT) for _ in range(2)]
        self.wd = [P.sb([128, 22, 128], WDT) for _ in range(2)]
        self.sq = [P.sb([128, NT]) for _ in range(2)]
        self.rstd = P.sb([128, NT])
        self.tmp = [P.sb([128, NT]) for _ in range(2)]
        self.ones = P.sb([128, 128])
        self.gs = P.sb([128, KC])
        self.ps_ss = P.ps([128, NT])
        self.ps_g = [P.ps([128, NT]) for _ in range(2)]
        self.ps_u = [P.ps([128, NT]) for _ in range(2)]
        self.ps_o = [P.ps([128, NT]) for _ in range(2)]
        self.cnt = 0
        P.op("pool", lambda e: e.memset(self.ones[:], 1.0), w=["ones"])

    def xkeys(self):
        return [("x", k) for k in range(KC)]

    def hkeys(self):
        return [("h", k) for k in range(KC)]

    def rms_modulate(self, g_ap, scale_ap, shift_ap, vkey):
        P = self.P
        x, h = self.x, self.h
        for k in range(KC):
            s = self.sq[k % 2]
            P.op("act", lambda e, s=s, k=k: e.activation(out=s[:], in_=x[:, k, :], func=AF.Square),
                 r=[("x", k)], w=[("sq", k % 2)])
            P.op("pe", lambda e, s=s, k=k: e.matmul(self.ps_ss[:], lhsT=self.ones[:], rhs=s[:],
                                                     start=(k == 0), stop=(k == KC - 1)),
                 r=["ones", ("sq", k % 2)], w=["ps_ss"])
        t0 = self.tmp[0]
        P.op("act", lambda e: e.activation(out=t0[:], in_=self.ps_ss[:], func=AF.Sqrt, scale=1.0 / D, bias=EPS),
             r=["ps_ss"], w=[("tmp", 0)])
        P.op("dve", lambda e: e.reciprocal(out=self.rstd[:], in_=t0[:]), r=[("tmp", 0)], w=["rstd"])
        P.op("dve", lambda e: e.scalar_tensor_tensor(out=self.gs[:], in0=scale_ap, scalar=1.0, in1=g_ap,
                                                      op0=ALU.add, op1=ALU.mult), r=[vkey], w=["gs"])
        for k in range(KC):
            t = self.tmp[k % 2]
            P.op("dve", lambda e, t=t, k=k: e.tensor_tensor(out=t[:], in0=x[:, k, :], in1=self.rstd[:], op=ALU.mult),
                 r=[("x", k), "rstd"], w=[("tmp", k % 2)])
            P.op("dve", lambda e, t=t, k=k: e.tensor_scalar(out=RR(h[:, k, :]), in0=t[:], scalar1=self.gs[:, k:k + 1],
                                                             scalar2=shift_ap[:, k:k + 1], op0=ALU.mult, op1=ALU.add),
                 r=[("tmp", k % 2), "gs", vkey], w=[("h", k)])

    def load_w(self, buf, key, src):
        m = src.shape[1]
        self.P.dma(buf[:, :, 0:m], src.rearrange("(k p) m -> p k m", p=128), semkey=key, w=[key], eng=("pool" if FAST else "sp"))

    def round(self, ap, key):
        self.rcnt = getattr(self, "rcnt", 0) + 1
        if self.rcnt % 2:
            self.P.op("pool", lambda e: e.tensor_copy(out=RR(ap), in_=ap), r=[key], w=[key])
        else:
            self.P.op("act", lambda e: e.activation(out=RR(ap), in_=ap, func=AF.Copy), r=[key], w=[key])

    def ffn(self, wg, wu, wd, gate_ap, vkey):
        P = self.P
        x, h, a = self.x, self.h, self.a
        for half in range(2):
            for jj in range(22):
                j = half * 22 + jj
                c = self.cnt
                self.cnt += 1
                wa, wb = self.wa[c % 2], self.wb[c % 2]
                self.load_w(wa, ("wa", c % 2), wg[:, j * 128:(j + 1) * 128])
                self.load_w(wb, ("wb", c % 2), wu[:, j * 128:(j + 1) * 128])
                pg, pu = self.ps_g[c % 2], self.ps_u[c % 2]
                for k in range(KC):
                    P.op("pe", lambda e, k=k, wa=wa, pg=pg: e.matmul(pg[:], lhsT=RR(wa[:, k, :]), rhs=RR(h[:, k, :]),
                                                                   start=(k == 0), stop=(k == KC - 1)),
                         r=[("wa", c % 2), ("h", k)], w=[("ps_g", c % 2)])
                for k in range(KC):
                    P.op("pe", lambda e, k=k, wb=wb, pu=pu: e.matmul(pu[:], lhsT=RR(wb[:, k, :]), rhs=RR(h[:, k, :]),
                                                                   start=(k == 0), stop=(k == KC - 1)),
                         r=[("wb", c % 2), ("h", k)], w=[("ps_u", c % 2)])
                s = self.sq[c % 2]
                P.op("act", lambda e, s=s, pg=pg: e.activation(out=s[:], in_=pg[:], func=AF.Silu),
                     r=[("ps_g", c % 2)], w=[("sq", c % 2)])
                P.op("dve", lambda e, s=s, pu=pu, jj=jj: e.tensor_tensor(out=RR(a[:, jj, :]), in0=s[:], in1=pu[:], op=ALU.mult),
                     r=[("sq", c % 2), ("ps_u", c % 2)], w=[("a", jj)])
            for m in range(KC):
                c = self.cnt
                self.cnt += 1
                wdb = self.wd[c % 2]
                P.dma(wdb[:, :, :], wd[half * 2816:(half + 1) * 2816, m * 128:(m + 1) * 128].rearrange("(k p) m -> p k m", p=128),
                      semkey=("wd", c % 2), w=[("wd", c % 2)], eng=("pool" if FAST else "sp"))
                po = self.ps_o[c % 2]
                for jj in range(22):
                    P.op("pe", lambda e, jj=jj, wdb=wdb, po=po: e.matmul(po[:], lhsT=RR(wdb[:, jj, :]), rhs=RR(a[:, jj, :]),
                                                                       start=(jj == 0), stop=(jj == 21)),
                         r=[("wd", c % 2), ("a", jj)], w=[("ps_o", c % 2)])
                P.op("dve", lambda e, m=m, po=po: e.scalar_tensor_tensor(out=x[:, m, :], in0=po[:], scalar=gate_ap[:, m:m + 1],
                                                                       in1=x[:, m, :], op0=ALU.mult, op1=ALU.add),
                     r=[("ps_o", c % 2), vkey, ("x", m)], w=[("x", m)])

    def proj(self, w, ncols, out_fn):
        P = self.P
        h = self.h
        nch = (ncols + 127) // 128
        for m in range(nch):
            mc = min(128, ncols - m * 128)
            c = self.cnt
            self.cnt += 1
            wa = self.wa[c % 2]
            self.load_w(wa, ("wa", c % 2), w[:, m * 128:m * 128 + mc])
            po = self.ps_o[c % 2]
            for k in range(KC):
                P.op("pe", lambda e, k=k, wa=wa, po=po, mc=mc: e.matmul(po[0:mc, :], lhsT=(RR(wa[:, k, 0:mc]) if mc == 128 else wa[:, k, 0:mc].bitcast(F32)), rhs=(RR(h[:, k, :]) if mc == 128 else h[:, k, :]),
                                                                      start=(k == 0), stop=(k == KC - 1)),
                     r=[("wa", c % 2), ("h", k)], w=[("ps_o", c % 2)])
            out_fn(m, mc, po, ("ps_o", c % 2))


def build_t1(NB):
    P = Prog()
    xT = P.din("xT", [NB, 128, KC, NT])
    mv = P.din("mv", [NB, 128, 5, KC])
    gv = P.din("gv", [128, 2, KC])
    wg = P.din("wg", [D, DFF], WDT_)
    wu = P.din("wu", [D, DFF], WDT_)
    wd = P.din("wd", [DFF, D], WDT_)
    win = P.din("win", [D, DIN], WDT_)
    xo = P.dout("xo", [NB, 128, KC, NT])
    po_ = P.dout("pT", [NB, DIN, NT])
    T = TL(P)
    mvt = P.sb([128, 5, KC])
    gvt = P.sb([128, 2, KC])
    hg = P.sb([128, KC])
    ot = [P.sb([128, NT]) for _ in range(2)]
    P.dma(gvt[:], gv, semkey="gv", w=["gv"])
    for b in range(NB):
        P.dma(T.x[:], xT[b], semkey="x", w=T.xkeys())
        P.dma(mvt[:], mv[b], semkey="mv", w=["mv"])
        P.op("pool", lambda e: e.tensor_scalar(out=hg[:], in0=mvt[:, 2, :], scalar1=0.5, scalar2=None, op0=ALU.mult),
             r=["mv"], w=["hg"])
        T.rms_modulate(gvt[:, 0, :], mvt[:, 1, :], mvt[:, 0, :], "mv")
        T.ffn(wg, wu, wd, hg, "hg")
        P.dma(xo[b], T.x[:], semkey="xo", r=T.xkeys(), is_out=True)
        T.rms_modulate(gvt[:, 1, :], mvt[:, 4, :], mvt[:, 3, :], "mv")

        def out_fn(m, mc, ps, pskey, b=b):
            o = ot[m % 2]
            P.op("act", lambda e: e.activation(out=o[0:mc, :], in_=ps[0:mc, :], func=AF.Copy),
                 r=[pskey], w=[("ot", m % 2)])
            P.dma(po_[b, m * 128:m * 128 + mc, :], o[0:mc, :], semkey=("ot", m % 2), r=[("ot", m % 2)], is_out=True)
        T.proj(win, DIN, out_fn)
    return P.finish()


def build_t2(NB):
    P = Prog()
    xT = P.din("xT", [NB, 128, KC, NT])
    oT = P.din("oT", [NB, 128, KC, NT])
    mv = P.din("mv", [NB, 128, 4, KC])
    gv = P.din("gv", [128, KC])
    gw = P.din("gw", [512, 512])
    gb = P.din("gb", [128, 4])
    wo = P.din("wo", [D, D], WDT_)
    wg = P.din("wg", [D, DFF], WDT_)
    wu = P.din("wu", [D, DFF], WDT_)
    wd = P.din("wd", [DFF, D], WDT_)
    xo = P.dout("xo", [NB, 128, KC, NT])
    T = TL(P)
    mvt = P.sb([128, 4, KC])
    gvt = P.sb([128, KC])
    gwt = P.sb([128, 4, 512])
    gbt = P.sb([128, 4])
    hg = P.sb([128, KC])
    glu = P.sb([128, 4, NT])
    P.dma(gvt[:], gv, semkey="gv", w=["gv"])
    P.dma(gwt[:], gw.rearrange("(k p) m -> p k m", p=128), semkey="gw", w=["gw"])
    P.dma(gbt[:], gb, semkey="gb", w=["gb"])
    for b in range(NB):
        P.dma(T.x[:], xT[b], semkey="x", w=T.xkeys())
        P.dma(RR(T.h[:]), oT[b], semkey="h", w=T.hkeys(), eng=("pool" if FAST else "sp"))
        P.dma(mvt[:], mv[b], semkey="mv", w=["mv"])
        P.op("pool", lambda e: e.tensor_scalar(out=hg[:], in0=mvt[:, 3, :], scalar1=0.5, scalar2=None, op0=ALU.mult),
             r=["mv"], w=["hg"])
        for m in range(4):
            po = T.ps_g[m % 2]
            for k in range(4):
                P.op("pe", lambda e, m=m, k=k, po=po: e.matmul(po[:], lhsT=gwt[:, k, m * 128:(m + 1) * 128], rhs=T.h[:, 12 + k, :],
                                                             start=(k == 0), stop=(k == 3)),
                     r=["gw", ("h", 12 + k)], w=[("ps_g", m % 2)])
            P.op("act", lambda e, m=m, po=po: e.activation(out=glu[:, m, :], in_=po[:], func=AF.Sigmoid, bias=gbt[:, m:m + 1]),
                 r=[("ps_g", m % 2), "gb"], w=[("glu", m)])
        for m in range(4):
            P.op("dve", lambda e, m=m: e.tensor_tensor(out=RR(T.h[:, 12 + m, :]), in0=T.h[:, 12 + m, :], in1=glu[:, m, :], op=ALU.mult),
                 r=[("h", 12 + m), ("glu", m)], w=[("h", 12 + m)])

        def out_fn(m, mc, ps, pskey):
            P.op("dve", lambda e: e.scalar_tensor_tensor(out=T.x[:, m, :], in0=ps[:], scalar=mvt[:, 0, m:m + 1], in1=T.x[:, m, :],
                                                          op0=ALU.mult, op1=ALU.add),
                 r=[pskey, "mv", ("x", m)], w=[("x", m)])
        T.proj(wo, D, out_fn)
        T.rms_modulate(gvt[:, :], mvt[:, 2, :], mvt[:, 1, :], "mv")
        T.ffn(wg, wu, wd, hg, "hg")
        P.dma(xo[b], T.x[:], semkey="xo", r=T.xkeys(), is_out=True)
    return P.finish()


ADA_N = NMOD * D // 8


def build_ada(L):
    P = Prog()
    cT = P.din("cT", [128, KC, 3])
    w = P.din("w", [L, D, ADA_N])
    bb = P.din("b", [L, 3, ADA_N])
    out = P.dout("mod", [L, 3, ADA_N])
    ct = P.sb([128, KC, 3])
    sc = P.sb([128, KC, 3])
    wt = [P.sb([128, KC, 512]) for _ in range(2)]
    bt = P.sb([3, L, ADA_N])
    ot = P.sb([3, L, ADA_N])
    ps = [P.ps([3, 512]) for _ in range(2)]
    P.dma(ct[:], cT, semkey="c", w=["c"])
    P.dma(bt[:], bb.rearrange("l r n -> r l n"), semkey="b", w=["b"])
    P.op("act", lambda e: e.activation(out=sc[:], in_=ct[:], func=AF.Silu), r=["c"], w=["sc"])
    c = 0
    for l in range(L):
        for n0 in range(0, ADA_N, 512):
            nn = min(512, ADA_N - n0)
            wb, pb = wt[c % 2], ps[c % 2]
            P.dma(wb[:, :, 0:nn], w[l, :, n0:n0 + nn].rearrange("(k p) n -> p k n", p=128), semkey=("w", c % 2), w=[("w", c % 2)])
            for k in range(KC):
                P.op("pe", lambda e, k=k, wb=wb, pb=pb, nn=nn: e.matmul(pb[:, 0:nn], lhsT=sc[:, k, :], rhs=wb[:, k, 0:nn],
                                                                      start=(k == 0), stop=(k == KC - 1)),
                     r=["sc", ("w", c % 2)], w=[("ps", c % 2)])
            P.op("dve", lambda e, pb=pb, l=l, n0=n0, nn=nn: e.tensor_tensor(out=ot[:, l, n0:n0 + nn], in0=pb[:, 0:nn],
                                                                           in1=bt[:, l, n0:n0 + nn], op=ALU.add),
                 r=[("ps", c % 2), "b"], w=["ot"])
            c += 1
    P.dma(out.rearrange("l r n -> r l n"), ot[:], semkey="o", r=["ot"], is_out=True)
    return P.finish()


TLAT = 8192
TCTX = 256
TQ = TLAT + TCTX
NQT = TQ // 128
NEG = -1e30


def na_specs():
    types = {0: 0, 1: 1, 62: 3, 63: 4}
    specs = []
    for i in range(64):
        r0 = 2 * i
        lo = min(max(r0 - 4, 0), 120)
        hi = min(max(r0 + 1 - 4, 0), 120) + 7
        nch = (hi - lo + 2) // 2
        ty = types.get(i, 2)
        specs.append(([lo // 2 + c for c in range(nch)], ty))
    return specs


NA_REP = [0, 1, 2, 62, 63]
NA_NCH = [4, 4, 5, 4, 4]


def na_bias_tables(rpb):
    tabs = []
    q = np.arange(128)
    qr, qc = q // 64, q % 64
    for ty, i in enumerate(NA_REP):
        r0 = 2 * i
        lo = min(max(r0 - 4, 0), 120)
        r = r0 + qr
        rs = np.clip(r - 4, 0, 120)
        cs = np.clip(qc - 8, 0, 48)
        for c in range(NA_NCH[ty]):
            kl = np.arange(128)
            kr = lo + 2 * c + kl // 64
            kc = kl % 64
            valid = ((kr[:, None] >= rs[None, :]) & (kr[:, None] <= rs[None, :] + 7) &
                     (kc[:, None] >= cs[None, :]) & (kc[:, None] <= cs[None, :] + 15))
            bidx = (kr[:, None] - r[None, :] + 7) * 31 + (kc[:, None] - qc[None, :] + 15)
            bidx = np.where(valid, bidx, 0)
            t = np.where(valid[None], rpb[:, bidx], np.float32(NEG)).astype(np.float32)
            tabs.append(t)
    return np.stack(tabs, axis=2)


def wa_bias_table():
    j = np.arange(128)[:, None]
    q = np.arange(128)[None, :]
    prev = np.where(j >= q, 0.0, NEG)
    cur = np.zeros((128, 128))
    nxt = np.where(j <= q, 0.0, NEG)
    return np.stack([prev, cur, nxt], axis=1).astype(np.float32)


def rope_tables():
    t = np.arange(TLAT)
    inv = (1.0 / (np.float32(10000.0) ** (np.arange(16, dtype=np.float32) / np.float32(16)))).astype(np.float32)

    def ang(p):
        a = p.astype(np.float32)[:, None] * inv[None, :]
        return np.concatenate([a, a], -1)
    a = np.concatenate([ang(t // 64), ang(t % 64)], -1)
    return np.cos(a).astype(np.float32), np.sin(a).astype(np.float32)


def rot_matrix():
    m = np.zeros((128, 128), np.float32)
    for o in range(128):
        if o % 32 < 16:
            m[o + 16, o] = -1.0
        else:
            m[o - 16, o] = 1.0
    return m


def build_attn(kind):
    rope = kind == "C"
    P = Prog()
    NS = 2 * sum(NA_NCH) if kind == "A" else 3
    qT = P.din("qT", [128, TQ])
    kT = P.din("kT", [128, TQ])
    vtm = P.din("vtm", [TQ, 128])
    gq = P.din("gq", [128, 1])
    gk = P.din("gk", [128, 1])
    btab = P.din("btab", [128, NS, 128])
    if rope:
        cosT = P.din("cosT", [128, TLAT])
        sinT = P.din("sinT", [128, TLAT])
        prot = P.din("prot", [128, 128])
        sink = P.din("sink", [128, 2])
    o_tm = P.dout("o", [TQ, 128])

    qn = P.sb([128, TQ])
    kn = P.sb([128, TQ])
    V1 = P.sb([128, NQT, 2, 65])
    BT = P.sb([128, NS, 128])
    gqt = P.sb([128, 1])
    gkt = P.sb([128, 1])
    bones = P.sb([128, 128])
    raw = [P.sb([128, 512]) for _ in range(2)]
    sq = P.sb([128, 512])
    st = P.sb([128, 512])
    rs = P.sb([128, 512])
    tmpS = [P.sb([128, 640]) for _ in range(2)]
    E = [P.sb([128, 896]) for _ in range(2)]
    ot = [P.sb([128, 128]) for _ in range(2)]
    den = [P.sb([128, 2]) for _ in range(2)]
    banks = [P.ps([128, 512]) for _ in range(8)]

    P.dma(gqt[:], gq, semkey="gq", w=["gq"])
    P.dma(gkt[:], gk, semkey="gk", w=["gk"])
    P.dma(BT[:], btab, semkey="bt", w=["BT"])
    for h in range(2):
        P.dma(V1[:, :, h, 0:64], vtm[:, h * 64:(h + 1) * 64].rearrange("(t p) d -> p t d", p=128), semkey=("v", h), w=["V1"])
    P.op("pool", lambda e: e.memset(V1[:, :, :, 64:65], 1.0), w=["V1"])
    P.op("pool", lambda e: e.memset(bones[:], 0.0), w=["bones"])
    P.op("pool", lambda e: e.memset(bones[0:64, 0:64], 1.0), w=["bones"])
    P.op("pool", lambda e: e.memset(bones[64:128, 64:128], 1.0), w=["bones"])
    if rope:
        prt = P.sb([128, 128])
        cs = [P.sb([128, 512]) for _ in range(2)]
        sn = [P.sb([128, 512]) for _ in range(2)]
        t1 = P.sb([128, 512])
        t2 = P.sb([128, 512])
        skt = P.sb([128, 2])
        ske = P.sb([128, 2])
        P.dma(prt[:], prot, semkey="prot", w=["prot"])
        P.dma(skt[:], sink, semkey="sink", w=["sink"])
        P.op("act", lambda e: e.activation(out=ske[:], in_=skt[:], func=AF.Exp), r=["sink"], w=["ske"])

    c = 0
    for src, dst, g, nm in ((qT, qn, gqt, "qn"), (kT, kn, gkt, "kn")):
        for t0 in range(0, TQ, 512):
            nt = min(512, TQ - t0)
            rw = raw[c % 2]
            P.dma(rw[:, 0:nt], src[:, t0:t0 + nt], semkey=("raw", c % 2), w=[("raw", c % 2)])
            P.op("pool", lambda e, rw=rw, nt=nt: e.tensor_tensor(out=sq[:, 0:nt], in0=rw[:, 0:nt], in1=rw[:, 0:nt], op=ALU.mult),
                 r=[("raw", c % 2)], w=["sq"])
            P.op("pe", lambda e, nt=nt: e.matmul(banks[0][:, 0:nt], lhsT=bones[:], rhs=sq[:, 0:nt], start=True, stop=True),
                 r=["bones", "sq"], w=[("bank", 0)])
            P.op("act", lambda e, nt=nt: e.activation(out=st[:, 0:nt], in_=banks[0][:, 0:nt], func=AF.Sqrt, scale=1.0 / 64, bias=EPS),
                 r=[("bank", 0)], w=["st"])
            P.op("dve", lambda e, nt=nt: e.reciprocal(out=rs[:, 0:nt], in_=st[:, 0:nt]), r=["st"], w=["rs"])
            P.op("dve", lambda e, rw=rw, nt=nt, t0=t0, dst=dst, g=g: e.scalar_tensor_tensor(
                out=dst[:, t0:t0 + nt], in0=rw[:, 0:nt], scalar=g[:, 0:1], in1=rs[:, 0:nt], op0=ALU.mult, op1=ALU.mult),
                r=[("raw", c % 2), "rs", "gq", "gk"], w=[(nm, t0)])
            if rope and t0 < TLAT:
                P.dma(cs[c % 2][:], cosT[:, t0:t0 + 512], semkey=("cs", c % 2), w=[("cs", c % 2)])
                P.dma(sn[c % 2][:], sinT[:, t0:t0 + 512], semkey=("sn", c % 2), w=[("sn", c % 2)])
                P.op("pe", lambda e, t0=t0, dst=dst: e.matmul(banks[1][:], lhsT=prt[:], rhs=dst[:, t0:t0 + 512], start=True, stop=True),
                     r=["prot", (nm, t0)], w=[("bank", 1)])
                P.op("dve", lambda e, t0=t0, dst=dst, cc=cs[c % 2]: e.tensor_tensor(out=t1[:], in0=dst[:, t0:t0 + 512], in1=cc[:], op=ALU.mult),
                     r=[(nm, t0), ("cs", c % 2)], w=["t1"])
                P.op("dve", lambda e, ss=sn[c % 2]: e.tensor_tensor(out=t2[:], in0=banks[1][:], in1=ss[:], op=ALU.mult),
                     r=[("bank", 1), ("sn", c % 2)], w=["t2"])
                P.op("pool", lambda e, t0=t0, dst=dst: e.tensor_tensor(out=dst[:, t0:t0 + 512], in0=t1[:], in1=t2[:], op=ALU.add),
                     r=["t1", "t2"], w=[(nm, t0)])
            c += 1

    def nkeys(nm, tile):
        return [(nm, (tile * 128) // 512 * 512)]

    if kind == "A":
        specs = na_specs()
        base = np.concatenate([[0], np.cumsum(NA_NCH)])[:5]
        ntab = sum(NA_NCH)
    it = 0
    for i in range(NQT):
        if i < 64:
            if kind == "A":
                ktiles, ty = specs[i]
            else:
                ktiles = [k for k in (i - 1, i, i + 1) if 0 <= k < 64]
        else:
            ktiles = []
        nW = len(ktiles)
        par = i % 2
        po = banks[6 + par]
        for h in range(2):
            hs = slice(h * 64, (h + 1) * 64)
            ip = it % 2
            it += 1
            bA, bB = banks[2 + 2 * ip], banks[3 + 2 * ip]
            if kind == "A":
                s0 = (h * ntab + int(base[ty])) if nW else 0
            else:
                s0 = 1 if i == 0 else 0

            def sl(s):
                return (bA, s * 128) if s < 4 else (bB, (s - 4) * 128)
            chunks = [(kt, s) for s, kt in enumerate(ktiles)] + [(64, 5), (65, 6)]
            for kt, s in chunks:
                bk, off = sl(s)
                P.op("pe", lambda e, bk=bk, off=off, kt=kt, hs=hs, i=i: e.matmul(
                    bk[:, off:off + 128], lhsT=kn[hs, kt * 128:(kt + 1) * 128], rhs=qn[hs, i * 128:(i + 1) * 128], start=True, stop=True),
                    r=nkeys("kn", kt) + nkeys("qn", i), w=[("bank", 2 + 2 * ip + (0 if s < 4 else 1))])
            nA = min(nW, 4)
            Eb, tS = E[ip], tmpS[ip]
            if nA:
                P.op("dve", lambda e, nA=nA, s0=s0, tS=tS, bA=bA: e.scalar_tensor_tensor(
                    out=tS[:, 0:nA * 128], in0=bA[:, 0:nA * 128], scalar=0.125, in1=BT[:, s0:s0 + nA, :].rearrange("p s q -> p (s q)"),
                    op0=ALU.mult, op1=ALU.add), r=[("bank", 2 + 2 * ip), "BT"], w=[("tS", ip)])
            if nW == 5:
                P.op("dve", lambda e, s0=s0, tS=tS, bB=bB: e.scalar_tensor_tensor(
                    out=tS[:, 512:640], in0=bB[:, 0:128], scalar=0.125, in1=BT[:, s0 + 4, :],
                    op0=ALU.mult, op1=ALU.add), r=[("bank", 3 + 2 * ip), "BT"], w=[("tS", ip)])
            if nW:
                P.op("act", lambda e, nW=nW, Eb=Eb, tS=tS: e.activation(out=Eb[:, 0:nW * 128], in_=tS[:, 0:nW * 128], func=AF.Exp),
                     r=[("tS", ip)], w=[("E", ip)])
            P.op("act", lambda e, Eb=Eb, bB=bB: e.activation(out=Eb[:, 640:896], in_=bB[:, 128:384], func=AF.Exp, scale=0.125),
                 r=[("bank", 3 + 2 * ip)], w=[("E", ip)])
            for n, (kt, s) in enumerate(chunks):
                eo = s * 128
                P.op("pe", lambda e, n=n, kt=kt, eo=eo, Eb=Eb, h=h, po=po: e.matmul(
                    po[:, h * 65:(h + 1) * 65], lhsT=Eb[:, eo:eo + 128], rhs=V1[:, kt, h, :], start=(n == 0), stop=(n == len(chunks) - 1)),
                    r=[("E", ip), "V1"], w=[("bank", 6 + par)])
        dn, o = den[par], ot[par]
        pov = po[:, 0:130].rearrange("p (h c) -> p h c", h=2)
        if rope:
            P.op("dve", lambda e, dn=dn, pov=pov: e.tensor_tensor(out=dn[:, :].rearrange("p (h o) -> p h o", o=1), in0=pov[:, :, 64:65],
                                                                 in1=ske[:, :].rearrange("p (h o) -> p h o", o=1), op=ALU.add),
                 r=[("bank", 6 + par), "ske"], w=[("den", par)])
            P.op("dve", lambda e, dn=dn: e.reciprocal(out=dn[:], in_=dn[:]), r=[("den", par)], w=[("den", par)])
        else:
            P.op("dve", lambda e, dn=dn, pov=pov: e.reciprocal(out=dn[:, :].rearrange("p (h o) -> p h o", o=1), in_=pov[:, :, 64:65]),
                 r=[("bank", 6 + par)], w=[("den", par)])
        for h in range(2):
            P.op("dve", lambda e, h=h, dn=dn, o=o, po=po: e.tensor_scalar(out=o[:, h * 64:(h + 1) * 64], in0=po[:, h * 65:h * 65 + 64],
                                                                       scalar1=dn[:, h:h + 1], scalar2=None, op0=ALU.mult),
                 r=[("bank", 6 + par), ("den", par)], w=[("ot", par)])
        P.dma(o_tm[i * 128:(i + 1) * 128, :], o[:], semkey=("ot", par), r=[("ot", par)], is_out=True)
    return P.finish()


def attn_inputs(kind, plat, pctx, j, prm):
    def fm(cols):
        return np.ascontiguousarray(np.concatenate([plat[cols], pctx[cols]], axis=1))
    a = np.arange(128)
    if kind == "A":
        qc, kc, vc = j * 128 + a, 512 + j * 128 + a, 1024 + j * 128 + a
        tabs = prm["na_tabs"]
        d = dict(btab=np.ascontiguousarray(np.concatenate([tabs[2 * j], tabs[2 * j + 1]], axis=1)))
        gq, gk = prm["na_q_g"], prm["na_k_g"]
    else:
        kvh = j // 2
        a64 = np.tile(np.arange(64), 2)
        qc, kc, vc = 3232 + j * 128 + a, 3744 + kvh * 64 + a64, 3872 + kvh * 64 + a64
        d = dict(btab=prm["wa_tab"], cosT=prm["cosT"], sinT=prm["sinT"], prot=prm["prot"],
                 sink=np.ascontiguousarray(np.broadcast_to(prm["wa_sink"][2 * j:2 * j + 2][None, :], (128, 2))))
        gq, gk = prm["wa_q_g"], prm["wa_k_g"]
    d.update(qT=fm(qc), kT=fm(kc), vtm=np.ascontiguousarray(fm(vc).T),
             gq=np.ascontiguousarray(np.tile(gq, 2)[:, None]), gk=np.ascontiguousarray(np.tile(gk, 2)[:, None]))
    return d


def attn_consts():
    cos, sin = rope_tables()
    return dict(wa_tab=wa_bias_table(), cosT=np.ascontiguousarray(np.tile(cos.T, (2, 1))),
                sinT=np.ascontiguousarray(np.tile(sin.T, (2, 1))), prot=rot_matrix())


I32 = mybir.dt.int32
S5_BLOCKS = [(0, TCTX)] + [(TCTX + 512 * i, 512) for i in range(16)]
TWO_PI = 2.0 * np.pi


def build_s5():
    P = Prog()
    uTd = P.din("uT", [128, TQ])
    bre = P.din("bre", [128, 4, 128])
    bim = P.din("bim", [128, 4, 128])
    cre = P.din("cre", [128, 4, 128])
    cim = P.din("cim", [128, 4, 128])
    pare = P.din("are", [128, 8])
    paim = P.din("aim", [128, 8])
    pldt = P.din("ldt", [128, 8])
    pdsk = P.din("dsk", [128, 1])
    yo = P.dout("yT", [128, TQ])

    uT = P.sb([128, TQ])
    yacc = P.sb([128, TQ])
    re = [P.sb([128, TQ]) for _ in range(2)]
    nim = P.sb([128, TQ])
    Bre, Bim, Cre, Cim = (P.sb([128, 4, 128]) for _ in range(4))
    small = {}

    def sm(name, shape=(128, 8), dt=F32):
        small[name] = P.sb(list(shape), dt)
        return small[name]
    are, aim, ldt, dsk = sm("are"), sm("aim"), sm("ldt"), sm("dsk", (128, 1))
    NL = 14
    pwr, pwi, npwi = sm("pwr", (128, NL, 8)), sm("pwi", (128, NL, 8)), sm("npwi", (128, NL, 8))
    tmp = [P.sb([128, 512]) for _ in range(2)]
    g1 = [P.sb([128, 512]) for _ in range(2)]
    g2 = [P.sb([128, 512]) for _ in range(2)]
    ob = [P.sb([128, 512]) for _ in range(2)]
    banks = [P.ps([128, 512]) for _ in range(6)]

    P.dma(uT[:], uTd, semkey="u", w=["uT"])
    for t, s, k in ((Bre, bre, "Bre"), (Bim, bim, "Bim"), (Cre, cre, "Cre"), (Cim, cim, "Cim"),
                    (are, pare, "are"), (aim, paim, "aim"), (ldt, pldt, "ldt"), (dsk, pdsk, "dsk")):
        P.dma(t[:], s, semkey=k, w=[k])

    def V(name, fn, r, w):
        P.op("dve", fn, r=r, w=w)

    dt_, dre, mag, ang = sm("dt"), sm("dre"), sm("mag"), sm("ang")
    P.op("act", lambda e: e.activation(out=dt_[:], in_=ldt[:], func=AF.Exp), r=["ldt"], w=["dt"])
    V("", lambda e: e.tensor_tensor(out=dre[:], in0=dt_[:], in1=are[:], op=ALU.mult), ["dt", "are"], ["dre"])
    P.op("act", lambda e: e.activation(out=mag[:], in_=dre[:], func=AF.Exp), r=["dre"], w=["mag"])
    V("", lambda e: e.tensor_tensor(out=ang[:], in0=dt_[:], in1=aim[:], op=ALU.mult), ["dt", "aim"], ["ang"])

    def sin_of(dst, shift, nm):
        z, ki, kf, r_, m_ = sm(nm + "z"), sm(nm + "ki", dt=I32), sm(nm + "kf"), sm(nm + "r"), sm(nm + "m")
        V("", lambda e: e.tensor_scalar(out=z[:], in0=ang[:], scalar1=1.0 / TWO_PI, scalar2=shift / TWO_PI + 0.5, op0=ALU.mult, op1=ALU.add),
          ["ang"], [nm + "z"])
        V("", lambda e: e.tensor_copy(out=ki[:], in_=z[:]), [nm + "z"], [nm + "ki"])
        V("", lambda e: e.tensor_copy(out=kf[:], in_=ki[:]), [nm + "ki"], [nm + "kf"])
        V("", lambda e: e.scalar_tensor_tensor(out=r_[:], in0=kf[:], scalar=-TWO_PI, in1=ang[:], op0=ALU.mult, op1=ALU.add),
          [nm + "kf", "ang"], [nm + "r"])
        if shift:
            V("", lambda e: e.tensor_scalar(out=r_[:], in0=r_[:], scalar1=shift, scalar2=None, op0=ALU.add), [nm + "r"], [nm + "r"])
        V("", lambda e: e.tensor_scalar(out=m_[:], in0=r_[:], scalar1=-np.pi, scalar2=TWO_PI, op0=ALU.is_lt, op1=ALU.mult), [nm + "r"], [nm + "m"])
        V("", lambda e: e.tensor_tensor(out=r_[:], in0=r_[:], in1=m_[:], op=ALU.add), [nm + "r", nm + "m"], [nm + "r"])
        V("", lambda e: e.tensor_scalar(out=m_[:], in0=r_[:], scalar1=np.pi, scalar2=-TWO_PI, op0=ALU.is_gt, op1=ALU.mult), [nm + "r"], [nm + "m"])
        V("", lambda e: e.tensor_tensor(out=r_[:], in0=r_[:], in1=m_[:], op=ALU.add), [nm + "r", nm + "m"], [nm + "r"])
        P.op("act", lambda e: e.activation(out=dst[:], in_=r_[:], func=AF.Sin), r=[nm + "r"], w=[nm + "s"])
    sn_, cs_ = sm("sn"), sm("cs")
    sin_of(sn_, 0.0, "sn")
    sin_of(cs_, np.pi / 2, "cs")
    V("", lambda e: e.tensor_tensor(out=pwr[:, 0, :], in0=mag[:], in1=cs_[:], op=ALU.mult), ["mag", "css"], ["pwr"])
    V("", lambda e: e.tensor_tensor(out=pwi[:, 0, :], in0=mag[:], in1=sn_[:], op=ALU.mult), ["mag", "sns"], ["pwi"])
    den, nr, t1, t2, cfr, cfi, ncfr, ncfi = (sm(n) for n in ("den", "nr", "t1", "t2", "cfr", "cfi", "ncfr", "ncfi"))
    V("", lambda e: e.tensor_tensor(out=den[:], in0=are[:], in1=are[:], op=ALU.mult), ["are"], ["den"])
    V("", lambda e: e.tensor_tensor(out=t1[:], in0=aim[:], in1=aim[:], op=ALU.mult), ["aim"], ["t1"])
    V("", lambda e: e.tensor_tensor(out=den[:], in0=den[:], in1=t1[:], op=ALU.add), ["den", "t1"], ["den"])
    V("", lambda e: e.reciprocal(out=den[:], in_=den[:]), ["den"], ["den"])
    V("", lambda e: e.tensor_scalar(out=nr[:], in0=pwr[:, 0, :], scalar1=-1.0, scalar2=None, op0=ALU.add), ["pwr"], ["nr"])
    V("", lambda e: e.tensor_tensor(out=t1[:], in0=nr[:], in1=are[:], op=ALU.mult), ["nr", "are", "den"], ["t1"])
    V("", lambda e: e.tensor_tensor(out=t2[:], in0=pwi[:, 0, :], in1=aim[:], op=ALU.mult), ["pwi", "aim"], ["t2"])
    V("", lambda e: e.tensor_tensor(out=t1[:], in0=t1[:], in1=t2[:], op=ALU.add), ["t1", "t2"], ["t1"])
    V("", lambda e: e.tensor_tensor(out=cfr[:], in0=t1[:], in1=den[:], op=ALU.mult), ["t1", "den"], ["cfr"])
    V("", lambda e: e.tensor_tensor(out=t1[:], in0=pwi[:, 0, :], in1=are[:], op=ALU.mult), ["pwi", "are", "cfr"], ["t1"])
    V("", lambda e: e.tensor_tensor(out=t2[:], in0=nr[:], in1=aim[:], op=ALU.mult), ["nr", "aim", "cfr"], ["t2"])
    V("", lambda e: e.tensor_tensor(out=t1[:], in0=t1[:], in1=t2[:], op=ALU.subtract), ["t1", "t2"], ["t1"])
    V("", lambda e: e.tensor_tensor(out=cfi[:], in0=t1[:], in1=den[:], op=ALU.mult), ["t1", "den"], ["cfi"])
    V("", lambda e: e.tensor_scalar(out=ncfr[:], in0=cfr[:], scalar1=-1.0, scalar2=None, op0=ALU.mult), ["cfr"], ["ncfr"])
    V("", lambda e: e.tensor_scalar(out=ncfi[:], in0=cfi[:], scalar1=-1.0, scalar2=None, op0=ALU.mult), ["cfi"], ["ncfi"])
    for l in range(1, NL):
        V("", lambda e, l=l: e.tensor_tensor(out=t1[:], in0=pwr[:, l - 1, :], in1=pwr[:, l - 1, :], op=ALU.mult), ["pwr", "cfi", "ncfi"], ["t1"])
        V("", lambda e, l=l: e.tensor_tensor(out=t2[:], in0=pwi[:, l - 1, :], in1=pwi[:, l - 1, :], op=ALU.mult), ["pwi", "cfi", "ncfi"], ["t2"])
        V("", lambda e, l=l: e.tensor_tensor(out=pwi[:, l, :], in0=pwr[:, l - 1, :], in1=pwi[:, l - 1, :], op=ALU.mult), ["pwr", "pwi"], ["pwi"])
        V("", lambda e, l=l: e.tensor_tensor(out=pwr[:, l, :], in0=t1[:], in1=t2[:], op=ALU.subtract), ["t1", "t2"], ["pwr"])
        V("", lambda e, l=l: e.tensor_scalar(out=pwi[:, l, :], in0=pwi[:, l, :], scalar1=2.0, scalar2=None, op0=ALU.mult), ["pwi"], ["pwi"])
    V("", lambda e: e.tensor_scalar(out=npwi[:].rearrange("p l c -> p (l c)"), in0=pwi[:].rearrange("p l c -> p (l c)"), scalar1=-1.0, scalar2=None, op0=ALU.mult),
      ["pwi"], ["npwi"])

    V("", lambda e: e.tensor_scalar(out=yacc[:], in0=uT[:], scalar1=dsk[:, 0:1], scalar2=None, op0=ALU.mult), ["uT", "dsk"], ["yacc"])

    cnt = 0
    for d in range(2):
        for c in range(4):
            dc = d * 4 + c
            cur = 0
            for (t0, nt) in S5_BLOCKS:
                if d == 0:
                    o0 = t0
                else:
                    o0 = TLAT if t0 == 0 else t0 - TCTX
                p1, p2 = banks[(cnt % 2) * 2], banks[(cnt % 2) * 2 + 1]
                k1, k2 = ("bank", (cnt % 2) * 2), ("bank", (cnt % 2) * 2 + 1)
                tb = tmp[cnt % 2]
                cnt += 1
                P.op("pe", lambda e, p1=p1, c=c, t0=t0, nt=nt: e.matmul(p1[:, 0:nt], lhsT=Bre[:, c, :], rhs=uT[:, t0:t0 + nt], start=True, stop=True),
                     r=["Bre", "uT"], w=[k1])
                P.op("pe", lambda e, p2=p2, c=c, t0=t0, nt=nt: e.matmul(p2[:, 0:nt], lhsT=Bim[:, c, :], rhs=uT[:, t0:t0 + nt], start=True, stop=True),
                     r=["Bim", "uT"], w=[k2])
                V("", lambda e, tb=tb, p2=p2, nt=nt, dc=dc: e.tensor_scalar(out=tb[:, 0:nt], in0=p2[:, 0:nt], scalar1=cfi[:, dc:dc + 1], scalar2=None, op0=ALU.mult),
                  [k2, "cfi"], [("tmp", id(tb))])
                V("", lambda e, tb=tb, p1=p1, nt=nt, dc=dc, o0=o0: e.scalar_tensor_tensor(
                    out=re[0][:, o0:o0 + nt], in0=p1[:, 0:nt], scalar=cfr[:, dc:dc + 1], in1=tb[:, 0:nt], op0=ALU.mult, op1=ALU.subtract),
                    [k1, "cfr", ("tmp", id(tb))], ["re0"])
                V("", lambda e, tb=tb, p1=p1, nt=nt, dc=dc: e.tensor_scalar(out=tb[:, 0:nt], in0=p1[:, 0:nt], scalar1=ncfi[:, dc:dc + 1], scalar2=None, op0=ALU.mult),
                  [k1, "ncfi"], [("tmp", id(tb))])
                V("", lambda e, tb=tb, p2=p2, nt=nt, dc=dc, o0=o0: e.scalar_tensor_tensor(
                    out=nim[:, o0:o0 + nt], in0=p2[:, 0:nt], scalar=ncfr[:, dc:dc + 1], in1=tb[:, 0:nt], op0=ALU.mult, op1=ALU.add),
                    [k2, "ncfr", ("tmp", id(tb))], ["nim"])
            for l in range(NL):
                s = 1 << l
                n = TQ - s
                ro, rn = re[cur], re[1 - cur]
                kro, krn = "re%d" % cur, "re%d" % (1 - cur)
                ar, ai, nai = pwr[:, l, dc:dc + 1], pwi[:, l, dc:dc + 1], npwi[:, l, dc:dc + 1]
                if d == 0:
                    dst, src, keep = slice(s, TQ), slice(0, n), slice(0, s)
                else:
                    dst, src, keep = slice(0, n), slice(s, TQ), slice(n, TQ)
                V("", lambda e, ro=ro, rn=rn, ar=ar, dst=dst, src=src: e.scalar_tensor_tensor(
                    out=rn[:, dst], in0=ro[:, src], scalar=ar, in1=ro[:, dst], op0=ALU.mult, op1=ALU.add), [kro, "pwr"], [krn])
                V("", lambda e, rn=rn, ai=ai, dst=dst, src=src: e.scalar_tensor_tensor(
                    out=rn[:, dst], in0=nim[:, src], scalar=ai, in1=rn[:, dst], op0=ALU.mult, op1=ALU.add), [krn, "nim", "pwi"], [krn])
                P.op("pool", lambda e, ro=ro, rn=rn, keep=keep: e.tensor_copy(out=rn[:, keep], in_=ro[:, keep]), r=[kro], w=[krn])
                if d == 0:
                    dsr, ssr = slice(TQ - 1, s - 1, -1), (slice(n - 1, None, -1))
                else:
                    dsr, ssr = dst, src
                V("", lambda e, ar=ar, dsr=dsr, ssr=ssr: e.scalar_tensor_tensor(
                    out=nim[:, dsr], in0=nim[:, ssr], scalar=ar, in1=nim[:, dsr], op0=ALU.mult, op1=ALU.add), ["nim", "pwr"], ["nim"])
                V("", lambda e, ro=ro, nai=nai, dst=dst, src=src: e.scalar_tensor_tensor(
                    out=nim[:, dst], in0=ro[:, src], scalar=nai, in1=nim[:, dst], op0=ALU.mult, op1=ALU.add), ["nim", kro, "npwi"], ["nim"])
                cur = 1 - cur
            xr, kxr = re[cur], "re%d" % cur
            for (t0, nt) in S5_BLOCKS:
                if d == 0:
                    o0 = t0
                else:
                    o0 = TLAT if t0 == 0 else t0 - TCTX
                pb, kb = banks[4 + cnt % 2], ("bank", 4 + cnt % 2)
                cnt += 1
                P.op("pe", lambda e, pb=pb, c=c, o0=o0, nt=nt, xr=xr: e.matmul(pb[:, 0:nt], lhsT=Cre[:, c, :], rhs=xr[:, o0:o0 + nt], start=True, stop=False),
                     r=["Cre", kxr], w=[kb])
                P.op("pe", lambda e, pb=pb, c=c, o0=o0, nt=nt: e.matmul(pb[:, 0:nt], lhsT=Cim[:, c, :], rhs=nim[:, o0:o0 + nt], start=False, stop=True),
                     r=["Cim", "nim"], w=[kb])
                V("", lambda e, pb=pb, t0=t0, nt=nt: e.tensor_tensor(out=yacc[:, t0:t0 + nt], in0=pb[:, 0:nt], in1=yacc[:, t0:t0 + nt], op=ALU.add),
                  [kb, "yacc"], ["yacc"])
            if cur != 0:
                re[0], re[1] = re[1], re[0]

    for n_, (t0, nt) in enumerate(S5_BLOCKS):
        a, b_, o = g1[n_ % 2], g2[n_ % 2], ob[n_ % 2]
        ka, kb2, ko = ("g1", n_ % 2), ("g2", n_ % 2), ("ob", n_ % 2)
        ys = yacc[:, t0:t0 + nt]
        P.op("pool", lambda e, a=a, ys=ys, nt=nt: e.tensor_tensor(out=a[:, 0:nt], in0=ys, in1=ys, op=ALU.mult), r=["yacc"], w=[ka])
        P.op("pool", lambda e, a=a, nt=nt: e.tensor_scalar(out=a[:, 0:nt], in0=a[:, 0:nt], scalar1=0.044715, scalar2=1.0, op0=ALU.mult, op1=ALU.add), r=[ka], w=[ka])
        P.op("pool", lambda e, a=a, ys=ys, nt=nt: e.tensor_tensor(out=a[:, 0:nt], in0=a[:, 0:nt], in1=ys, op=ALU.mult), r=[ka, "yacc"], w=[ka])
        P.op("act", lambda e, a=a, b_=b_, nt=nt: e.activation(out=b_[:, 0:nt], in_=a[:, 0:nt], func=AF.Sigmoid, scale=1.5957691216057308), r=[ka], w=[kb2])
        V("", lambda e, b_=b_, o=o, ys=ys, nt=nt: e.tensor_tensor(out=o[:, 0:nt], in0=b_[:, 0:nt], in1=ys, op=ALU.mult), [kb2, "yacc"], [ko])
        P.dma(yo[:, t0:t0 + nt], o[:, 0:nt], semkey=ko, r=[ko], is_out=True)
    return P.finish()


def s5_inputs(ulat, uctx, j, prm):
    ch = slice(j * 128, (j + 1) * 128)
    G0 = 8 * j
    bre = np.zeros((128, 4, 128), np.float32)
    bim = np.zeros((128, 4, 128), np.float32)
    cre = np.zeros((128, 4, 128), np.float32)
    cim = np.zeros((128, 4, 128), np.float32)
    for c in range(4):
        for g2 in range(2):
            g = 2 * c + g2
            bre[g * 16:(g + 1) * 16, c, g2 * 64:(g2 + 1) * 64] = prm["s5_b_re"][G0 + g].T
            bim[g * 16:(g + 1) * 16, c, g2 * 64:(g2 + 1) * 64] = prm["s5_b_im"][G0 + g].T
            cre[g2 * 64:(g2 + 1) * 64, c, g * 16:(g + 1) * 16] = prm["s5_c_re"][G0 + g].T
            cim[g2 * 64:(g2 + 1) * 64, c, g * 16:(g + 1) * 16] = prm["s5_c_im"][G0 + g].T

    def st(a):
        a = a[:, G0:G0 + 8].reshape(2, 4, 2, 64)
        return np.ascontiguousarray(a.transpose(2, 3, 0, 1).reshape(128, 8))
    ldt = np.broadcast_to(prm["s5_log_dt"][:, :, None], (2, 32, 64))
    return dict(uT=np.ascontiguousarray(np.concatenate([uctx[ch], ulat[ch]], axis=1)), bre=bre, bim=bim, cre=cre, cim=cim,
                are=st(prm["s5_a_re"]), aim=st(prm["s5_a_im"]), ldt=st(ldt),
                dsk=np.ascontiguousarray(prm["s5_d"][ch][:, None]))


RW_T = TCTX + TLAT
RW_NCH = RW_T // 64
RW_BLOCKS = [(0, TCTX, 0)] + [(TCTX + 512 * i, 512, TCTX + 2 + 512 * i) for i in range(16)]
RW_PADT = TCTX + 2 + TLAT + 2
NEG_SQRT_E = -0.6065306597126334


def rw_consts():
    i = np.arange(128) % 64
    jj = np.arange(128) % 64
    strict = (np.arange(128) < 64)[None, :]
    I, J = i[:, None], jj[None, :]
    mf = np.where(strict, I < J, I <= J)
    mb = np.where(strict, I > J, I >= J)
    mask = np.stack([mf, mb], axis=1).astype(np.float32)
    a = np.arange(64)
    nmask = np.stack([a[:, None] > a[None, :], a[:, None] < a[None, :]], axis=1).astype(np.float32)
    m0 = np.ones((128, 512), np.float32)
    m0[:, ::64] = 0.0
    bones = np.zeros((128, 128), np.float32)
    bones[:64, :64] = 1.0
    bones[64:, 64:] = 1.0
    hsel = np.zeros((128, 2), np.float32)
    hsel[:64, 0] = 1.0
    hsel[64:, 1] = 1.0
    ib = np.concatenate([np.eye(64, dtype=np.float32)] * 2, axis=0)
    return dict(mask=mask, nmask=nmask, m0=m0, bones=bones, hsel=hsel, ib=ib, ident=np.eye(128, dtype=np.float32))


def build_rwkv(nblk=17, dbg=99):
    P = Prog()
    din = P.din
    rP, kP, vP = din("rP", [128, RW_PADT]), din("kP", [128, RW_PADT]), din("vP", [128, RW_PADT])
    dwT, daT, dgT = din("dwT", [32, RW_T]), din("daT", [32, RW_T]), din("dgT", [96, RW_T])
    cwD, wupD, aupD, gupD = din("cw", [128, 9]), din("wup", [32, 2, 128]), din("aup", [32, 2, 128]), din("gup", [96, 128])
    w0D, a0D, kkwD, kaD, rkD = din("w0", [128, 2]), din("a0", [128, 2]), din("kkw", [128, 1]), din("ka", [128, 1]), din("rk", [128, 1])
    gnwD, gnbD = din("gnw", [64, 128]), din("gnb", [64, 128])
    maskD, nmaskD, m0D, bonesD, hselD, ibD, identD = (din("mask", [128, 2, 128]), din("nmask", [64, 2, 64]), din("m0", [128, 512]),
                                                      din("bones", [128, 128]), din("hsel", [128, 2]), din("ib", [128, 64]), din("ident", [128, 128]))
    o_tm = P.dout("o", [RW_T, 128])

    def load(shape, src, key):
        t = P.sb(shape)
        P.dma(t[:], src, semkey=key, w=[key])
        return t
    cw, wup, aup, gup = load([128, 9], cwD, "cw"), load([32, 2, 128], wupD, "wup"), load([32, 2, 128], aupD, "aup"), load([96, 128], gupD, "gup")
    w0, a0, kkw, ka, rk = load([128, 2], w0D, "w0"), load([128, 2], a0D, "a0"), load([128, 1], kkwD, "kkw"), load([128, 1], kaD, "ka"), load([128, 1], rkD, "rk")
    gnw, gnb = load([64, 128], gnwD, "gnw"), load([64, 128], gnbD, "gnb")
    MASK, NMASK, mask0 = load([128, 2, 128], maskD, "mask"), load([64, 2, 64], nmaskD, "nmask"), load([128, 512], m0D, "m0")
    bones, HSEL, IB, ident = load([128, 128], bonesD, "bones"), load([128, 2], hselD, "hsel"), load([128, 64], ibD, "ib"), load([128, 128], identD, "ident")

    bk = [P.ps([128, 512]) for _ in range(8)]

    def B(i):
        return ("bank", i)
    tiles = {}

    def T(name, shape=(128, 512)):
        if name not in tiles:
            tiles[name] = P.sb(list(shape))
        return tiles[name]
    R1 = [P.sb([128, 8, 128]) for _ in range(2)]
    L1 = [P.sb([128, 8, 128]) for _ in range(2)]
    Z1 = [P.sb([128, 8, 128]) for _ in range(2)]
    Z2 = [P.sb([128, 8, 128]) for _ in range(2)]
    WL = [P.sb([128, 8]) for _ in range(2)]
    RKK = [P.sb([128, 512]) for _ in range(2)]
    SG = [P.sb([96, 512]) for _ in range(2)]
    XV = [P.sb([128, 2, 128]) for _ in range(2)]
    BK = [P.sb([128, 2, 64]) for _ in range(2)]
    ATs = [P.sb([128, 2, 128]) for _ in range(2)]
    PP0s = [P.sb([64, 2, 2, 64]) for _ in range(2)]
    PPs = [[P.sb([64, 2, 2, 64]) for _ in range(2)] for _ in range(2)]
    MTss = [P.sb([128, 64]) for _ in range(2)]
    RTss = [P.sb([128, 64]) for _ in range(2)]
    S = [P.sb([128, 64]) for _ in range(2)]
    Yf = P.sb([64, RW_NCH, 128])
    P.op("pool", lambda e: e.memset(S[0][:], 0.0), w=[("S", 0)])

    def v3(ap, nch):
        return ap[:, 0:nch * 64].rearrange("p (c j) -> p c j", j=64)

    def prep(bi, d, par):
        s0, nt, p0 = RW_BLOCKS[bi]
        nch = nt // 64
        raws = {}
        for nm, src in (("r", rP), ("k", kP), ("v", vP)):
            t = T("raw" + nm, (128, 514))
            P.dma(t[:, 0:nt + 2], src[:, p0 - 1 + 0:p0 - 1 + nt + 2] if False else src[:, p0:p0 + nt + 2], semkey="raw" + nm, w=["raw" + nm])
            raws[nm] = t
        dw, da = T("dw", (32, 512)), T("da", (32, 512))
        P.dma(dw[:, 0:nt], dwT[:, s0:s0 + nt], semkey="dw", w=["dw"])
        P.dma(da[:, 0:nt], daT[:, s0:s0 + nt], semkey="da", w=["da"])
        outs = {"r": T("rc"), "k": T("kc"), "v": T("vc")}
        for ai, nm in enumerate(("r", "k", "v")):
            rw, o = raws[nm], outs[nm]
            P.op("dve", lambda e, rw=rw, o=o, ai=ai: e.tensor_scalar(out=o[:, 0:nt], in0=rw[:, 0:nt], scalar1=cw[:, 3 * ai:3 * ai + 1], scalar2=None, op0=ALU.mult),
                 r=["raw" + nm, "cw"], w=[nm + "c"])
            for tap in (1, 2):
                P.op("dve", lambda e, rw=rw, o=o, ai=ai, tap=tap: e.scalar_tensor_tensor(
                    out=o[:, 0:nt], in0=rw[:, tap:tap + nt], scalar=cw[:, 3 * ai + tap:3 * ai + tap + 1], in1=o[:, 0:nt], op0=ALU.mult, op1=ALU.add),
                    r=["raw" + nm, "cw", nm + "c"], w=[nm + "c"])
        rc, kc, vc = outs["r"], outs["k"], outs["v"]
        P.op("pool", lambda e: e.tensor_copy(out=Z1[par][:, 0:nch, 64:128], in_=v3(vc, nch)), r=["vc"], w=[("Z1", par)])
        th = T("th", (32, 512))
        P.op("act", lambda e: e.activation(out=th[:, 0:nt], in_=dw[:, 0:nt], func=AF.Tanh), r=["dw"], w=["th"])
        dirs = [0] if d == 0 else [0, 1]
        aa, kt = {}, {}
        for dd in dirs:
            aa[dd] = T("a%d" % dd)
            P.op("pe", lambda e, dd=dd: e.matmul(bk[0][:, 0:nt], lhsT=aup[:, dd, :], rhs=da[:, 0:nt], start=True, stop=True), r=["aup", "da"], w=[B(0)])
            P.op("act", lambda e, dd=dd: e.activation(out=aa[dd][:, 0:nt], in_=bk[0][:, 0:nt], func=AF.Sigmoid, bias=a0[:, dd:dd + 1]),
                 r=[B(0), "a0"], w=["a%d" % dd])
        sw, lw = T("sw"), T("lw")
        P.op("pe", lambda e: e.matmul(bk[0][:, 0:nt], lhsT=wup[:, d, :], rhs=th[:, 0:nt], start=True, stop=True), r=["wup", "th"], w=[B(0)])
        P.op("act", lambda e: e.activation(out=sw[:, 0:nt], in_=bk[0][:, 0:nt], func=AF.Sigmoid, bias=w0[:, d:d + 1]), r=[B(0), "w0"], w=["sw"])
        P.op("dve", lambda e: e.tensor_scalar(out=lw[:, 0:nt], in0=sw[:, 0:nt], scalar1=NEG_SQRT_E, scalar2=None, op0=ALU.mult), r=["sw"], w=["lw"])
        kkr, sq, nr, kk = T("kkr"), T("sq"), T("nr"), T("kk")
        P.op("dve", lambda e: e.tensor_scalar(out=kkr[:, 0:nt], in0=kc[:, 0:nt], scalar1=kkw[:, 0:1], scalar2=None, op0=ALU.mult), r=["kc", "kkw"], w=["kkr"])
        P.op("pool", lambda e: e.tensor_tensor(out=sq[:, 0:nt], in0=kkr[:, 0:nt], in1=kkr[:, 0:nt], op=ALU.mult), r=["kkr"], w=["sq"])
        P.op("pe", lambda e: e.matmul(bk[0][:, 0:nt], lhsT=bones[:], rhs=sq[:, 0:nt], start=True, stop=True), r=["bones", "sq"], w=[B(0)])
        P.op("act", lambda e: e.activation(out=nr[:, 0:nt], in_=bk[0][:, 0:nt], func=AF.Sqrt), r=[B(0)], w=["nr"])
        P.op("dve", lambda e: e.tensor_scalar(out=nr[:, 0:nt], in0=nr[:, 0:nt], scalar1=1e-12, scalar2=None, op0=ALU.max), r=["nr"], w=["nr"])
        P.op("dve", lambda e: e.reciprocal(out=nr[:, 0:nt], in_=nr[:, 0:nt]), r=["nr"], w=["nr"])
        P.op("dve", lambda e: e.tensor_tensor(out=kk[:, 0:nt], in0=kkr[:, 0:nt], in1=nr[:, 0:nt], op=ALU.mult), r=["kkr", "nr"], w=["kk"])
        for dd in dirs:
            kt[dd] = T("kt%d" % dd)
            tk = T("tk")
            P.op("pool", lambda e, dd=dd: e.tensor_scalar(out=tk[:, 0:nt], in0=aa[dd][:, 0:nt], scalar1=-1.0, scalar2=ka[:, 0:1], op0=ALU.add, op1=ALU.mult),
                 r=["a%d" % dd, "ka"], w=["tk"])
            P.op("dve", lambda e, dd=dd: e.scalar_tensor_tensor(out=kt[dd][:, 0:nt], in0=tk[:, 0:nt], scalar=1.0, in1=kc[:, 0:nt], op0=ALU.add, op1=ALU.mult),
                 r=["tk", "kc"], w=["kt%d" % dd])
        beta = T("beta")
        P.op("pool", lambda e: e.tensor_tensor(out=beta[:, 0:nt], in0=kk[:, 0:nt], in1=aa[d][:, 0:nt], op=ALU.mult), r=["kk", "a%d" % d], w=["beta"])
        lWf, lW, lWex, dl = T("lWf"), T("lW"), T("lWex"), T("dl")
        P.op("dve", lambda e: e.tensor_tensor_scan(out=lWf[:, 0:nt], data0=mask0[:, 0:nt], data1=lw[:, 0:nt], initial=0.0, op0=ALU.mult, op1=ALU.add),
             r=["m0", "lw"], w=["lWf"])
        tot = v3(lWf, nch)[:, :, 63:64]
        totb = tot.broadcast_to([128, nch, 64])
        if d == 0:
            lW = lWf
            klW = "lWf"
        else:
            klW = "lW"
            P.op("dve", lambda e: e.tensor_tensor(out=v3(lW, nch), in0=totb, in1=v3(lWf, nch), op=ALU.subtract), r=["lWf"], w=["lW"])
            P.op("dve", lambda e: e.tensor_tensor(out=lW[:, 0:nt], in0=lW[:, 0:nt], in1=lw[:, 0:nt], op=ALU.add), r=["lW", "lw"], w=["lW"])
        P.op("pool", lambda e: e.tensor_tensor(out=lWex[:, 0:nt], in0=lW[:, 0:nt], in1=lw[:, 0:nt], op=ALU.subtract), r=[klW, "lw"], w=["lWex"])
        P.op("dve", lambda e: e.tensor_tensor(out=v3(dl, nch), in0=totb, in1=v3(lW, nch), op=ALU.subtract), r=["lWf", klW], w=["dl"])
        E1, E2, E3, E4 = T("E1"), T("E2"), T("E3"), T("E4")
        P.op("act", lambda e: e.activation(out=E1[:, 0:nt], in_=lWex[:, 0:nt], func=AF.Exp), r=["lWex"], w=["E1"])
        P.op("act", lambda e: e.activation(out=E2[:, 0:nt], in_=lW[:, 0:nt], func=AF.Exp), r=[klW], w=["E2"])
        P.op("act", lambda e: e.activation(out=E3[:, 0:nt], in_=lW[:, 0:nt], func=AF.Exp, scale=-1.0), r=[klW], w=["E3"])
        P.op("act", lambda e: e.activation(out=E4[:, 0:nt], in_=dl[:, 0:nt], func=AF.Exp), r=["dl"], w=["E4"])
        P.op("act", lambda e: e.activation(out=WL[par][:, 0:nch].rearrange("p (c o) -> p c o", o=1), in_=tot, func=AF.Exp), r=["lWf"], w=[("WL", par)])
        P.op("dve", lambda e: e.scalar_tensor_tensor(out=R1[par][:, 0:nch, 0:64], in0=v3(kk, nch), scalar=-1.0, in1=v3(E1, nch), op0=ALU.mult, op1=ALU.mult),
             r=["kk", "E1"], w=[("R1", par)])
        P.op("pool", lambda e: e.tensor_copy(out=Z1[par][:, 0:nch, 0:64], in_=R1[par][:, 0:nch, 0:64]), r=[("R1", par)], w=[("Z1", par)])
        P.op("dve", lambda e: e.tensor_tensor(out=R1[par][:, 0:nch, 64:128], in0=v3(rc, nch), in1=v3(E2, nch), op=ALU.mult), r=["rc", "E2"], w=[("R1", par)])
        P.op("pool", lambda e: e.tensor_tensor(out=L1[par][:, 0:nch, 0:64], in0=v3(beta, nch), in1=v3(E3, nch), op=ALU.mult), r=["beta", "E3"], w=[("L1", par)])
        P.op("dve", lambda e: e.tensor_tensor(out=L1[par][:, 0:nch, 64:128], in0=v3(kt[d], nch), in1=v3(E3, nch), op=ALU.mult), r=["kt%d" % d, "E3"], w=[("L1", par)])
        P.op("pool", lambda e: e.tensor_tensor(out=Z2[par][:, 0:nch, 0:64], in0=v3(beta, nch), in1=v3(E4, nch), op=ALU.mult), r=["beta", "E4"], w=[("Z2", par)])
        P.op("dve", lambda e: e.tensor_tensor(out=Z2[par][:, 0:nch, 64:128], in0=v3(kt[d], nch), in1=v3(E4, nch), op=ALU.mult), r=["kt%d" % d, "E4"], w=[("Z2", par)])
        if d == 1:
            dg = T("dg", (96, 512))
            P.dma(dg[:, 0:nt], dgT[:, s0:s0 + nt], semkey="dg", w=["dg"])
            P.op("act", lambda e: e.activation(out=SG[par][:, 0:nt], in_=dg[:, 0:nt], func=AF.Sigmoid), r=["dg"], w=[("SG", par)])
            ks = T("ks")
            P.op("pool", lambda e: e.tensor_tensor(out=ks[:, 0:nt], in0=kt[0][:, 0:nt], in1=kt[1][:, 0:nt], op=ALU.add), r=["kt0", "kt1"], w=["ks"])
            P.op("dve", lambda e: e.scalar_tensor_tensor(out=RKK[par][:, 0:nt], in0=rc[:, 0:nt], scalar=rk[:, 0:1], in1=ks[:, 0:nt], op0=ALU.mult, op1=ALU.mult),
                 r=["rc", "rk", "ks"], w=[("RKK", par)])

    state = {"cur": 0, "q": 0}

    def mm(out, lhsT, rhs, r, w, rows, start=True, stop=True):
        P.op("pe", lambda e: e.matmul(out, lhsT=lhsT, rhs=rhs, start=start, stop=stop), r=r, w=w)

    def chunk(bi, c, d, par):
        s0, nt, _ = RW_BLOCKS[bi]
        cg = s0 // 64 + c
        q = state["q"]
        state["q"] = 1 - q
        xv, bkq, ats = XV[q], BK[q], ATs[q]
        kxv, kbk, kat = ("XV", q), ("BK", q), ("ATs", q)
        PP0, PP, MTs, RTs = PP0s[q], PPs[q], MTss[q], RTss[q]
        kPP0, kMT, kRT = ("PP0", q), ("MTs", q), ("RTs", q)
        r1, l1, z1, z2 = R1[par], L1[par], Z1[par], Z2[par]
        if dbg < 1:
            return
        mm(bk[0][:, 0:128], z1[:, c, :], ident[:], [("Z1", par), "ident"], [B(0)], 128)
        mm(bk[0][:, 128:256], z2[:, c, :], ident[:], [("Z2", par), "ident"], [B(0)], 128)
        if d == 1:
            mm(bk[0][0:64, 256:384], z1[:, c, 64:128], ident[:], [("Z1", par), "ident"], [B(0)], 128)
        P.op("act", lambda e: e.activation(out=xv[0:64, :, 64:128], in_=bk[0][0:64, 0:128].rearrange("p (h k) -> p h k", h=2), func=AF.Copy),
             r=[B(0)], w=[kxv])
        P.op("act", lambda e: e.activation(out=xv[64:128, :, 0:64], in_=bk[0][64:128, 0:128].rearrange("p (h k) -> p h k", h=2), func=AF.Copy),
             r=[B(0)], w=[kxv])
        P.op("dve", lambda e: e.tensor_copy(out=bkq[:, :, :], in_=bk[0][:, 128:256].rearrange("p (h k) -> p h k", h=2)), r=[B(0)], w=[kbk])
        if d == 1:
            vtm = T("vtm%d" % q, (64, 128))
            P.op("act", lambda e: e.activation(out=vtm[:, :], in_=bk[0][0:64, 256:384], func=AF.Copy), r=[B(0)], w=[("vtm", q)])
        yield
        if dbg < 2:
            return
        for h in range(2):
            hs = slice(64 * h, 64 * h + 64)
            mm(bk[1 + h][:, 0:128], l1[hs, c, :], r1[hs, c, :], [("L1", par), ("R1", par)], [B(1 + h)], 64)
            mm(bk[1 + h][0:64, 128:192], r1[hs, c, 0:64], l1[hs, c, 0:64], [("L1", par), ("R1", par)], [B(1 + h)], 64)
        for h in range(2):
            P.op("dve", lambda e, h=h: e.tensor_tensor(out=ats[:, h, :], in0=bk[1 + h][:, 0:128], in1=MASK[:, d, :], op=ALU.mult), r=[B(1 + h), "mask"], w=[kat])
            P.op("dve", lambda e, h=h: e.tensor_tensor(out=PP0[:, h, 0, :], in0=bk[1 + h][0:64, 128:192], in1=NMASK[:, d, :], op=ALU.mult),
                 r=[B(1 + h), "nmask"], w=[kPP0])
        yield
        if dbg < 3:
            return
        for h in range(2):
            mm(bk[4][0:64, 64 * h:64 * h + 64], ats[64:128, h, 0:64], xv[64:128, h, 0:64], [kat, kxv], [B(4)], 64)
        P.op("act", lambda e: e.activation(out=xv[0:64, :, 0:64], in_=bk[4][0:64, 0:128].rearrange("p (h k) -> p h k", h=2), func=AF.Copy), r=[B(4)], w=[kxv])
        yield
        if dbg < 4:
            return
        ppc, kpp = PP0, kPP0
        for l in range(6):
            for h in range(2):
                pt = ats[0:64, h, 0:64] if l == 0 else ppc[:, h, 1, :]
                mm(bk[3][0:64, 128 * h:128 * h + 128], pt, xv[0:64, h, :], [kat, kpp, kxv], [B(3)], 64)
            if l < 5:
                for h in range(2):
                    pt = ats[0:64, h, 0:64] if l == 0 else ppc[:, h, 1, :]
                    pm = ppc[:, h, 0, :]
                    mm(bk[5][0:64, 128 * h:128 * h + 64], pt, pm, [kat, kpp], [B(5)], 64)
                    mm(bk[5][0:64, 128 * h + 64:128 * h + 128], pm, pt, [kat, kpp], [B(5)], 64)
            P.op("dve", lambda e: e.tensor_tensor(out=xv[0:64, :, :], in0=bk[3][0:64, 0:256].rearrange("p (h k) -> p h k", h=2), in1=xv[0:64, :, :], op=ALU.add),
                 r=[B(3), kxv], w=[kxv])
            if l < 5:
                nx = PP[l % 2]
                P.op("act", lambda e, nx=nx: e.activation(out=nx[:, :, :, :].rearrange("p h t k -> p (h t k)"), in_=bk[5][0:64, 0:256], func=AF.Copy),
                     r=[B(5)], w=[("PP", q, l % 2)])
                ppc, kpp = nx, ("PP", q, l % 2)
            yield
        yield
        if dbg < 5:
            return
        for h in range(2):
            hs = slice(64 * h, 64 * h + 64)
            mm(bk[6][hs, 0:64], xv[0:64, h, 64:128], bkq[0:64, h, :], [kxv, kbk], [B(6)], 64)
            mm(bk[6][hs, 64:128], xv[0:64, h, 64:128], ats[0:64, h, 64:128], [kxv, kat], [B(6)], 64)
        P.op("dve", lambda e: e.scalar_tensor_tensor(out=MTs[:, :], in0=IB[:, :], scalar=WL[par][:, c:c + 1], in1=bk[6][:, 0:64], op0=ALU.mult, op1=ALU.add),
             r=["ib", ("WL", par), B(6)], w=[kMT])
        P.op("dve", lambda e: e.tensor_tensor(out=RTs[:, :], in0=bk[6][:, 64:128], in1=r1[:, c, 64:128], op=ALU.add), r=[B(6), ("R1", par)], w=[kRT])
        if dbg < 6:
            return
        yield
        cur = state["cur"]
        for h in range(2):
            hs = slice(64 * h, 64 * h + 64)
            mm(bk[7][0:64, 64 * h:64 * h + 64], ats[:, h, 64:128], xv[:, h, 0:64], [kat, kxv], [B(7)], 128, True, False)
            mm(bk[7][0:64, 64 * h:64 * h + 64], RTs[hs, :], S[cur][hs, :], [kRT, ("S", cur)], [B(7)], 64, False, True)
        for h in range(2):
            hs = slice(64 * h, 64 * h + 64)
            mm(bk[6][hs, 128:192], bkq[:, h, :], xv[:, h, 0:64], [kbk, kxv], [B(6)], 128, True, False)
            mm(bk[6][hs, 128:192], MTs[hs, :], S[cur][hs, :], [kMT, ("S", cur)], [B(6)], 64, False, True)
        P.op("act", lambda e: e.activation(out=S[1 - cur][:, :], in_=bk[6][:, 128:192], func=AF.Copy), r=[B(6)], w=[("S", 1 - cur)])
        state["cur"] = 1 - cur
        if d == 0:
            P.op("act", lambda e: e.activation(out=Yf[:, cg, :], in_=bk[7][0:64, 0:128], func=AF.Copy), r=[B(7)], w=["Yf"])
            return
        if dbg < 7:
            return
        y, yc, sq, o = T("y%d" % q, (64, 128)), T("yc%d" % q, (64, 128)), T("ysq%d" % q, (64, 128)), T("o%d" % (cg % 2), (64, 128))
        mu, var, rstd = T("mu%d" % q, (64, 2)), T("var%d" % q, (64, 2)), T("rstd%d" % q, (64, 2))
        tl = slice(64 * c, 64 * c + 64)
        P.op("dve", lambda e: e.tensor_tensor(out=y[:, :], in0=bk[7][0:64, 0:128], in1=Yf[:, cg, :], op=ALU.add), r=[B(7), "Yf"], w=[("y", q)])
        yield
        mm(bk[0][0:64, 384:512], SG[par][:, tl], gup[:, :], [("SG", par), "gup"], [B(0)], 96)
        mm(bk[4][0:64, 128:130], RKK[par][:, tl], HSEL[:, :], [("RKK", par), "hsel"], [B(4)], 128)
        for h in range(2):
            hc = slice(64 * h, 64 * h + 64)
            P.op("dve", lambda e, h=h, hc=hc: e.scalar_tensor_tensor(out=y[:, hc], in0=vtm[:, hc], scalar=bk[4][0:64, 128 + h:129 + h], in1=y[:, hc],
                                                                   op0=ALU.mult, op1=ALU.add), r=[("vtm", q), B(4), ("y", q)], w=[("y", q)])
        P.op("dve", lambda e: e.tensor_reduce(out=mu[:, :], in_=y[:, :].rearrange("p (h k) -> p h k", h=2), axis=AX.X, op=ALU.add), r=[("y", q)], w=[("mu", q)])
        P.op("dve", lambda e: e.tensor_scalar(out=mu[:, :], in0=mu[:, :], scalar1=-1.0 / 64, scalar2=None, op0=ALU.mult), r=[("mu", q)], w=[("mu", q)])
        for h in range(2):
            hc = slice(64 * h, 64 * h + 64)
            P.op("dve", lambda e, h=h, hc=hc: e.tensor_scalar(out=yc[:, hc], in0=y[:, hc], scalar1=mu[:, h:h + 1], scalar2=None, op0=ALU.add), r=[("y", q), ("mu", q)], w=[("yc", q)])
        P.op("pool", lambda e: e.tensor_tensor(out=sq[:, :], in0=yc[:, :], in1=yc[:, :], op=ALU.mult), r=[("yc", q)], w=[("ysq", q)])
        P.op("dve", lambda e: e.tensor_reduce(out=var[:, :], in_=sq[:, :].rearrange("p (h k) -> p h k", h=2), axis=AX.X, op=ALU.add), r=[("ysq", q)], w=[("var", q)])
        P.op("act", lambda e: e.activation(out=rstd[:, :], in_=var[:, :], func=AF.Sqrt, scale=1.0 / 64, bias=64e-5), r=[("var", q)], w=[("rstd", q)])
        P.op("dve", lambda e: e.reciprocal(out=rstd[:, :], in_=rstd[:, :]), r=[("rstd", q)], w=[("rstd", q)])
        for h in range(2):
            hc = slice(64 * h, 64 * h + 64)
            P.op("dve", lambda e, h=h, hc=hc: e.scalar_tensor_tensor(out=yc[:, hc], in0=yc[:, hc], scalar=rstd[:, h:h + 1], in1=gnw[:, hc], op0=ALU.mult, op1=ALU.mult),
                 r=[("yc", q), ("rstd", q), "gnw"], w=[("yc", q)])
        P.op("pool", lambda e: e.tensor_tensor(out=yc[:, :], in0=yc[:, :], in1=gnb[:, :], op=ALU.add), r=[("yc", q), "gnb"], w=[("yc", q)])
        P.op("dve", lambda e: e.tensor_tensor(out=o[:, :], in0=bk[0][0:64, 384:512], in1=yc[:, :], op=ALU.mult), r=[B(0), ("yc", q)], w=[("o", cg % 2)])
        P.dma(o_tm[64 * cg:64 * cg + 64, :], o[:, :], semkey=("o", cg % 2), r=[("o", cg % 2)], is_out=True)

    nb = 0
    for d in range(2):
        order = list(range(17)) if d == 0 else [0] + list(range(16, 0, -1))
        for bi in order[:nblk]:
            par = nb % 2
            nb += 1
            prep(bi, d, par)
            nch = RW_BLOCKS[bi][1] // 64
            cl = list(range(nch) if d == 0 else range(nch - 1, -1, -1))
            for i0 in range(0, len(cl), 2):
                gens = [chunk(bi, c, d, par) for c in cl[i0:i0 + 2]]
                while gens:
                    for g in list(gens):
                        try:
                            next(g)
                        except StopIteration:
                            gens.remove(g)
        if d == 0:
            P.op("pool", lambda e: e.memset(S[state["cur"]][:], 0.0), w=[("S", state["cur"])])
    return P.finish()


def rwkv_inputs(plat, pctx, j, prm, cst):
    a = np.arange(128)
    cols = j * 128 + a

    def pad(c0):
        z = np.zeros((128, 1), np.float32)
        return np.ascontiguousarray(np.concatenate([z, pctx[c0 + cols], z, z, plat[c0 + cols], z], axis=1))

    def sfm(c0, n):
        return np.ascontiguousarray(np.concatenate([pctx[c0:c0 + n], plat[c0:c0 + n]], axis=1))
    cw = np.stack([prm["rw_conv"][tap, ai * 512 + cols] for ai in range(3) for tap in range(3)], axis=1)
    d = dict(rP=pad(1536), kP=pad(2048), vP=pad(2560), dwT=sfm(3072, 32), daT=sfm(3104, 32), dgT=sfm(3136, 96),
             cw=np.ascontiguousarray(cw), wup=np.ascontiguousarray(prm["rw_w_up"].transpose(1, 0, 2)[:, :, cols]),
             aup=np.ascontiguousarray(prm["rw_a_up"].transpose(1, 0, 2)[:, :, cols]), gup=np.ascontiguousarray(prm["rw_g_up"][:, cols]),
             w0=np.ascontiguousarray(prm["rw_w0"][:, cols].T), a0=np.ascontiguousarray(prm["rw_a0"][:, cols].T),
             kkw=np.ascontiguousarray(prm["rw_k_k"][cols][:, None]), ka=np.ascontiguousarray(prm["rw_k_a"][cols][:, None]),
             rk=np.ascontiguousarray(prm["rw_r_k"].reshape(512)[cols][:, None]),
             gnw=np.ascontiguousarray(np.broadcast_to(prm["rw_gn_w"][cols][None, :], (64, 128))),
             gnb=np.ascontiguousarray(np.broadcast_to(prm["rw_gn_b"][cols][None, :], (64, 128))))
    d.update(cst)
    return d


TC = 8
NBLK = 33
_PROGS = {}


def _prog(name, fn):
    if name not in _PROGS:
        _PROGS[name] = fn()
    return _PROGS[name]


def _fm(v):
    return np.swapaxes(v.reshape(v.shape[:-1] + (KC, 128)), -1, -2)


def _tok_to_blocks(t):
    return np.ascontiguousarray(t.reshape(NBLK, NT, KC, 128).transpose(0, 3, 2, 1))


def _blocks_to_tok(b):
    return b.transpose(0, 3, 2, 1).reshape(NBLK * NT, D)


def _run(nc, ims):
    return run_bass_kernel_spmd(nc, ims, core_ids=list(range(len(ims)))).results


def kernel(x, c, ctx, c_ctx, w_ada, b_ada, norm_g, ffn1_wg, ffn1_wu, ffn1_wd, ffn2_wg, ffn2_wu, ffn2_wd,
           w_in, w_out, na_q_g, na_k_g, na_rpb, rw_conv, rw_w0, rw_w_up, rw_a0, rw_a_up, rw_g_up,
           rw_k_k, rw_k_a, rw_r_k, rw_gn_w, rw_gn_b, wa_q_g, wa_k_g, wa_sink, s5_a_re, s5_a_im,
           s5_log_dt, s5_b_re, s5_b_im, s5_c_re, s5_c_im, s5_d, s5_glu_w, s5_glu_b):
    f = lambda a: np.ascontiguousarray(np.asarray(a, dtype=np.float32))
    x, c, ctx, c_ctx = f(x), f(c), f(ctx), f(c_ctx)
    L = w_ada.shape[0]
    NB = -(-NBLK // TC)
    bidx = np.minimum(np.arange(TC * NB), NBLK - 1).reshape(TC, NB)
    c3 = np.stack([c[0], c[1], c_ctx])
    cT = np.ascontiguousarray(c3.reshape(3, KC, 128).transpose(2, 1, 0))
    w_ada, b_ada = np.asarray(w_ada), np.asarray(b_ada)
    ims = [dict(cT=cT, w=f(w_ada[:, :, i * ADA_N:(i + 1) * ADA_N]),
                b=f(np.broadcast_to(b_ada[:, None, i * ADA_N:(i + 1) * ADA_N], (L, 3, ADA_N)))) for i in range(8)]
    res = _run(_prog("ada", lambda: build_ada(L)), ims)
    mods = np.concatenate([r["mod"] for r in res], axis=-1).reshape(L, 3, NMOD, D)
    rows = np.array([0] * 16 + [1] * 16 + [2])

    toks = np.concatenate([x[0], x[1], ctx[0], ctx[1]], axis=0)
    xb = _tok_to_blocks(toks)
    cst_at = attn_consts()
    cst_rw = rw_consts()
    for l in range(L):
        ml = _fm(mods[l])
        mvb = ml[rows]
        mv1 = np.ascontiguousarray(mvb[:, 0:5].transpose(0, 2, 1, 3))
        mv2 = np.ascontiguousarray(mvb[:, 5:9].transpose(0, 2, 1, 3))
        gl = _fm(f(norm_g[l]))
        wg, wu, wd, win = f(ffn1_wg[l]), f(ffn1_wu[l]), f(ffn1_wd[l]), f(w_in[l])
        gv1 = np.ascontiguousarray(gl[0:2].transpose(1, 0, 2))
        ims = [dict(xT=np.ascontiguousarray(xb[bidx[i]]), mv=np.ascontiguousarray(mv1[bidx[i]]),
                    gv=gv1, wg=wg, wu=wu, wd=wd, win=win) for i in range(TC)]
        res = _run(_prog("t1", lambda: build_t1(NB)), ims)
        xb = np.concatenate([r["xo"] for r in res], axis=0)[:NBLK]
        pT = np.concatenate([r["pT"] for r in res], axis=0)[:NBLK]
        del wg, wu, wd, win, ims, res
        plat = [np.ascontiguousarray(pT[b * 16:(b + 1) * 16].transpose(1, 0, 2).reshape(DIN, TLAT)) for b in range(2)]
        pctx = [np.ascontiguousarray(pT[32][:, b * TCTX:(b + 1) * TCTX]) for b in range(2)]
        prm = dict(cst_at)
        prm.update(na_q_g=f(na_q_g[l]), na_k_g=f(na_k_g[l]), wa_q_g=f(wa_q_g[l]), wa_k_g=f(wa_k_g[l]), wa_sink=f(wa_sink[l]),
                   na_tabs=na_bias_tables(f(na_rpb[l])))
        lp = dict(rw_conv=f(rw_conv[l]), rw_w0=f(rw_w0[l]), rw_w_up=f(rw_w_up[l]), rw_a0=f(rw_a0[l]), rw_a_up=f(rw_a_up[l]),
                  rw_g_up=f(rw_g_up[l]), rw_k_k=f(rw_k_k[l]), rw_k_a=f(rw_k_a[l]), rw_r_k=f(rw_r_k[l]), rw_gn_w=f(rw_gn_w[l]),
                  rw_gn_b=f(rw_gn_b[l]), s5_a_re=f(s5_a_re[l]), s5_a_im=f(s5_a_im[l]), s5_log_dt=f(s5_log_dt[l]),
                  s5_b_re=f(s5_b_re[l]), s5_b_im=f(s5_b_im[l]), s5_c_re=f(s5_c_re[l]), s5_c_im=f(s5_c_im[l]), s5_d=f(s5_d[l]))
        cores = [(cid // 4, cid % 4) for cid in range(8)]
        rA = _run(_prog("A", lambda: build_attn("A")), [attn_inputs("A", plat[b], pctx[b], j, prm) for b, j in cores])
        rC = _run(_prog("C", lambda: build_attn("C")), [attn_inputs("C", plat[b], pctx[b], j, prm) for b, j in cores])
        rB = _run(_prog("rw", build_rwkv), [rwkv_inputs(plat[b], pctx[b], j, lp, cst_rw) for b, j in cores])
        rD = _run(_prog("s5", build_s5), [s5_inputs(plat[b][4000:4512], pctx[b][4000:4512], j, lp) for b, j in cores])
        olat = np.empty((2, TLAT, D), np.float32)
        octx = np.empty((2, TCTX, D), np.float32)
        for cid, (b, j) in enumerate(cores):
            cs = slice(j * 128, (j + 1) * 128)
            oa, oc_, ob, od = rA[cid]["o"], rC[cid]["o"], rB[cid]["o"], rD[cid]["yT"].T
            olat[b][:, 0:512][:, cs] = oa[:TLAT]
            octx[b][:, 0:512][:, cs] = oa[TLAT:]
            olat[b][:, 512:1024][:, cs] = ob[TCTX:]
            octx[b][:, 512:1024][:, cs] = ob[:TCTX]
            olat[b][:, 1024:1536][:, cs] = oc_[:TLAT]
            octx[b][:, 1024:1536][:, cs] = oc_[TLAT:]
            olat[b][:, 1536:2048][:, cs] = od[TCTX:]
            octx[b][:, 1536:2048][:, cs] = od[:TCTX]
        ob_ = _tok_to_blocks(np.concatenate([olat[0], olat[1], octx[0], octx[1]], axis=0))
        del pT, plat, pctx, rA, rB, rC, rD, olat, octx
        wg, wu, wd, wo = f(ffn2_wg[l]), f(ffn2_wu[l]), f(ffn2_wd[l]), f(w_out[l])
        gw = f(s5_glu_w[l])
        gb = np.ascontiguousarray(f(s5_glu_b[l]).reshape(4, 128).T)
        gv2 = np.ascontiguousarray(gl[2])
        ims = [dict(xT=np.ascontiguousarray(xb[bidx[i]]), oT=np.ascontiguousarray(ob_[bidx[i]]),
                    mv=np.ascontiguousarray(mv2[bidx[i]]), gv=gv2, gw=gw, gb=gb, wo=wo, wg=wg, wu=wu, wd=wd) for i in range(TC)]
        res = _run(_prog("t2", lambda: build_t2(NB)), ims)
        xb = np.concatenate([r["xo"] for r in res], axis=0)[:NBLK]
        del wg, wu, wd, wo, ims, res, ob_
    out = _blocks_to_tok(xb)[:2 * TLAT].reshape(2, TLAT, D)
    return np.ascontiguousarray(out.astype(np.float32))
```

```python
import numpy as np
from contextlib import ExitStack
import concourse.bass as bass
import concourse.mybir as mybir
from concourse.bass_utils import run_bass_kernel_spmd

F32 = mybir.dt.float32
ALU = mybir.AluOpType
AF = mybir.ActivationFunctionType
AX = mybir.AxisListType

D = 2048
DFF = 5632
DIN = 4512
NMOD = 9
EPS = 1e-6


class _Op:
    __slots__ = ("eng", "fn", "deps", "inc", "seq", "dsem", "dval", "dinc", "flushed", "last")

    def __init__(self, eng, fn):
        self.eng, self.fn, self.deps, self.inc, self.seq = eng, fn, [], False, 0
        self.dsem, self.dval, self.dinc = None, 0, 16
        self.flushed, self.last = False, None


class Prog:
    ENG = ("pe", "act", "dve", "pool", "sp")

    def __init__(self, fused=False):
        self.nc = bass.Bass("TRN2", target_bir_lowering=False)
        self.es = ExitStack()
        self.esp = ExitStack()
        self.ops = {e: [] for e in self.ENG}
        self.last_w = {}
        self.readers = {}
        self.dma_sems = {}
        self.dma_cnt = {}
        self.n = 0
        self.out_dmas = []
        self.fused = fused
        self.bind = {}
        self.prefix = ""
        self.ext_inputs = {}
        self.fence_ops = []
        self.fenced = set(self.ENG)
        self.esem = None
        self.seqc = {e: 0 for e in self.ENG}
        self.seen = {e: {} for e in self.ENG}
        self.touched = set()

    def din(self, name, shape, dtype=F32):
        if name in self.bind:
            return self.bind[name]
        full = self.prefix + name
        self.ext_inputs[full] = tuple(shape)
        return self.nc.dram_tensor(full, list(shape), dtype, kind="ExternalInput").ap()

    def dout(self, name, shape):
        if name in self.bind:
            return self.bind[name]
        return self.nc.dram_tensor(self.prefix + name, list(shape), F32, kind="ExternalOutput").ap()

    def scratch(self, name, shape):
        return self.nc.dram_tensor(name, list(shape), F32)

    def sb(self, shape, dtype=F32, persist=False):
        self.n += 1
        return (self.esp if persist else self.es).enter_context(self.nc.sbuf_tensor("sb%d" % self.n, list(shape), dtype))

    def ps(self, shape, dtype=F32):
        self.n += 1
        return self.es.enter_context(self.nc.psum_tensor("ps%d" % self.n, list(shape), dtype))

    def op(self, eng, fn, r=(), w=()):
        o = _Op(eng, fn)
        deps = []
        if eng not in self.fenced:
            deps.extend(self.fence_ops)
            self.fenced.add(eng)
        for k in r:
            lw = self.last_w.get(k)
            if lw is not None:
                deps.append(lw)
            if isinstance(k, tuple) and k[0] == "bank":
                for rd in self.readers.get(k, {}).values():
                    if rd.eng != eng:
                        deps.append(rd)
        for k in w:
            lw = self.last_w.get(k)
            if lw is not None:
                deps.append(lw)
            for rd in self.readers.get(k, {}).values():
                deps.append(rd)
        for d in deps:
            if d is o:
                continue
            if d.flushed and d.dsem is None and not d.inc:
                d = d.last
            if d.eng == eng and eng != "sp" and d.dsem is None and not d.flushed:
                if not any(self.last_w.get(k) is d for k in r):
                    continue
            o.deps.append(d)
            if d.dsem is None:
                d.inc = True
        for k in r:
            self.readers.setdefault(k, {})[eng] = o
        for k in w:
            self.last_w[k] = o
            self.readers[k] = {}
        self.ops[eng].append(o)
        return o

    def _async(self, o, semkey, inc):
        if semkey not in self.dma_sems:
            self.dma_sems[semkey] = self.esp.enter_context(self.nc.semaphore("dq%d" % len(self.dma_sems)))
            self.dma_cnt[semkey] = 0
        self.dma_cnt[semkey] += inc
        o.dsem, o.dval, o.dinc = self.dma_sems[semkey], self.dma_cnt[semkey], inc
        self.touched.add(semkey)
        return o

    def dma(self, out, in_, semkey, r=(), w=(), is_out=False, eng="sp"):
        o = self.op(eng, lambda e: e.dma_start(out=out, in_=in_), r=r, w=w)
        self._async(o, semkey, 16)
        if is_out:
            self.out_dmas.append(o)
        return o

    def coll(self, kind, src, dst, groups, semkey, r=(), w=()):
        o = self.op("pool", lambda e: e.collective_compute(kind, ALU.bypass, replica_groups=groups, ins=[src.opt()], outs=[dst.opt()]), r=r, w=w)
        return self._async(o, semkey, 1)

    def flush(self, final=False):
        nc = self.nc
        if self.esem is None:
            self.esem = {e: self.esp.enter_context(nc.semaphore("e_" + e)) for e in self.ENG}
        esem = self.esem
        lasts = {}
        for e in self.ENG:
            comp = [o for o in self.ops[e] if o.dsem is None]
            if comp:
                comp[-1].inc = True
                lasts[e] = comp[-1]
            c = self.seqc[e]
            for o in self.ops[e]:
                if o.dsem is None and o.inc:
                    c += 1
                    o.seq = c
            self.seqc[e] = c
        ops, out_dmas = self.ops, self.out_dmas

        def emit(ename, e):
            seen = self.seen[ename]
            for o in ops[ename]:
                need = {}
                for d in o.deps:
                    if d.dsem is not None:
                        s, v = d.dsem, d.dval
                    else:
                        s, v = esem[d.eng], d.seq
                    key = id(s)
                    if need.get(key, (None, 0))[1] < v:
                        need[key] = (s, v)
                for key, (s, v) in need.items():
                    if seen.get(key, 0) < v:
                        e.wait_ge(s, v)
                        seen[key] = v
                ins = o.fn(e)
                if o.dsem is not None:
                    if o.dinc == 16:
                        ins.then_inc(o.dsem, 16)
                    else:
                        ins.then_inc(o.dsem)
                elif o.inc:
                    ins.then_inc(esem[ename], 1)
            if ename == "sp" and final:
                fin = {}
                for o in out_dmas:
                    if fin.get(id(o.dsem), (None, 0))[1] < o.dval:
                        fin[id(o.dsem)] = (o.dsem, o.dval)
                for s, v in fin.values():
                    e.wait_ge(s, v)

        with nc.Block() as block:
            if ops["pe"]:
                @block.tensor
                def _(e):
                    emit("pe", e)
            if ops["act"]:
                @block.scalar
                def _(e):
                    emit("act", e)
            if ops["dve"]:
                @block.vector
                def _(e):
                    emit("dve", e)
            if ops["pool"]:
                @block.gpsimd
                def _(e):
                    emit("pool", e)
            if ops["sp"] or final:
                @block.sync
                def _(e):
                    emit("sp", e)
        fence = list(lasts.values())
        for k in self.touched:
            f = _Op("sp", None)
            f.dsem, f.dval, f.flushed = self.dma_sems[k], self.dma_cnt[k], True
            fence.append(f)
        self.touched = set()
        for e in self.ENG:
            for o in self.ops[e]:
                o.flushed = True
                o.last = lasts.get(e, o)
                o.fn = None
            self.ops[e] = []
        self.fence_ops = fence
        self.fenced = set()
        self.es.close()
        self.es = ExitStack()

    def finish(self):
        if self.fused:
            self.flush()
            return self
        self.flush(final=True)
        self.esp.close()
        return self.nc

    def close(self):
        self.flush(final=True)
        self.esp.close()
        return self.nc


NT = 512
KC = D // 128
F32R = mybir.dt.float32r
FAST = True


WDT_ = F32


def RR(ap):
    return ap.bitcast(F32R) if FAST else ap


class TL:
    def __init__(self, P):
        self.P = P
        self.x = P.sb([128, KC, NT])
        self.h = P.sb([128, KC, NT])
        self.a = P.sb([128, 22, NT])
        WDT = F32R if FAST else F32
        self.wa = [P.sb([128, KC, 128], WDT) for _ in range(2)]
        self.wb = [P.sb([128, KC, 128], WDT) for _ in range(2)]
        self.wd = [P.sb([128, 22, 128], WDT) for _ in range(2)]
        self.sq = [P.sb([128, NT]) for _ in range(2)]
        self.rstd = P.sb([128, NT])
        self.tmp = [P.sb([128, NT]) for _ in range(2)]
        self.ones = P.sb([128, 128])
        self.gs = P.sb([128, KC])
        self.ps_ss = P.ps([128, NT])
        self.ps_g = [P.ps([128, NT]) for _ in range(2)]
        self.ps_u = [P.ps([128, NT]) for _ in range(2)]
        self.ps_o = [P.ps([128, NT]) for _ in range(2)]
        self.cnt = 0
        P.op("pool", lambda e: e.memset(self.ones[:], 1.0), w=["ones"])

    def xkeys(self):
        return [("x", k) for k in range(KC)]

    def hkeys(self):
        return [("h", k) for k in range(KC)]

    def rms_modulate(self, g_ap, scale_ap, shift_ap, vkey):
        P = self.P
        x, h = self.x, self.h
        for k in range(KC):
            s = self.sq[k % 2]
            P.op("act", lambda e, s=s, k=k: e.activation(out=s[:], in_=x[:, k, :], func=AF.Square),
                 r=[("x", k)], w=[("sq", k % 2)])
            P.op("pe", lambda e, s=s, k=k: e.matmul(self.ps_ss[:], lhsT=self.ones[:], rhs=s[:],
                                                     start=(k == 0), stop=(k == KC - 1)),
                 r=["ones", ("sq", k % 2)], w=["ps_ss"])
        t0 = self.tmp[0]
        P.op("act", lambda e: e.activation(out=t0[:], in_=self.ps_ss[:], func=AF.Sqrt, scale=1.0 / D, bias=EPS),
             r=["ps_ss"], w=[("tmp", 0)])
        P.op("dve", lambda e: e.reciprocal(out=self.rstd[:], in_=t0[:]), r=[("tmp", 0)], w=["rstd"])
        P.op("dve", lambda e: e.scalar_tensor_tensor(out=self.gs[:], in0=scale_ap, scalar=1.0, in1=g_ap,
                                                      op0=ALU.add, op1=ALU.mult), r=[vkey], w=["gs"])
        for k in range(KC):
            t = self.tmp[k % 2]
            P.op("dve", lambda e, t=t, k=k: e.tensor_tensor(out=t[:], in0=x[:, k, :], in1=self.rstd[:], op=ALU.mult),
                 r=[("x", k), "rstd"], w=[("tmp", k % 2)])
            P.op("dve", lambda e, t=t, k=k: e.tensor_scalar(out=RR(h[:, k, :]), in0=t[:], scalar1=self.gs[:, k:k + 1],
                                                             scalar2=shift_ap[:, k:k + 1], op0=ALU.mult, op1=ALU.add),
                 r=[("tmp", k % 2), "gs", vkey], w=[("h", k)])

    def load_w(self, buf, key, src):
        self.P.dma(buf[:, :, :], src, semkey=key, w=[key], eng=("pool" if FAST else "sp"))

    def ffn(self, wg, wu, wd, gate_ap, vkey):
        P = self.P
        x, h, a = self.x, self.h, self.a
        for half in range(2):
            for jj in range(22):
                j = half * 22 + jj
                c = self.cnt
                self.cnt += 1
                wa, wb = self.wa[c % 2], self.wb[c % 2]
                self.load_w(wa, ("wa", c % 2), wg[j])
                self.load_w(wb, ("wb", c % 2), wu[j])
                pg, pu = self.ps_g[c % 2], self.ps_u[c % 2]
                for k in range(KC):
                    P.op("pe", lambda e, k=k, wa=wa, pg=pg: e.matmul(pg[:], lhsT=RR(wa[:, k, :]), rhs=RR(h[:, k, :]),
                                                                   start=(k == 0), stop=(k == KC - 1)),
                         r=[("wa", c % 2), ("h", k)], w=[("ps_g", c % 2)])
                for k in range(KC):
                    P.op("pe", lambda e, k=k, wb=wb, pu=pu: e.matmul(pu[:], lhsT=RR(wb[:, k, :]), rhs=RR(h[:, k, :]),
                                                                   start=(k == 0), stop=(k == KC - 1)),
                         r=[("wb", c % 2), ("h", k)], w=[("ps_u", c % 2)])
                s = self.sq[c % 2]
                P.op("act", lambda e, s=s, pg=pg: e.activation(out=s[:], in_=pg[:], func=AF.Silu),
                     r=[("ps_g", c % 2)], w=[("sq", c % 2)])
                P.op("dve", lambda e, s=s, pu=pu, jj=jj: e.tensor_tensor(out=RR(a[:, jj, :]), in0=s[:], in1=pu[:], op=ALU.mult),
                     r=[("sq", c % 2), ("ps_u", c % 2)], w=[("a", jj)])
            for m in range(KC):
                c = self.cnt
                self.cnt += 1
                wdb = self.wd[c % 2]
                P.dma(wdb[:, :, :], wd[half, m],
                      semkey=("wd", c % 2), w=[("wd", c % 2)], eng=("pool" if FAST else "sp"))
                po = self.ps_o[c % 2]
                for jj in range(22):
                    P.op("pe", lambda e, jj=jj, wdb=wdb, po=po: e.matmul(po[:], lhsT=RR(wdb[:, jj, :]), rhs=RR(a[:, jj, :]),
                                                                       start=(jj == 0), stop=(jj == 21)),
                         r=[("wd", c % 2), ("a", jj)], w=[("ps_o", c % 2)])
                P.op("dve", lambda e, m=m, po=po: e.scalar_tensor_tensor(out=x[:, m, :], in0=po[:], scalar=gate_ap[:, m:m + 1],
                                                                       in1=x[:, m, :], op0=ALU.mult, op1=ALU.add),
                     r=[("ps_o", c % 2), vkey, ("x", m)], w=[("x", m)])

    def proj(self, w, ncols, out_fn):
        P = self.P
        h = self.h
        nch = (ncols + 127) // 128
        for m in range(nch):
            mc = min(128, ncols - m * 128)
            c = self.cnt
            self.cnt += 1
            wa = self.wa[c % 2]
            self.load_w(wa, ("wa", c % 2), w[m])
            po = self.ps_o[c % 2]
            for k in range(KC):
                P.op("pe", lambda e, k=k, wa=wa, po=po, mc=mc: e.matmul(po[0:mc, :], lhsT=(RR(wa[:, k, 0:mc]) if mc == 128 else wa[:, k, 0:mc].bitcast(F32)), rhs=(RR(h[:, k, :]) if mc == 128 else h[:, k, :]),
                                                                      start=(k == 0), stop=(k == KC - 1)),
                     r=[("wa", c % 2), ("h", k)], w=[("ps_o", c % 2)])
            out_fn(m, mc, po, ("ps_o", c % 2))


def build_t1(NB):
    P = Prog()
    xT = P.din("xT", [NB, 128, KC, NT])
    mv = P.din("mv", [NB, 128, 5, KC])
    gv = P.din("gv", [128, 2, KC])
    wg = P.din("wg", [DFF // 128, 128, KC, 128])
    wu = P.din("wu", [DFF // 128, 128, KC, 128])
    wd = P.din("wd", [2, KC, 128, 22, 128])
    win = P.din("win", [36, 128, KC, 128])
    xo = P.dout("xo", [NB, 128, KC, NT])
    po_ = P.dout("pT", [NB, DIN, NT])
    T = TL(P)
    mvt = P.sb([128, 5, KC])
    gvt = P.sb([128, 2, KC])
    hg = P.sb([128, KC])
    ot = [P.sb([128, NT]) for _ in range(2)]
    P.dma(gvt[:], gv, semkey="gv", w=["gv"])
    for b in range(NB):
        P.dma(T.x[:], xT[b], semkey="x", w=T.xkeys())
        P.dma(mvt[:], mv[b], semkey="mv", w=["mv"])
        P.op("pool", lambda e: e.tensor_scalar(out=hg[:], in0=mvt[:, 2, :], scalar1=0.5, scalar2=None, op0=ALU.mult),
             r=["mv"], w=["hg"])
        T.rms_modulate(gvt[:, 0, :], mvt[:, 1, :], mvt[:, 0, :], "mv")
        T.ffn(wg, wu, wd, hg, "hg")
        P.dma(xo[b], T.x[:], semkey="xo", r=T.xkeys(), is_out=True)
        T.rms_modulate(gvt[:, 1, :], mvt[:, 4, :], mvt[:, 3, :], "mv")

        def out_fn(m, mc, ps, pskey, b=b):
            o = ot[m % 2]
            P.op("act", lambda e: e.activation(out=o[0:mc, :], in_=ps[0:mc, :], func=AF.Copy),
                 r=[pskey], w=[("ot", m % 2)])
            P.dma(po_[b, m * 128:m * 128 + mc, :], o[0:mc, :], semkey=("ot", m % 2), r=[("ot", m % 2)], is_out=True)
        T.proj(win, DIN, out_fn)
    return P.finish()


def build_t2(NB):
    P = Prog()
    xT = P.din("xT", [NB, 128, KC, NT])
    oT = P.din("oT", [NB, 128, KC, NT])
    mv = P.din("mv", [NB, 128, 4, KC])
    gv = P.din("gv", [128, KC])
    gw = P.din("gw", [512, 512])
    gb = P.din("gb", [128, 4])
    wo = P.din("wo", [KC, 128, KC, 128])
    wg = P.din("wg", [DFF // 128, 128, KC, 128])
    wu = P.din("wu", [DFF // 128, 128, KC, 128])
    wd = P.din("wd", [2, KC, 128, 22, 128])
    xo = P.dout("xo", [NB, 128, KC, NT])
    T = TL(P)
    mvt = P.sb([128, 4, KC])
    gvt = P.sb([128, KC])
    gwt = P.sb([128, 4, 512])
    gbt = P.sb([128, 4])
    hg = P.sb([128, KC])
    glu = P.sb([128, 4, NT])
    P.dma(gvt[:], gv, semkey="gv", w=["gv"])
    P.dma(gwt[:], gw.rearrange("(k p) m -> p k m", p=128), semkey="gw", w=["gw"])
    P.dma(gbt[:], gb, semkey="gb", w=["gb"])
    for b in range(NB):
        P.dma(T.x[:], xT[b], semkey="x", w=T.xkeys())
        P.dma(RR(T.h[:]), oT[b], semkey="h", w=T.hkeys(), eng=("pool" if FAST else "sp"))
        P.dma(mvt[:], mv[b], semkey="mv", w=["mv"])
        P.op("pool", lambda e: e.tensor_scalar(out=hg[:], in0=mvt[:, 3, :], scalar1=0.5, scalar2=None, op0=ALU.mult),
             r=["mv"], w=["hg"])
        for m in range(4):
            po = T.ps_g[m % 2]
            for k in range(4):
                P.op("pe", lambda e, m=m, k=k, po=po: e.matmul(po[:], lhsT=gwt[:, k, m * 128:(m + 1) * 128], rhs=T.h[:, 12 + k, :],
                                                             start=(k == 0), stop=(k == 3)),
                     r=["gw", ("h", 12 + k)], w=[("ps_g", m % 2)])
            P.op("act", lambda e, m=m, po=po: e.activation(out=glu[:, m, :], in_=po[:], func=AF.Sigmoid, bias=gbt[:, m:m + 1]),
                 r=[("ps_g", m % 2), "gb"], w=[("glu", m)])
        for m in range(4):
            P.op("dve", lambda e, m=m: e.tensor_tensor(out=RR(T.h[:, 12 + m, :]), in0=T.h[:, 12 + m, :], in1=glu[:, m, :], op=ALU.mult),
                 r=[("h", 12 + m), ("glu", m)], w=[("h", 12 + m)])

        def out_fn(m, mc, ps, pskey):
            P.op("dve", lambda e: e.scalar_tensor_tensor(out=T.x[:, m, :], in0=ps[:], scalar=mvt[:, 0, m:m + 1], in1=T.x[:, m, :],
                                                          op0=ALU.mult, op1=ALU.add),
                 r=[pskey, "mv", ("x", m)], w=[("x", m)])
        T.proj(wo, D, out_fn)
        T.rms_modulate(gvt[:, :], mvt[:, 2, :], mvt[:, 1, :], "mv")
        T.ffn(wg, wu, wd, hg, "hg")
        P.dma(xo[b], T.x[:], semkey="xo", r=T.xkeys(), is_out=True)
    return P.finish()


ADA_N = NMOD * D // 8


def build_ada(L):
    P = Prog()
    cT = P.din("cT", [128, KC, 3])
    w = P.din("w", [L, D, ADA_N])
    bb = P.din("b", [L, 3, ADA_N])
    out = P.dout("mod", [L, 3, ADA_N])
    ct = P.sb([128, KC, 3])
    sc = P.sb([128, KC, 3])
    wt = [P.sb([128, KC, 512]) for _ in range(2)]
    bt = P.sb([3, L, ADA_N])
    ot = P.sb([3, L, ADA_N])
    ps = [P.ps([3, 512]) for _ in range(2)]
    P.dma(ct[:], cT, semkey="c", w=["c"])
    P.dma(bt[:], bb.rearrange("l r n -> r l n"), semkey="b", w=["b"])
    P.op("act", lambda e: e.activation(out=sc[:], in_=ct[:], func=AF.Silu), r=["c"], w=["sc"])
    c = 0
    for l in range(L):
        for n0 in range(0, ADA_N, 512):
            nn = min(512, ADA_N - n0)
            wb, pb = wt[c % 2], ps[c % 2]
            P.dma(wb[:, :, 0:nn], w[l, :, n0:n0 + nn].rearrange("(k p) n -> p k n", p=128), semkey=("w", c % 2), w=[("w", c % 2)])
            for k in range(KC):
                P.op("pe", lambda e, k=k, wb=wb, pb=pb, nn=nn: e.matmul(pb[:, 0:nn], lhsT=sc[:, k, :], rhs=wb[:, k, 0:nn],
                                                                      start=(k == 0), stop=(k == KC - 1)),
                     r=["sc", ("w", c % 2)], w=[("ps", c % 2)])
            P.op("dve", lambda e, pb=pb, l=l, n0=n0, nn=nn: e.tensor_tensor(out=ot[:, l, n0:n0 + nn], in0=pb[:, 0:nn],
                                                                           in1=bt[:, l, n0:n0 + nn], op=ALU.add),
                 r=[("ps", c % 2), "b"], w=["ot"])
            c += 1
    P.dma(out.rearrange("l r n -> r l n"), ot[:], semkey="o", r=["ot"], is_out=True)
    return P.finish()


TLAT = 8192
TCTX = 256
TQ = TLAT + TCTX
NQT = TQ // 128
NEG = -1e30


def na_specs():
    types = {0: 0, 1: 1, 62: 3, 63: 4}
    specs = []
    for i in range(64):
        r0 = 2 * i
        lo = min(max(r0 - 4, 0), 120)
        hi = min(max(r0 + 1 - 4, 0), 120) + 7
        nch = (hi - lo + 2) // 2
        ty = types.get(i, 2)
        specs.append(([lo // 2 + c for c in range(nch)], ty))
    return specs


NA_REP = [0, 1, 2, 62, 63]
NA_NCH = [4, 4, 5, 4, 4]


def na_bias_tables(rpb):
    tabs = []
    q = np.arange(128)
    qr, qc = q // 64, q % 64
    for ty, i in enumerate(NA_REP):
        r0 = 2 * i
        lo = min(max(r0 - 4, 0), 120)
        r = r0 + qr
        rs = np.clip(r - 4, 0, 120)
        cs = np.clip(qc - 8, 0, 48)
        for c in range(NA_NCH[ty]):
            kl = np.arange(128)
            kr = lo + 2 * c + kl // 64
            kc = kl % 64
            valid = ((kr[:, None] >= rs[None, :]) & (kr[:, None] <= rs[None, :] + 7) &
                     (kc[:, None] >= cs[None, :]) & (kc[:, None] <= cs[None, :] + 15))
            bidx = (kr[:, None] - r[None, :] + 7) * 31 + (kc[:, None] - qc[None, :] + 15)
            bidx = np.where(valid, bidx, 0)
            t = np.where(valid[None], rpb[:, bidx], np.float32(NEG)).astype(np.float32)
            tabs.append(t)
    return np.stack(tabs, axis=2)


def wa_bias_table():
    j = np.arange(128)[:, None]
    q = np.arange(128)[None, :]
    prev = np.where(j >= q, 0.0, NEG)
    cur = np.zeros((128, 128))
    nxt = np.where(j <= q, 0.0, NEG)
    return np.stack([prev, cur, nxt], axis=1).astype(np.float32)


def rope_tables():
    t = np.arange(TLAT)
    inv = (1.0 / (np.float32(10000.0) ** (np.arange(16, dtype=np.float32) / np.float32(16)))).astype(np.float32)

    def ang(p):
        a = p.astype(np.float32)[:, None] * inv[None, :]
        return np.concatenate([a, a], -1)
    a = np.concatenate([ang(t // 64), ang(t % 64)], -1)
    return np.cos(a).astype(np.float32), np.sin(a).astype(np.float32)


def rot_matrix():
    m = np.zeros((128, 128), np.float32)
    for o in range(128):
        if o % 32 < 16:
            m[o + 16, o] = -1.0
        else:
            m[o - 16, o] = 1.0
    return m


def build_attn(kind):
    rope = kind == "C"
    P = Prog()
    NS = 2 * sum(NA_NCH) if kind == "A" else 3
    qT = P.din("qT", [128, TQ])
    kT = P.din("kT", [128, TQ])
    vtm = P.din("vtm", [TQ, 128])
    gq = P.din("gq", [128, 1])
    gk = P.din("gk", [128, 1])
    btab = P.din("btab", [128, NS, 128])
    if rope:
        cosT = P.din("cosT", [128, TLAT])
        sinT = P.din("sinT", [128, TLAT])
        prot = P.din("prot", [128, 128])
        sink = P.din("sink", [128, 2])
    o_tm = P.dout("o", [TQ, 128])

    qn = P.sb([128, TQ])
    kn = P.sb([128, TQ])
    V1 = P.sb([128, NQT, 2, 65])
    BT = P.sb([128, NS, 128])
    gqt = P.sb([128, 1])
    gkt = P.sb([128, 1])
    bones = P.sb([128, 128])
    raw = [P.sb([128, 512]) for _ in range(2)]
    sq = P.sb([128, 512])
    st = P.sb([128, 512])
    rs = P.sb([128, 512])
    tmpS = [P.sb([128, 640]) for _ in range(2)]
    E = [P.sb([128, 896]) for _ in range(2)]
    ot = [P.sb([128, 128]) for _ in range(2)]
    den = [P.sb([128, 2]) for _ in range(2)]
    banks = [P.ps([128, 512]) for _ in range(8)]

    P.dma(gqt[:], gq, semkey="gq", w=["gq"])
    P.dma(gkt[:], gk, semkey="gk", w=["gk"])
    P.dma(BT[:], btab, semkey="bt", w=["BT"])
    for h in range(2):
        P.dma(V1[:, :, h, 0:64], vtm[:, h * 64:(h + 1) * 64].rearrange("(t p) d -> p t d", p=128), semkey=("v", h), w=["V1"])
    P.op("pool", lambda e: e.memset(V1[:, :, :, 64:65], 1.0), w=["V1"])
    P.op("pool", lambda e: e.memset(bones[:], 0.0), w=["bones"])
    P.op("pool", lambda e: e.memset(bones[0:64, 0:64], 1.0), w=["bones"])
    P.op("pool", lambda e: e.memset(bones[64:128, 64:128], 1.0), w=["bones"])
    if rope:
        prt = P.sb([128, 128])
        cs = [P.sb([128, 512]) for _ in range(2)]
        sn = [P.sb([128, 512]) for _ in range(2)]
        t1 = P.sb([128, 512])
        t2 = P.sb([128, 512])
        skt = P.sb([128, 2])
        ske = P.sb([128, 2])
        P.dma(prt[:], prot, semkey="prot", w=["prot"])
        P.dma(skt[:], sink, semkey="sink", w=["sink"])
        P.op("act", lambda e: e.activation(out=ske[:], in_=skt[:], func=AF.Exp), r=["sink"], w=["ske"])

    c = 0
    for src, dst, g, nm in ((qT, qn, gqt, "qn"), (kT, kn, gkt, "kn")):
        for t0 in range(0, TQ, 512):
            nt = min(512, TQ - t0)
            rw = raw[c % 2]
            P.dma(rw[:, 0:nt], src[:, t0:t0 + nt], semkey=("raw", c % 2), w=[("raw", c % 2)])
            P.op("pool", lambda e, rw=rw, nt=nt: e.tensor_tensor(out=sq[:, 0:nt], in0=rw[:, 0:nt], in1=rw[:, 0:nt], op=ALU.mult),
                 r=[("raw", c % 2)], w=["sq"])
            P.op("pe", lambda e, nt=nt: e.matmul(banks[0][:, 0:nt], lhsT=bones[:], rhs=sq[:, 0:nt], start=True, stop=True),
                 r=["bones", "sq"], w=[("bank", 0)])
            P.op("act", lambda e, nt=nt: e.activation(out=st[:, 0:nt], in_=banks[0][:, 0:nt], func=AF.Sqrt, scale=1.0 / 64, bias=EPS),
                 r=[("bank", 0)], w=["st"])
            P.op("dve", lambda e, nt=nt: e.reciprocal(out=rs[:, 0:nt], in_=st[:, 0:nt]), r=["st"], w=["rs"])
            P.op("dve", lambda e, rw=rw, nt=nt, t0=t0, dst=dst, g=g: e.scalar_tensor_tensor(
                out=dst[:, t0:t0 + nt], in0=rw[:, 0:nt], scalar=g[:, 0:1], in1=rs[:, 0:nt], op0=ALU.mult, op1=ALU.mult),
                r=[("raw", c % 2), "rs", "gq", "gk"], w=[(nm, t0)])
            if rope and t0 < TLAT:
                P.dma(cs[c % 2][:], cosT[:, t0:t0 + 512], semkey=("cs", c % 2), w=[("cs", c % 2)])
                P.dma(sn[c % 2][:], sinT[:, t0:t0 + 512], semkey=("sn", c % 2), w=[("sn", c % 2)])
                P.op("pe", lambda e, t0=t0, dst=dst: e.matmul(banks[1][:], lhsT=prt[:], rhs=dst[:, t0:t0 + 512], start=True, stop=True),
                     r=["prot", (nm, t0)], w=[("bank", 1)])
                P.op("dve", lambda e, t0=t0, dst=dst, cc=cs[c % 2]: e.tensor_tensor(out=t1[:], in0=dst[:, t0:t0 + 512], in1=cc[:], op=ALU.mult),
                     r=[(nm, t0), ("cs", c % 2)], w=["t1"])
                P.op("dve", lambda e, ss=sn[c % 2]: e.tensor_tensor(out=t2[:], in0=banks[1][:], in1=ss[:], op=ALU.mult),
                     r=[("bank", 1), ("sn", c % 2)], w=["t2"])
                P.op("pool", lambda e, t0=t0, dst=dst: e.tensor_tensor(out=dst[:, t0:t0 + 512], in0=t1[:], in1=t2[:], op=ALU.add),
                     r=["t1", "t2"], w=[(nm, t0)])
            c += 1

    def nkeys(nm, tile):
        return [(nm, (tile * 128) // 512 * 512)]

    if kind == "A":
        specs = na_specs()
        base = np.concatenate([[0], np.cumsum(NA_NCH)])[:5]
        ntab = sum(NA_NCH)
    it = 0
    for i in range(NQT):
        if i < 64:
            if kind == "A":
                ktiles, ty = specs[i]
            else:
                ktiles = [k for k in (i - 1, i, i + 1) if 0 <= k < 64]
        else:
            ktiles = []
        nW = len(ktiles)
        par = i % 2
        po = banks[6 + par]
        for h in range(2):
            hs = slice(h * 64, (h + 1) * 64)
            ip = it % 2
            it += 1
            bA, bB = banks[2 + 2 * ip], banks[3 + 2 * ip]
            if kind == "A":
                s0 = (h * ntab + int(base[ty])) if nW else 0
            else:
                s0 = 1 if i == 0 else 0

            def sl(s):
                return (bA, s * 128) if s < 4 else (bB, (s - 4) * 128)
            chunks = [(kt, s) for s, kt in enumerate(ktiles)] + [(64, 5), (65, 6)]
            for kt, s in chunks:
                bk, off = sl(s)
                P.op("pe", lambda e, bk=bk, off=off, kt=kt, hs=hs, i=i: e.matmul(
                    bk[:, off:off + 128], lhsT=kn[hs, kt * 128:(kt + 1) * 128], rhs=qn[hs, i * 128:(i + 1) * 128], start=True, stop=True),
                    r=nkeys("kn", kt) + nkeys("qn", i), w=[("bank", 2 + 2 * ip + (0 if s < 4 else 1))])
            nA = min(nW, 4)
            Eb, tS = E[ip], tmpS[ip]
            if nA:
                P.op("dve", lambda e, nA=nA, s0=s0, tS=tS, bA=bA: e.scalar_tensor_tensor(
                    out=tS[:, 0:nA * 128], in0=bA[:, 0:nA * 128], scalar=0.125, in1=BT[:, s0:s0 + nA, :].rearrange("p s q -> p (s q)"),
                    op0=ALU.mult, op1=ALU.add), r=[("bank", 2 + 2 * ip), "BT"], w=[("tS", ip)])
            if nW == 5:
                P.op("dve", lambda e, s0=s0, tS=tS, bB=bB: e.scalar_tensor_tensor(
                    out=tS[:, 512:640], in0=bB[:, 0:128], scalar=0.125, in1=BT[:, s0 + 4, :],
                    op0=ALU.mult, op1=ALU.add), r=[("bank", 3 + 2 * ip), "BT"], w=[("tS", ip)])
            if nW:
                P.op("act", lambda e, nW=nW, Eb=Eb, tS=tS: e.activation(out=Eb[:, 0:nW * 128], in_=tS[:, 0:nW * 128], func=AF.Exp),
                     r=[("tS", ip)], w=[("E", ip)])
            P.op("act", lambda e, Eb=Eb, bB=bB: e.activation(out=Eb[:, 640:896], in_=bB[:, 128:384], func=AF.Exp, scale=0.125),
                 r=[("bank", 3 + 2 * ip)], w=[("E", ip)])
            for n, (kt, s) in enumerate(chunks):
                eo = s * 128
                P.op("pe", lambda e, n=n, kt=kt, eo=eo, Eb=Eb, h=h, po=po: e.matmul(
                    po[:, h * 65:(h + 1) * 65], lhsT=Eb[:, eo:eo + 128], rhs=V1[:, kt, h, :], start=(n == 0), stop=(n == len(chunks) - 1)),
                    r=[("E", ip), "V1"], w=[("bank", 6 + par)])
        dn, o = den[par], ot[par]
        pov = po[:, 0:130].rearrange("p (h c) -> p h c", h=2)
        if rope:
            P.op("dve", lambda e, dn=dn, pov=pov: e.tensor_tensor(out=dn[:, :].rearrange("p (h o) -> p h o", o=1), in0=pov[:, :, 64:65],
                                                                 in1=ske[:, :].rearrange("p (h o) -> p h o", o=1), op=ALU.add),
                 r=[("bank", 6 + par), "ske"], w=[("den", par)])
            P.op("dve", lambda e, dn=dn: e.reciprocal(out=dn[:], in_=dn[:]), r=[("den", par)], w=[("den", par)])
        else:
            P.op("dve", lambda e, dn=dn, pov=pov: e.reciprocal(out=dn[:, :].rearrange("p (h o) -> p h o", o=1), in_=pov[:, :, 64:65]),
                 r=[("bank", 6 + par)], w=[("den", par)])
        for h in range(2):
            P.op("dve", lambda e, h=h, dn=dn, o=o, po=po: e.tensor_scalar(out=o[:, h * 64:(h + 1) * 64], in0=po[:, h * 65:h * 65 + 64],
                                                                       scalar1=dn[:, h:h + 1], scalar2=None, op0=ALU.mult),
                 r=[("bank", 6 + par), ("den", par)], w=[("ot", par)])
        P.dma(o_tm[i * 128:(i + 1) * 128, :], o[:], semkey=("ot", par), r=[("ot", par)], is_out=True)
    return P.finish()


def attn_inputs(kind, plat, pctx, j, prm):
    def fm(cols):
        return np.ascontiguousarray(np.concatenate([plat[cols], pctx[cols]], axis=1))
    a = np.arange(128)
    if kind == "A":
        qc, kc, vc = j * 128 + a, 512 + j * 128 + a, 1024 + j * 128 + a
        tabs = prm["na_tabs"]
        d = dict(btab=np.ascontiguousarray(np.concatenate([tabs[2 * j], tabs[2 * j + 1]], axis=1)))
        gq, gk = prm["na_q_g"], prm["na_k_g"]
    else:
        kvh = j // 2
        a64 = np.tile(np.arange(64), 2)
        qc, kc, vc = 3232 + j * 128 + a, 3744 + kvh * 64 + a64, 3872 + kvh * 64 + a64
        d = dict(btab=prm["wa_tab"], cosT=prm["cosT"], sinT=prm["sinT"], prot=prm["prot"],
                 sink=np.ascontiguousarray(np.broadcast_to(prm["wa_sink"][2 * j:2 * j + 2][None, :], (128, 2))))
        gq, gk = prm["wa_q_g"], prm["wa_k_g"]
    d.update(qT=fm(qc), kT=fm(kc), vtm=np.ascontiguousarray(fm(vc).T),
             gq=np.ascontiguousarray(np.tile(gq, 2)[:, None]), gk=np.ascontiguousarray(np.tile(gk, 2)[:, None]))
    return d


def attn_consts():
    cos, sin = rope_tables()
    return dict(wa_tab=wa_bias_table(), cosT=np.ascontiguousarray(np.tile(cos.T, (2, 1))),
                sinT=np.ascontiguousarray(np.tile(sin.T, (2, 1))), prot=rot_matrix())


I32 = mybir.dt.int32
S5_BLOCKS = [(0, TCTX)] + [(TCTX + 512 * i, 512) for i in range(16)]
TWO_PI = 2.0 * np.pi


def build_s5():
    P = Prog()
    uTd = P.din("uT", [128, TQ])
    bre = P.din("bre", [128, 4, 128])
    bim = P.din("bim", [128, 4, 128])
    cre = P.din("cre", [128, 4, 128])
    cim = P.din("cim", [128, 4, 128])
    pare = P.din("are", [128, 8])
    paim = P.din("aim", [128, 8])
    pldt = P.din("ldt", [128, 8])
    pdsk = P.din("dsk", [128, 1])
    yo = P.dout("yT", [128, TQ])

    uT = P.sb([128, TQ])
    yacc = P.sb([128, TQ])
    re = [P.sb([128, TQ]) for _ in range(2)]
    nim = P.sb([128, TQ])
    Bre, Bim, Cre, Cim = (P.sb([128, 4, 128]) for _ in range(4))
    small = {}

    def sm(name, shape=(128, 8), dt=F32):
        small[name] = P.sb(list(shape), dt)
        return small[name]
    are, aim, ldt, dsk = sm("are"), sm("aim"), sm("ldt"), sm("dsk", (128, 1))
    NL = 14
    pwr, pwi, npwi = sm("pwr", (128, NL, 8)), sm("pwi", (128, NL, 8)), sm("npwi", (128, NL, 8))
    tmp = [P.sb([128, 512]) for _ in range(2)]
    g1 = [P.sb([128, 512]) for _ in range(2)]
    g2 = [P.sb([128, 512]) for _ in range(2)]
    ob = [P.sb([128, 512]) for _ in range(2)]
    banks = [P.ps([128, 512]) for _ in range(6)]

    P.dma(uT[:], uTd, semkey="u", w=["uT"])
    for t, s, k in ((Bre, bre, "Bre"), (Bim, bim, "Bim"), (Cre, cre, "Cre"), (Cim, cim, "Cim"),
                    (are, pare, "are"), (aim, paim, "aim"), (ldt, pldt, "ldt"), (dsk, pdsk, "dsk")):
        P.dma(t[:], s, semkey=k, w=[k])

    def V(name, fn, r, w):
        P.op("dve", fn, r=r, w=w)

    dt_, dre, mag, ang = sm("dt"), sm("dre"), sm("mag"), sm("ang")
    P.op("act", lambda e: e.activation(out=dt_[:], in_=ldt[:], func=AF.Exp), r=["ldt"], w=["dt"])
    V("", lambda e: e.tensor_tensor(out=dre[:], in0=dt_[:], in1=are[:], op=ALU.mult), ["dt", "are"], ["dre"])
    P.op("act", lambda e: e.activation(out=mag[:], in_=dre[:], func=AF.Exp), r=["dre"], w=["mag"])
    V("", lambda e: e.tensor_tensor(out=ang[:], in0=dt_[:], in1=aim[:], op=ALU.mult), ["dt", "aim"], ["ang"])

    def sin_of(dst, shift, nm):
        z, ki, kf, r_, m_ = sm(nm + "z"), sm(nm + "ki", dt=I32), sm(nm + "kf"), sm(nm + "r"), sm(nm + "m")
        V("", lambda e: e.tensor_scalar(out=z[:], in0=ang[:], scalar1=1.0 / TWO_PI, scalar2=shift / TWO_PI + 0.5, op0=ALU.mult, op1=ALU.add),
          ["ang"], [nm + "z"])
        V("", lambda e: e.tensor_copy(out=ki[:], in_=z[:]), [nm + "z"], [nm + "ki"])
        V("", lambda e: e.tensor_copy(out=kf[:], in_=ki[:]), [nm + "ki"], [nm + "kf"])
        V("", lambda e: e.scalar_tensor_tensor(out=r_[:], in0=kf[:], scalar=-TWO_PI, in1=ang[:], op0=ALU.mult, op1=ALU.add),
          [nm + "kf", "ang"], [nm + "r"])
        if shift:
            V("", lambda e: e.tensor_scalar(out=r_[:], in0=r_[:], scalar1=shift, scalar2=None, op0=ALU.add), [nm + "r"], [nm + "r"])
        V("", lambda e: e.tensor_scalar(out=m_[:], in0=r_[:], scalar1=-np.pi, scalar2=TWO_PI, op0=ALU.is_lt, op1=ALU.mult), [nm + "r"], [nm + "m"])
        V("", lambda e: e.tensor_tensor(out=r_[:], in0=r_[:], in1=m_[:], op=ALU.add), [nm + "r", nm + "m"], [nm + "r"])
        V("", lambda e: e.tensor_scalar(out=m_[:], in0=r_[:], scalar1=np.pi, scalar2=-TWO_PI, op0=ALU.is_gt, op1=ALU.mult), [nm + "r"], [nm + "m"])
        V("", lambda e: e.tensor_tensor(out=r_[:], in0=r_[:], in1=m_[:], op=ALU.add), [nm + "r", nm + "m"], [nm + "r"])
        P.op("act", lambda e: e.activation(out=dst[:], in_=r_[:], func=AF.Sin), r=[nm + "r"], w=[nm + "s"])
    sn_, cs_ = sm("sn"), sm("cs")
    sin_of(sn_, 0.0, "sn")
    sin_of(cs_, np.pi / 2, "cs")
    V("", lambda e: e.tensor_tensor(out=pwr[:, 0, :], in0=mag[:], in1=cs_[:], op=ALU.mult), ["mag", "css"], ["pwr"])
    V("", lambda e: e.tensor_tensor(out=pwi[:, 0, :], in0=mag[:], in1=sn_[:], op=ALU.mult), ["mag", "sns"], ["pwi"])
    den, nr, t1, t2, cfr, cfi, ncfr, ncfi = (sm(n) for n in ("den", "nr", "t1", "t2", "cfr", "cfi", "ncfr", "ncfi"))
    V("", lambda e: e.tensor_tensor(out=den[:], in0=are[:], in1=are[:], op=ALU.mult), ["are"], ["den"])
    V("", lambda e: e.tensor_tensor(out=t1[:], in0=aim[:], in1=aim[:], op=ALU.mult), ["aim"], ["t1"])
    V("", lambda e: e.tensor_tensor(out=den[:], in0=den[:], in1=t1[:], op=ALU.add), ["den", "t1"], ["den"])
    V("", lambda e: e.reciprocal(out=den[:], in_=den[:]), ["den"], ["den"])
    V("", lambda e: e.tensor_scalar(out=nr[:], in0=pwr[:, 0, :], scalar1=-1.0, scalar2=None, op0=ALU.add), ["pwr"], ["nr"])
    V("", lambda e: e.tensor_tensor(out=t1[:], in0=nr[:], in1=are[:], op=ALU.mult), ["nr", "are", "den"], ["t1"])
    V("", lambda e: e.tensor_tensor(out=t2[:], in0=pwi[:, 0, :], in1=aim[:], op=ALU.mult), ["pwi", "aim"], ["t2"])
    V("", lambda e: e.tensor_tensor(out=t1[:], in0=t1[:], in1=t2[:], op=ALU.add), ["t1", "t2"], ["t1"])
    V("", lambda e: e.tensor_tensor(out=cfr[:], in0=t1[:], in1=den[:], op=ALU.mult), ["t1", "den"], ["cfr"])
    V("", lambda e: e.tensor_tensor(out=t1[:], in0=pwi[:, 0, :], in1=are[:], op=ALU.mult), ["pwi", "are", "cfr"], ["t1"])
    V("", lambda e: e.tensor_tensor(out=t2[:], in0=nr[:], in1=aim[:], op=ALU.mult), ["nr", "aim", "cfr"], ["t2"])
    V("", lambda e: e.tensor_tensor(out=t1[:], in0=t1[:], in1=t2[:], op=ALU.subtract), ["t1", "t2"], ["t1"])
    V("", lambda e: e.tensor_tensor(out=cfi[:], in0=t1[:], in1=den[:], op=ALU.mult), ["t1", "den"], ["cfi"])
    V("", lambda e: e.tensor_scalar(out=ncfr[:], in0=cfr[:], scalar1=-1.0, scalar2=None, op0=ALU.mult), ["cfr"], ["ncfr"])
    V("", lambda e: e.tensor_scalar(out=ncfi[:], in0=cfi[:], scalar1=-1.0, scalar2=None, op0=ALU.mult), ["cfi"], ["ncfi"])
    for l in range(1, NL):
        V("", lambda e, l=l: e.tensor_tensor(out=t1[:], in0=pwr[:, l - 1, :], in1=pwr[:, l - 1, :], op=ALU.mult), ["pwr", "cfi", "ncfi"], ["t1"])
        V("", lambda e, l=l: e.tensor_tensor(out=t2[:], in0=pwi[:, l - 1, :], in1=pwi[:, l - 1, :], op=ALU.mult), ["pwi", "cfi", "ncfi"], ["t2"])
        V("", lambda e, l=l: e.tensor_tensor(out=pwi[:, l, :], in0=pwr[:, l - 1, :], in1=pwi[:, l - 1, :], op=ALU.mult), ["pwr", "pwi"], ["pwi"])
        V("", lambda e, l=l: e.tensor_tensor(out=pwr[:, l, :], in0=t1[:], in1=t2[:], op=ALU.subtract), ["t1", "t2"], ["pwr"])
        V("", lambda e, l=l: e.tensor_scalar(out=pwi[:, l, :], in0=pwi[:, l, :], scalar1=2.0, scalar2=None, op0=ALU.mult), ["pwi"], ["pwi"])
    V("", lambda e: e.tensor_scalar(out=npwi[:].rearrange("p l c -> p (l c)"), in0=pwi[:].rearrange("p l c -> p (l c)"), scalar1=-1.0, scalar2=None, op0=ALU.mult),
      ["pwi"], ["npwi"])

    V("", lambda e: e.tensor_scalar(out=yacc[:], in0=uT[:], scalar1=dsk[:, 0:1], scalar2=None, op0=ALU.mult), ["uT", "dsk"], ["yacc"])

    cnt = 0
    for d in range(2):
        for c in range(4):
            dc = d * 4 + c
            cur = 0
            for (t0, nt) in S5_BLOCKS:
                if d == 0:
                    o0 = t0
                else:
                    o0 = TLAT if t0 == 0 else t0 - TCTX
                p1, p2 = banks[(cnt % 2) * 2], banks[(cnt % 2) * 2 + 1]
                k1, k2 = ("bank", (cnt % 2) * 2), ("bank", (cnt % 2) * 2 + 1)
                tb = tmp[cnt % 2]
                cnt += 1
                P.op("pe", lambda e, p1=p1, c=c, t0=t0, nt=nt: e.matmul(p1[:, 0:nt], lhsT=Bre[:, c, :], rhs=uT[:, t0:t0 + nt], start=True, stop=True),
                     r=["Bre", "uT"], w=[k1])
                P.op("pe", lambda e, p2=p2, c=c, t0=t0, nt=nt: e.matmul(p2[:, 0:nt], lhsT=Bim[:, c, :], rhs=uT[:, t0:t0 + nt], start=True, stop=True),
                     r=["Bim", "uT"], w=[k2])
                V("", lambda e, tb=tb, p2=p2, nt=nt, dc=dc: e.tensor_scalar(out=tb[:, 0:nt], in0=p2[:, 0:nt], scalar1=cfi[:, dc:dc + 1], scalar2=None, op0=ALU.mult),
                  [k2, "cfi"], [("tmp", id(tb))])
                V("", lambda e, tb=tb, p1=p1, nt=nt, dc=dc, o0=o0: e.scalar_tensor_tensor(
                    out=re[0][:, o0:o0 + nt], in0=p1[:, 0:nt], scalar=cfr[:, dc:dc + 1], in1=tb[:, 0:nt], op0=ALU.mult, op1=ALU.subtract),
                    [k1, "cfr", ("tmp", id(tb))], ["re0"])
                V("", lambda e, tb=tb, p1=p1, nt=nt, dc=dc: e.tensor_scalar(out=tb[:, 0:nt], in0=p1[:, 0:nt], scalar1=ncfi[:, dc:dc + 1], scalar2=None, op0=ALU.mult),
                  [k1, "ncfi"], [("tmp", id(tb))])
                V("", lambda e, tb=tb, p2=p2, nt=nt, dc=dc, o0=o0: e.scalar_tensor_tensor(
                    out=nim[:, o0:o0 + nt], in0=p2[:, 0:nt], scalar=ncfr[:, dc:dc + 1], in1=tb[:, 0:nt], op0=ALU.mult, op1=ALU.add),
                    [k2, "ncfr", ("tmp", id(tb))], ["nim"])
            for l in range(NL):
                s = 1 << l
                n = TQ - s
                ro, rn = re[cur], re[1 - cur]
                kro, krn = "re%d" % cur, "re%d" % (1 - cur)
                ar, ai, nai = pwr[:, l, dc:dc + 1], pwi[:, l, dc:dc + 1], npwi[:, l, dc:dc + 1]
                if d == 0:
                    dst, src, keep = slice(s, TQ), slice(0, n), slice(0, s)
                else:
                    dst, src, keep = slice(0, n), slice(s, TQ), slice(n, TQ)
                V("", lambda e, ro=ro, rn=rn, ar=ar, dst=dst, src=src: e.scalar_tensor_tensor(
                    out=rn[:, dst], in0=ro[:, src], scalar=ar, in1=ro[:, dst], op0=ALU.mult, op1=ALU.add), [kro, "pwr"], [krn])
                V("", lambda e, rn=rn, ai=ai, dst=dst, src=src: e.scalar_tensor_tensor(
                    out=rn[:, dst], in0=nim[:, src], scalar=ai, in1=rn[:, dst], op0=ALU.mult, op1=ALU.add), [krn, "nim", "pwi"], [krn])
                P.op("pool", lambda e, ro=ro, rn=rn, keep=keep: e.tensor_copy(out=rn[:, keep], in_=ro[:, keep]), r=[kro], w=[krn])
                if d == 0:
                    dsr, ssr = slice(TQ - 1, s - 1, -1), (slice(n - 1, None, -1))
                else:
                    dsr, ssr = dst, src
                V("", lambda e, ar=ar, dsr=dsr, ssr=ssr: e.scalar_tensor_tensor(
                    out=nim[:, dsr], in0=nim[:, ssr], scalar=ar, in1=nim[:, dsr], op0=ALU.mult, op1=ALU.add), ["nim", "pwr"], ["nim"])
                V("", lambda e, ro=ro, nai=nai, dst=dst, src=src: e.scalar_tensor_tensor(
                    out=nim[:, dst], in0=ro[:, src], scalar=nai, in1=nim[:, dst], op0=ALU.mult, op1=ALU.add), ["nim", kro, "npwi"], ["nim"])
                cur = 1 - cur
            xr, kxr = re[cur], "re%d" % cur
            for (t0, nt) in S5_BLOCKS:
                if d == 0:
                    o0 = t0
                else:
                    o0 = TLAT if t0 == 0 else t0 - TCTX
                pb, kb = banks[4 + cnt % 2], ("bank", 4 + cnt % 2)
                cnt += 1
                P.op("pe", lambda e, pb=pb, c=c, o0=o0, nt=nt, xr=xr: e.matmul(pb[:, 0:nt], lhsT=Cre[:, c, :], rhs=xr[:, o0:o0 + nt], start=True, stop=False),
                     r=["Cre", kxr], w=[kb])
                P.op("pe", lambda e, pb=pb, c=c, o0=o0, nt=nt: e.matmul(pb[:, 0:nt], lhsT=Cim[:, c, :], rhs=nim[:, o0:o0 + nt], start=False, stop=True),
                     r=["Cim", "nim"], w=[kb])
                V("", lambda e, pb=pb, t0=t0, nt=nt: e.tensor_tensor(out=yacc[:, t0:t0 + nt], in0=pb[:, 0:nt], in1=yacc[:, t0:t0 + nt], op=ALU.add),
                  [kb, "yacc"], ["yacc"])
            if cur != 0:
                re[0], re[1] = re[1], re[0]

    for n_, (t0, nt) in enumerate(S5_BLOCKS):
        a, b_, o = g1[n_ % 2], g2[n_ % 2], ob[n_ % 2]
        ka, kb2, ko = ("g1", n_ % 2), ("g2", n_ % 2), ("ob", n_ % 2)
        ys = yacc[:, t0:t0 + nt]
        P.op("pool", lambda e, a=a, ys=ys, nt=nt: e.tensor_tensor(out=a[:, 0:nt], in0=ys, in1=ys, op=ALU.mult), r=["yacc"], w=[ka])
        P.op("pool", lambda e, a=a, nt=nt: e.tensor_scalar(out=a[:, 0:nt], in0=a[:, 0:nt], scalar1=0.044715, scalar2=1.0, op0=ALU.mult, op1=ALU.add), r=[ka], w=[ka])
        P.op("pool", lambda e, a=a, ys=ys, nt=nt: e.tensor_tensor(out=a[:, 0:nt], in0=a[:, 0:nt], in1=ys, op=ALU.mult), r=[ka, "yacc"], w=[ka])
        P.op("act", lambda e, a=a, b_=b_, nt=nt: e.activation(out=b_[:, 0:nt], in_=a[:, 0:nt], func=AF.Sigmoid, scale=1.5957691216057308), r=[ka], w=[kb2])
        V("", lambda e, b_=b_, o=o, ys=ys, nt=nt: e.tensor_tensor(out=o[:, 0:nt], in0=b_[:, 0:nt], in1=ys, op=ALU.mult), [kb2, "yacc"], [ko])
        P.dma(yo[:, t0:t0 + nt], o[:, 0:nt], semkey=ko, r=[ko], is_out=True)
    return P.finish()


def s5_inputs(ulat, uctx, j, prm):
    ch = slice(j * 128, (j + 1) * 128)
    G0 = 8 * j
    bre = np.zeros((128, 4, 128), np.float32)
    bim = np.zeros((128, 4, 128), np.float32)
    cre = np.zeros((128, 4, 128), np.float32)
    cim = np.zeros((128, 4, 128), np.float32)
    for c in range(4):
        for g2 in range(2):
            g = 2 * c + g2
            bre[g * 16:(g + 1) * 16, c, g2 * 64:(g2 + 1) * 64] = prm["s5_b_re"][G0 + g].T
            bim[g * 16:(g + 1) * 16, c, g2 * 64:(g2 + 1) * 64] = prm["s5_b_im"][G0 + g].T
            cre[g2 * 64:(g2 + 1) * 64, c, g * 16:(g + 1) * 16] = prm["s5_c_re"][G0 + g].T
            cim[g2 * 64:(g2 + 1) * 64, c, g * 16:(g + 1) * 16] = prm["s5_c_im"][G0 + g].T

    def st(a):
        a = a[:, G0:G0 + 8].reshape(2, 4, 2, 64)
        return np.ascontiguousarray(a.transpose(2, 3, 0, 1).reshape(128, 8))
    ldt = np.broadcast_to(prm["s5_log_dt"][:, :, None], (2, 32, 64))
    return dict(uT=np.ascontiguousarray(np.concatenate([uctx[ch], ulat[ch]], axis=1)), bre=bre, bim=bim, cre=cre, cim=cim,
                are=st(prm["s5_a_re"]), aim=st(prm["s5_a_im"]), ldt=st(ldt),
                dsk=np.ascontiguousarray(prm["s5_d"][ch][:, None]))


RW_T = TCTX + TLAT
RW_NCH = RW_T // 64
RW_BLOCKS = [(0, TCTX, 0)] + [(TCTX + 512 * i, 512, TCTX + 2 + 512 * i) for i in range(16)]
RW_PADT = TCTX + 2 + TLAT + 2
NEG_SQRT_E = -0.6065306597126334


def rw_consts():
    i = np.arange(128) % 64
    jj = np.arange(128) % 64
    strict = (np.arange(128) < 64)[None, :]
    I, J = i[:, None], jj[None, :]
    mf = np.where(strict, I < J, I <= J)
    mb = np.where(strict, I > J, I >= J)
    mask = np.stack([mf, mb], axis=1).astype(np.float32)
    a = np.arange(64)
    nmask = np.stack([a[:, None] > a[None, :], a[:, None] < a[None, :]], axis=1).astype(np.float32)
    m0 = np.ones((128, 512), np.float32)
    m0[:, ::64] = 0.0
    bones = np.zeros((128, 128), np.float32)
    bones[:64, :64] = 1.0
    bones[64:, 64:] = 1.0
    hsel = np.zeros((128, 2), np.float32)
    hsel[:64, 0] = 1.0
    hsel[64:, 1] = 1.0
    ib = np.concatenate([np.eye(64, dtype=np.float32)] * 2, axis=0)
    return dict(mask=mask, nmask=nmask, m0=m0, bones=bones, hsel=hsel, ib=ib, ident=np.eye(128, dtype=np.float32))


def build_rwkv(nblk=17, dbg=99):
    P = Prog()
    din = P.din
    rP, kP, vP = din("rP", [128, RW_PADT]), din("kP", [128, RW_PADT]), din("vP", [128, RW_PADT])
    dwT, daT, dgT = din("dwT", [32, RW_T]), din("daT", [32, RW_T]), din("dgT", [96, RW_T])
    cwD, wupD, aupD, gupD = din("cw", [128, 9]), din("wup", [32, 2, 128]), din("aup", [32, 2, 128]), din("gup", [96, 128])
    w0D, a0D, kkwD, kaD, rkD = din("w0", [128, 2]), din("a0", [128, 2]), din("kkw", [128, 1]), din("ka", [128, 1]), din("rk", [128, 1])
    gnwD, gnbD = din("gnw", [64, 128]), din("gnb", [64, 128])
    maskD, nmaskD, m0D, bonesD, hselD, ibD, identD = (din("mask", [128, 2, 128]), din("nmask", [64, 2, 64]), din("m0", [128, 512]),
                                                      din("bones", [128, 128]), din("hsel", [128, 2]), din("ib", [128, 64]), din("ident", [128, 128]))
    o_tm = P.dout("o", [RW_T, 128])

    def load(shape, src, key):
        t = P.sb(shape)
        P.dma(t[:], src, semkey=key, w=[key])
        return t
    cw, wup, aup, gup = load([128, 9], cwD, "cw"), load([32, 2, 128], wupD, "wup"), load([32, 2, 128], aupD, "aup"), load([96, 128], gupD, "gup")
    w0, a0, kkw, ka, rk = load([128, 2], w0D, "w0"), load([128, 2], a0D, "a0"), load([128, 1], kkwD, "kkw"), load([128, 1], kaD, "ka"), load([128, 1], rkD, "rk")
    gnw, gnb = load([64, 128], gnwD, "gnw"), load([64, 128], gnbD, "gnb")
    MASK, NMASK, mask0 = load([128, 2, 128], maskD, "mask"), load([64, 2, 64], nmaskD, "nmask"), load([128, 512], m0D, "m0")
    bones, HSEL, IB, ident = load([128, 128], bonesD, "bones"), load([128, 2], hselD, "hsel"), load([128, 64], ibD, "ib"), load([128, 128], identD, "ident")

    bk = [P.ps([128, 512]) for _ in range(8)]

    def B(i):
        return ("bank", i)
    tiles = {}

    def T(name, shape=(128, 512)):
        if name not in tiles:
            tiles[name] = P.sb(list(shape))
        return tiles[name]
    R1 = [P.sb([128, 8, 128]) for _ in range(2)]
    L1 = [P.sb([128, 8, 128]) for _ in range(2)]
    Z1 = [P.sb([128, 8, 128]) for _ in range(2)]
    Z2 = [P.sb([128, 8, 128]) for _ in range(2)]
    WL = [P.sb([128, 8]) for _ in range(2)]
    RKK = [P.sb([128, 512]) for _ in range(2)]
    SG = [P.sb([96, 512]) for _ in range(2)]
    XV = [P.sb([128, 2, 128]) for _ in range(2)]
    BK = [P.sb([128, 2, 64]) for _ in range(2)]
    ATs = [P.sb([128, 2, 128]) for _ in range(2)]
    PP0s = [P.sb([64, 2, 2, 64]) for _ in range(2)]
    PPs = [[P.sb([64, 2, 2, 64]) for _ in range(2)] for _ in range(2)]
    MTss = [P.sb([128, 64]) for _ in range(2)]
    RTss = [P.sb([128, 64]) for _ in range(2)]
    S = [P.sb([128, 64]) for _ in range(2)]
    Yf = P.sb([64, RW_NCH, 128])
    P.op("pool", lambda e: e.memset(S[0][:], 0.0), w=[("S", 0)])

    def v3(ap, nch):
        return ap[:, 0:nch * 64].rearrange("p (c j) -> p c j", j=64)

    def prep(bi, d, par):
        s0, nt, p0 = RW_BLOCKS[bi]
        nch = nt // 64
        raws = {}
        for nm, src in (("r", rP), ("k", kP), ("v", vP)):
            t = T("raw" + nm, (128, 514))
            P.dma(t[:, 0:nt + 2], src[:, p0 - 1 + 0:p0 - 1 + nt + 2] if False else src[:, p0:p0 + nt + 2], semkey="raw" + nm, w=["raw" + nm])
            raws[nm] = t
        dw, da = T("dw", (32, 512)), T("da", (32, 512))
        P.dma(dw[:, 0:nt], dwT[:, s0:s0 + nt], semkey="dw", w=["dw"])
        P.dma(da[:, 0:nt], daT[:, s0:s0 + nt], semkey="da", w=["da"])
        outs = {"r": T("rc"), "k": T("kc"), "v": T("vc")}
        for ai, nm in enumerate(("r", "k", "v")):
            rw, o = raws[nm], outs[nm]
            P.op("dve", lambda e, rw=rw, o=o, ai=ai: e.tensor_scalar(out=o[:, 0:nt], in0=rw[:, 0:nt], scalar1=cw[:, 3 * ai:3 * ai + 1], scalar2=None, op0=ALU.mult),
                 r=["raw" + nm, "cw"], w=[nm + "c"])
            for tap in (1, 2):
                P.op("dve", lambda e, rw=rw, o=o, ai=ai, tap=tap: e.scalar_tensor_tensor(
                    out=o[:, 0:nt], in0=rw[:, tap:tap + nt], scalar=cw[:, 3 * ai + tap:3 * ai + tap + 1], in1=o[:, 0:nt], op0=ALU.mult, op1=ALU.add),
                    r=["raw" + nm, "cw", nm + "c"], w=[nm + "c"])
        rc, kc, vc = outs["r"], outs["k"], outs["v"]
        P.op("pool", lambda e: e.tensor_copy(out=Z1[par][:, 0:nch, 64:128], in_=v3(vc, nch)), r=["vc"], w=[("Z1", par)])
        th = T("th", (32, 512))
        P.op("act", lambda e: e.activation(out=th[:, 0:nt], in_=dw[:, 0:nt], func=AF.Tanh), r=["dw"], w=["th"])
        dirs = [0] if d == 0 else [0, 1]
        aa, kt = {}, {}
        for dd in dirs:
            aa[dd] = T("a%d" % dd)
            P.op("pe", lambda e, dd=dd: e.matmul(bk[0][:, 0:nt], lhsT=aup[:, dd, :], rhs=da[:, 0:nt], start=True, stop=True), r=["aup", "da"], w=[B(0)])
            P.op("act", lambda e, dd=dd: e.activation(out=aa[dd][:, 0:nt], in_=bk[0][:, 0:nt], func=AF.Sigmoid, bias=a0[:, dd:dd + 1]),
                 r=[B(0), "a0"], w=["a%d" % dd])
        sw, lw = T("sw"), T("lw")
        P.op("pe", lambda e: e.matmul(bk[0][:, 0:nt], lhsT=wup[:, d, :], rhs=th[:, 0:nt], start=True, stop=True), r=["wup", "th"], w=[B(0)])
        P.op("act", lambda e: e.activation(out=sw[:, 0:nt], in_=bk[0][:, 0:nt], func=AF.Sigmoid, bias=w0[:, d:d + 1]), r=[B(0), "w0"], w=["sw"])
        P.op("dve", lambda e: e.tensor_scalar(out=lw[:, 0:nt], in0=sw[:, 0:nt], scalar1=NEG_SQRT_E, scalar2=None, op0=ALU.mult), r=["sw"], w=["lw"])
        kkr, sq, nr, kk = T("kkr"), T("sq"), T("nr"), T("kk")
        P.op("dve", lambda e: e.tensor_scalar(out=kkr[:, 0:nt], in0=kc[:, 0:nt], scalar1=kkw[:, 0:1], scalar2=None, op0=ALU.mult), r=["kc", "kkw"], w=["kkr"])
        P.op("pool", lambda e: e.tensor_tensor(out=sq[:, 0:nt], in0=kkr[:, 0:nt], in1=kkr[:, 0:nt], op=ALU.mult), r=["kkr"], w=["sq"])
        P.op("pe", lambda e: e.matmul(bk[0][:, 0:nt], lhsT=bones[:], rhs=sq[:, 0:nt], start=True, stop=True), r=["bones", "sq"], w=[B(0)])
        P.op("act", lambda e: e.activation(out=nr[:, 0:nt], in_=bk[0][:, 0:nt], func=AF.Sqrt), r=[B(0)], w=["nr"])
        P.op("dve", lambda e: e.tensor_scalar(out=nr[:, 0:nt], in0=nr[:, 0:nt], scalar1=1e-12, scalar2=None, op0=ALU.max), r=["nr"], w=["nr"])
        P.op("dve", lambda e: e.reciprocal(out=nr[:, 0:nt], in_=nr[:, 0:nt]), r=["nr"], w=["nr"])
        P.op("dve", lambda e: e.tensor_tensor(out=kk[:, 0:nt], in0=kkr[:, 0:nt], in1=nr[:, 0:nt], op=ALU.mult), r=["kkr", "nr"], w=["kk"])
        for dd in dirs:
            kt[dd] = T("kt%d" % dd)
            tk = T("tk")
            P.op("pool", lambda e, dd=dd: e.tensor_scalar(out=tk[:, 0:nt], in0=aa[dd][:, 0:nt], scalar1=-1.0, scalar2=ka[:, 0:1], op0=ALU.add, op1=ALU.mult),
                 r=["a%d" % dd, "ka"], w=["tk"])
            P.op("dve", lambda e, dd=dd: e.scalar_tensor_tensor(out=kt[dd][:, 0:nt], in0=tk[:, 0:nt], scalar=1.0, in1=kc[:, 0:nt], op0=ALU.add, op1=ALU.mult),
                 r=["tk", "kc"], w=["kt%d" % dd])
        beta = T("beta")
        P.op("pool", lambda e: e.tensor_tensor(out=beta[:, 0:nt], in0=kk[:, 0:nt], in1=aa[d][:, 0:nt], op=ALU.mult), r=["kk", "a%d" % d], w=["beta"])
        lWf, lW, lWex, dl = T("lWf"), T("lW"), T("lWex"), T("dl")
        P.op("dve", lambda e: e.tensor_tensor_scan(out=lWf[:, 0:nt], data0=mask0[:, 0:nt], data1=lw[:, 0:nt], initial=0.0, op0=ALU.mult, op1=ALU.add),
             r=["m0", "lw"], w=["lWf"])
        tot = v3(lWf, nch)[:, :, 63:64]
        totb = tot.broadcast_to([128, nch, 64])
        if d == 0:
            lW = lWf
            klW = "lWf"
        else:
            klW = "lW"
            P.op("dve", lambda e: e.tensor_tensor(out=v3(lW, nch), in0=totb, in1=v3(lWf, nch), op=ALU.subtract), r=["lWf"], w=["lW"])
            P.op("dve", lambda e: e.tensor_tensor(out=lW[:, 0:nt], in0=lW[:, 0:nt], in1=lw[:, 0:nt], op=ALU.add), r=["lW", "lw"], w=["lW"])
        P.op("pool", lambda e: e.tensor_tensor(out=lWex[:, 0:nt], in0=lW[:, 0:nt], in1=lw[:, 0:nt], op=ALU.subtract), r=[klW, "lw"], w=["lWex"])
        P.op("dve", lambda e: e.tensor_tensor(out=v3(dl, nch), in0=totb, in1=v3(lW, nch), op=ALU.subtract), r=["lWf", klW], w=["dl"])
        E1, E2, E3, E4 = T("E1"), T("E2"), T("E3"), T("E4")
        P.op("act", lambda e: e.activation(out=E1[:, 0:nt], in_=lWex[:, 0:nt], func=AF.Exp), r=["lWex"], w=["E1"])
        P.op("act", lambda e: e.activation(out=E2[:, 0:nt], in_=lW[:, 0:nt], func=AF.Exp), r=[klW], w=["E2"])
        P.op("act", lambda e: e.activation(out=E3[:, 0:nt], in_=lW[:, 0:nt], func=AF.Exp, scale=-1.0), r=[klW], w=["E3"])
        P.op("act", lambda e: e.activation(out=E4[:, 0:nt], in_=dl[:, 0:nt], func=AF.Exp), r=["dl"], w=["E4"])
        P.op("act", lambda e: e.activation(out=WL[par][:, 0:nch].rearrange("p (c o) -> p c o", o=1), in_=tot, func=AF.Exp), r=["lWf"], w=[("WL", par)])
        P.op("dve", lambda e: e.scalar_tensor_tensor(out=R1[par][:, 0:nch, 0:64], in0=v3(kk, nch), scalar=-1.0, in1=v3(E1, nch), op0=ALU.mult, op1=ALU.mult),
             r=["kk", "E1"], w=[("R1", par)])
        P.op("pool", lambda e: e.tensor_copy(out=Z1[par][:, 0:nch, 0:64], in_=R1[par][:, 0:nch, 0:64]), r=[("R1", par)], w=[("Z1", par)])
        P.op("dve", lambda e: e.tensor_tensor(out=R1[par][:, 0:nch, 64:128], in0=v3(rc, nch), in1=v3(E2, nch), op=ALU.mult), r=["rc", "E2"], w=[("R1", par)])
        P.op("pool", lambda e: e.tensor_tensor(out=L1[par][:, 0:nch, 0:64], in0=v3(beta, nch), in1=v3(E3, nch), op=ALU.mult), r=["beta", "E3"], w=[("L1", par)])
        P.op("dve", lambda e: e.tensor_tensor(out=L1[par][:, 0:nch, 64:128], in0=v3(kt[d], nch), in1=v3(E3, nch), op=ALU.mult), r=["kt%d" % d, "E3"], w=[("L1", par)])
        P.op("pool", lambda e: e.tensor_tensor(out=Z2[par][:, 0:nch, 0:64], in0=v3(beta, nch), in1=v3(E4, nch), op=ALU.mult), r=["beta", "E4"], w=[("Z2", par)])
        P.op("dve", lambda e: e.tensor_tensor(out=Z2[par][:, 0:nch, 64:128], in0=v3(kt[d], nch), in1=v3(E4, nch), op=ALU.mult), r=["kt%d" % d, "E4"], w=[("Z2", par)])
        if d == 1:
            dg = T("dg", (96, 512))
            P.dma(dg[:, 0:nt], dgT[:, s0:s0 + nt], semkey="dg", w=["dg"])
            P.op("act", lambda e: e.activation(out=SG[par][:, 0:nt], in_=dg[:, 0:nt], func=AF.Sigmoid), r=["dg"], w=[("SG", par)])
            ks = T("ks")
            P.op("pool", lambda e: e.tensor_tensor(out=ks[:, 0:nt], in0=kt[0][:, 0:nt], in1=kt[1][:, 0:nt], op=ALU.add), r=["kt0", "kt1"], w=["ks"])
            P.op("dve", lambda e: e.scalar_tensor_tensor(out=RKK[par][:, 0:nt], in0=rc[:, 0:nt], scalar=rk[:, 0:1], in1=ks[:, 0:nt], op0=ALU.mult, op1=ALU.mult),
                 r=["rc", "rk", "ks"], w=[("RKK", par)])

    state = {"cur": 0, "q": 0}

    def mm(out, lhsT, rhs, r, w, rows, start=True, stop=True):
        P.op("pe", lambda e: e.matmul(out, lhsT=lhsT, rhs=rhs, start=start, stop=stop), r=r, w=w)

    def chunk(bi, c, d, par):
        s0, nt, _ = RW_BLOCKS[bi]
        cg = s0 // 64 + c
        q = state["q"]
        state["q"] = 1 - q
        xv, bkq, ats = XV[q], BK[q], ATs[q]
        kxv, kbk, kat = ("XV", q), ("BK", q), ("ATs", q)
        PP0, PP, MTs, RTs = PP0s[q], PPs[q], MTss[q], RTss[q]
        kPP0, kMT, kRT = ("PP0", q), ("MTs", q), ("RTs", q)
        r1, l1, z1, z2 = R1[par], L1[par], Z1[par], Z2[par]
        if dbg < 1:
            return
        mm(bk[0][:, 0:128], z1[:, c, :], ident[:], [("Z1", par), "ident"], [B(0)], 128)
        mm(bk[0][:, 128:256], z2[:, c, :], ident[:], [("Z2", par), "ident"], [B(0)], 128)
        if d == 1:
            mm(bk[0][0:64, 256:384], z1[:, c, 64:128], ident[:], [("Z1", par), "ident"], [B(0)], 128)
        P.op("act", lambda e: e.activation(out=xv[0:64, :, 64:128], in_=bk[0][0:64, 0:128].rearrange("p (h k) -> p h k", h=2), func=AF.Copy),
             r=[B(0)], w=[kxv])
        P.op("act", lambda e: e.activation(out=xv[64:128, :, 0:64], in_=bk[0][64:128, 0:128].rearrange("p (h k) -> p h k", h=2), func=AF.Copy),
             r=[B(0)], w=[kxv])
        P.op("dve", lambda e: e.tensor_copy(out=bkq[:, :, :], in_=bk[0][:, 128:256].rearrange("p (h k) -> p h k", h=2)), r=[B(0)], w=[kbk])
        if d == 1:
            vtm = T("vtm%d" % q, (64, 128))
            P.op("act", lambda e: e.activation(out=vtm[:, :], in_=bk[0][0:64, 256:384], func=AF.Copy), r=[B(0)], w=[("vtm", q)])
        yield
        if dbg < 2:
            return
        for h in range(2):
            hs = slice(64 * h, 64 * h + 64)
            mm(bk[1 + h][:, 0:128], l1[hs, c, :], r1[hs, c, :], [("L1", par), ("R1", par)], [B(1 + h)], 64)
            mm(bk[1 + h][0:64, 128:192], r1[hs, c, 0:64], l1[hs, c, 0:64], [("L1", par), ("R1", par)], [B(1 + h)], 64)
        for h in range(2):
            P.op("dve", lambda e, h=h: e.tensor_tensor(out=ats[:, h, :], in0=bk[1 + h][:, 0:128], in1=MASK[:, d, :], op=ALU.mult), r=[B(1 + h), "mask"], w=[kat])
            P.op("dve", lambda e, h=h: e.tensor_tensor(out=PP0[:, h, 0, :], in0=bk[1 + h][0:64, 128:192], in1=NMASK[:, d, :], op=ALU.mult),
                 r=[B(1 + h), "nmask"], w=[kPP0])
        yield
        if dbg < 3:
            return
        for h in range(2):
            mm(bk[4][0:64, 64 * h:64 * h + 64], ats[64:128, h, 0:64], xv[64:128, h, 0:64], [kat, kxv], [B(4)], 64)
        P.op("act", lambda e: e.activation(out=xv[0:64, :, 0:64], in_=bk[4][0:64, 0:128].rearrange("p (h k) -> p h k", h=2), func=AF.Copy), r=[B(4)], w=[kxv])
        yield
        if dbg < 4:
            return
        ppc, kpp = PP0, kPP0
        for l in range(6):
            for h in range(2):
                pt = ats[0:64, h, 0:64] if l == 0 else ppc[:, h, 1, :]
                mm(bk[3][0:64, 128 * h:128 * h + 128], pt, xv[0:64, h, :], [kat, kpp, kxv], [B(3)], 64)
            if l < 5:
                for h in range(2):
                    pt = ats[0:64, h, 0:64] if l == 0 else ppc[:, h, 1, :]
                    pm = ppc[:, h, 0, :]
                    mm(bk[5][0:64, 128 * h:128 * h + 64], pt, pm, [kat, kpp], [B(5)], 64)
                    mm(bk[5][0:64, 128 * h + 64:128 * h + 128], pm, pt, [kat, kpp], [B(5)], 64)
            P.op("dve", lambda e: e.tensor_tensor(out=xv[0:64, :, :], in0=bk[3][0:64, 0:256].rearrange("p (h k) -> p h k", h=2), in1=xv[0:64, :, :], op=ALU.add),
                 r=[B(3), kxv], w=[kxv])
            if l < 5:
                nx = PP[l % 2]
                P.op("act", lambda e, nx=nx: e.activation(out=nx[:, :, :, :].rearrange("p h t k -> p (h t k)"), in_=bk[5][0:64, 0:256], func=AF.Copy),
                     r=[B(5)], w=[("PP", q, l % 2)])
                ppc, kpp = nx, ("PP", q, l % 2)
            yield
        yield
        if dbg < 5:
            return
        for h in range(2):
            hs = slice(64 * h, 64 * h + 64)
            mm(bk[6][hs, 0:64], xv[0:64, h, 64:128], bkq[0:64, h, :], [kxv, kbk], [B(6)], 64)
            mm(bk[6][hs, 64:128], xv[0:64, h, 64:128], ats[0:64, h, 64:128], [kxv, kat], [B(6)], 64)
        P.op("dve", lambda e: e.scalar_tensor_tensor(out=MTs[:, :], in0=IB[:, :], scalar=WL[par][:, c:c + 1], in1=bk[6][:, 0:64], op0=ALU.mult, op1=ALU.add),
             r=["ib", ("WL", par), B(6)], w=[kMT])
        P.op("dve", lambda e: e.tensor_tensor(out=RTs[:, :], in0=bk[6][:, 64:128], in1=r1[:, c, 64:128], op=ALU.add), r=[B(6), ("R1", par)], w=[kRT])
        if dbg < 6:
            return
        yield
        cur = state["cur"]
        for h in range(2):
            hs = slice(64 * h, 64 * h + 64)
            mm(bk[7][0:64, 64 * h:64 * h + 64], ats[:, h, 64:128], xv[:, h, 0:64], [kat, kxv], [B(7)], 128, True, False)
            mm(bk[7][0:64, 64 * h:64 * h + 64], RTs[hs, :], S[cur][hs, :], [kRT, ("S", cur)], [B(7)], 64, False, True)
        for h in range(2):
            hs = slice(64 * h, 64 * h + 64)
            mm(bk[6][hs, 128:192], bkq[:, h, :], xv[:, h, 0:64], [kbk, kxv], [B(6)], 128, True, False)
            mm(bk[6][hs, 128:192], MTs[hs, :], S[cur][hs, :], [kMT, ("S", cur)], [B(6)], 64, False, True)
        P.op("act", lambda e: e.activation(out=S[1 - cur][:, :], in_=bk[6][:, 128:192], func=AF.Copy), r=[B(6)], w=[("S", 1 - cur)])
        state["cur"] = 1 - cur
        if d == 0:
            P.op("act", lambda e: e.activation(out=Yf[:, cg, :], in_=bk[7][0:64, 0:128], func=AF.Copy), r=[B(7)], w=["Yf"])
            return
        if dbg < 7:
            return
        y, yc, sq, o = T("y%d" % q, (64, 128)), T("yc%d" % q, (64, 128)), T("ysq%d" % q, (64, 128)), T("o%d" % (cg % 2), (64, 128))
        mu, var, rstd = T("mu%d" % q, (64, 2)), T("var%d" % q, (64, 2)), T("rstd%d" % q, (64, 2))
        tl = slice(64 * c, 64 * c + 64)
        P.op("dve", lambda e: e.tensor_tensor(out=y[:, :], in0=bk[7][0:64, 0:128], in1=Yf[:, cg, :], op=ALU.add), r=[B(7), "Yf"], w=[("y", q)])
        yield
        mm(bk[0][0:64, 384:512], SG[par][:, tl], gup[:, :], [("SG", par), "gup"], [B(0)], 96)
        mm(bk[4][0:64, 128:130], RKK[par][:, tl], HSEL[:, :], [("RKK", par), "hsel"], [B(4)], 128)
        for h in range(2):
            hc = slice(64 * h, 64 * h + 64)
            P.op("dve", lambda e, h=h, hc=hc: e.scalar_tensor_tensor(out=y[:, hc], in0=vtm[:, hc], scalar=bk[4][0:64, 128 + h:129 + h], in1=y[:, hc],
                                                                   op0=ALU.mult, op1=ALU.add), r=[("vtm", q), B(4), ("y", q)], w=[("y", q)])
        P.op("dve", lambda e: e.tensor_reduce(out=mu[:, :], in_=y[:, :].rearrange("p (h k) -> p h k", h=2), axis=AX.X, op=ALU.add), r=[("y", q)], w=[("mu", q)])
        P.op("dve", lambda e: e.tensor_scalar(out=mu[:, :], in0=mu[:, :], scalar1=-1.0 / 64, scalar2=None, op0=ALU.mult), r=[("mu", q)], w=[("mu", q)])
        for h in range(2):
            hc = slice(64 * h, 64 * h + 64)
            P.op("dve", lambda e, h=h, hc=hc: e.tensor_scalar(out=yc[:, hc], in0=y[:, hc], scalar1=mu[:, h:h + 1], scalar2=None, op0=ALU.add), r=[("y", q), ("mu", q)], w=[("yc", q)])
        P.op("pool", lambda e: e.tensor_tensor(out=sq[:, :], in0=yc[:, :], in1=yc[:, :], op=ALU.mult), r=[("yc", q)], w=[("ysq", q)])
        P.op("dve", lambda e: e.tensor_reduce(out=var[:, :], in_=sq[:, :].rearrange("p (h k) -> p h k", h=2), axis=AX.X, op=ALU.add), r=[("ysq", q)], w=[("var", q)])
        P.op("act", lambda e: e.activation(out=rstd[:, :], in_=var[:, :], func=AF.Sqrt, scale=1.0 / 64, bias=64e-5), r=[("var", q)], w=[("rstd", q)])
        P.op("dve", lambda e: e.reciprocal(out=rstd[:, :], in_=rstd[:, :]), r=[("rstd", q)], w=[("rstd", q)])
        for h in range(2):
            hc = slice(64 * h, 64 * h + 64)
            P.op("dve", lambda e, h=h, hc=hc: e.scalar_tensor_tensor(out=yc[:, hc], in0=yc[:, hc], scalar=rstd[:, h:h + 1], in1=gnw[:, hc], op0=ALU.mult, op1=ALU.mult),
                 r=[("yc", q), ("rstd", q), "gnw"], w=[("yc", q)])
        P.op("pool", lambda e: e.tensor_tensor(out=yc[:, :], in0=yc[:, :], in1=gnb[:, :], op=ALU.add), r=[("yc", q), "gnb"], w=[("yc", q)])
        P.op("dve", lambda e: e.tensor_tensor(out=o[:, :], in0=bk[0][0:64, 384:512], in1=yc[:, :], op=ALU.mult), r=[B(0), ("yc", q)], w=[("o", cg % 2)])
        P.dma(o_tm[64 * cg:64 * cg + 64, :], o[:, :], semkey=("o", cg % 2), r=[("o", cg % 2)], is_out=True)

    nb = 0
    for d in range(2):
        order = list(range(17)) if d == 0 else [0] + list(range(16, 0, -1))
        for bi in order[:nblk]:
            par = nb % 2
            nb += 1
            prep(bi, d, par)
            nch = RW_BLOCKS[bi][1] // 64
            cl = list(range(nch) if d == 0 else range(nch - 1, -1, -1))
            for i0 in range(0, len(cl), 2):
                gens = [chunk(bi, c, d, par) for c in cl[i0:i0 + 2]]
                while gens:
                    for g in list(gens):
                        try:
                            next(g)
                        except StopIteration:
                            gens.remove(g)
        if d == 0:
            P.op("pool", lambda e: e.memset(S[state["cur"]][:], 0.0), w=[("S", state["cur"])])
    return P.finish()


def rwkv_inputs(plat, pctx, j, prm, cst):
    a = np.arange(128)
    cols = j * 128 + a

    def pad(c0):
        z = np.zeros((128, 1), np.float32)
        return np.ascontiguousarray(np.concatenate([z, pctx[c0 + cols], z, z, plat[c0 + cols], z], axis=1))

    def sfm(c0, n):
        return np.ascontiguousarray(np.concatenate([pctx[c0:c0 + n], plat[c0:c0 + n]], axis=1))
    cw = np.stack([prm["rw_conv"][tap, ai * 512 + cols] for ai in range(3) for tap in range(3)], axis=1)
    d = dict(rP=pad(1536), kP=pad(2048), vP=pad(2560), dwT=sfm(3072, 32), daT=sfm(3104, 32), dgT=sfm(3136, 96),
             cw=np.ascontiguousarray(cw), wup=np.ascontiguousarray(prm["rw_w_up"].transpose(1, 0, 2)[:, :, cols]),
             aup=np.ascontiguousarray(prm["rw_a_up"].transpose(1, 0, 2)[:, :, cols]), gup=np.ascontiguousarray(prm["rw_g_up"][:, cols]),
             w0=np.ascontiguousarray(prm["rw_w0"][:, cols].T), a0=np.ascontiguousarray(prm["rw_a0"][:, cols].T),
             kkw=np.ascontiguousarray(prm["rw_k_k"][cols][:, None]), ka=np.ascontiguousarray(prm["rw_k_a"][cols][:, None]),
             rk=np.ascontiguousarray(prm["rw_r_k"].reshape(512)[cols][:, None]),
             gnw=np.ascontiguousarray(np.broadcast_to(prm["rw_gn_w"][cols][None, :], (64, 128))),
             gnb=np.ascontiguousarray(np.broadcast_to(prm["rw_gn_b"][cols][None, :], (64, 128))))
    d.update(cst)
    return d


TC = 8
NBLK = 33
_PROGS = {}


def _prog(name, fn):
    if name not in _PROGS:
        _PROGS[name] = fn()
    return _PROGS[name]


def _fm(v):
    return np.swapaxes(v.reshape(v.shape[:-1] + (KC, 128)), -1, -2)


def _tok_to_blocks(t):
    return np.ascontiguousarray(t.reshape(NBLK, NT, KC, 128).transpose(0, 3, 2, 1))


def _blocks_to_tok(b):
    return b.transpose(0, 3, 2, 1).reshape(NBLK * NT, D)


def _wchunks(w):
    n = w.shape[1]
    npad = -(-n // 128) * 128
    if npad != n:
        w = np.concatenate([w, np.zeros((w.shape[0], npad - n), np.float32)], axis=1)
    return np.ascontiguousarray(w.reshape(KC, 128, npad // 128, 128).transpose(2, 1, 0, 3))


def _wdchunks(w):
    return np.ascontiguousarray(w.reshape(2, 22, 128, KC, 128).transpose(0, 3, 2, 1, 4))


def _run(nc, ims):
    return run_bass_kernel_spmd(nc, ims, core_ids=list(range(len(ims)))).results


def kernel(x, c, ctx, c_ctx, w_ada, b_ada, norm_g, ffn1_wg, ffn1_wu, ffn1_wd, ffn2_wg, ffn2_wu, ffn2_wd,
           w_in, w_out, na_q_g, na_k_g, na_rpb, rw_conv, rw_w0, rw_w_up, rw_a0, rw_a_up, rw_g_up,
           rw_k_k, rw_k_a, rw_r_k, rw_gn_w, rw_gn_b, wa_q_g, wa_k_g, wa_sink, s5_a_re, s5_a_im,
           s5_log_dt, s5_b_re, s5_b_im, s5_c_re, s5_c_im, s5_d, s5_glu_w, s5_glu_b):
    f = lambda a: np.ascontiguousarray(np.asarray(a, dtype=np.float32))
    x, c, ctx, c_ctx = f(x), f(c), f(ctx), f(c_ctx)
    L = w_ada.shape[0]
    NB = -(-NBLK // TC)
    bidx = np.minimum(np.arange(TC * NB), NBLK - 1).reshape(TC, NB)
    c3 = np.stack([c[0], c[1], c_ctx])
    cT = np.ascontiguousarray(c3.reshape(3, KC, 128).transpose(2, 1, 0))
    w_ada, b_ada = np.asarray(w_ada), np.asarray(b_ada)
    ims = [dict(cT=cT, w=f(w_ada[:, :, i * ADA_N:(i + 1) * ADA_N]),
                b=f(np.broadcast_to(b_ada[:, None, i * ADA_N:(i + 1) * ADA_N], (L, 3, ADA_N)))) for i in range(8)]
    res = _run(_prog("ada", lambda: build_ada(L)), ims)
    mods = np.concatenate([r["mod"] for r in res], axis=-1).reshape(L, 3, NMOD, D)
    rows = np.array([0] * 16 + [1] * 16 + [2])

    toks = np.concatenate([x[0], x[1], ctx[0], ctx[1]], axis=0)
    xb = _tok_to_blocks(toks)
    cst_at = attn_consts()
    cst_rw = rw_consts()
    for l in range(L):
        ml = _fm(mods[l])
        mvb = ml[rows]
        mv1 = np.ascontiguousarray(mvb[:, 0:5].transpose(0, 2, 1, 3))
        mv2 = np.ascontiguousarray(mvb[:, 5:9].transpose(0, 2, 1, 3))
        gl = _fm(f(norm_g[l]))
        wg, wu, wd, win = _wchunks(f(ffn1_wg[l])), _wchunks(f(ffn1_wu[l])), _wdchunks(f(ffn1_wd[l])), _wchunks(f(w_in[l]))
        gv1 = np.ascontiguousarray(gl[0:2].transpose(1, 0, 2))
        ims = [dict(xT=np.ascontiguousarray(xb[bidx[i]]), mv=np.ascontiguousarray(mv1[bidx[i]]),
                    gv=gv1, wg=wg, wu=wu, wd=wd, win=win) for i in range(TC)]
        res = _run(_prog("t1", lambda: build_t1(NB)), ims)
        xb = np.concatenate([r["xo"] for r in res], axis=0)[:NBLK]
        pT = np.concatenate([r["pT"] for r in res], axis=0)[:NBLK]
        del wg, wu, wd, win, ims, res
        plat = [np.ascontiguousarray(pT[b * 16:(b + 1) * 16].transpose(1, 0, 2).reshape(DIN, TLAT)) for b in range(2)]
        pctx = [np.ascontiguousarray(pT[32][:, b * TCTX:(b + 1) * TCTX]) for b in range(2)]
        prm = dict(cst_at)
        prm.update(na_q_g=f(na_q_g[l]), na_k_g=f(na_k_g[l]), wa_q_g=f(wa_q_g[l]), wa_k_g=f(wa_k_g[l]), wa_sink=f(wa_sink[l]),
                   na_tabs=na_bias_tables(f(na_rpb[l])))
        lp = dict(rw_conv=f(rw_conv[l]), rw_w0=f(rw_w0[l]), rw_w_up=f(rw_w_up[l]), rw_a0=f(rw_a0[l]), rw_a_up=f(rw_a_up[l]),
                  rw_g_up=f(rw_g_up[l]), rw_k_k=f(rw_k_k[l]), rw_k_a=f(rw_k_a[l]), rw_r_k=f(rw_r_k[l]), rw_gn_w=f(rw_gn_w[l]),
                  rw_gn_b=f(rw_gn_b[l]), s5_a_re=f(s5_a_re[l]), s5_a_im=f(s5_a_im[l]), s5_log_dt=f(s5_log_dt[l]),
                  s5_b_re=f(s5_b_re[l]), s5_b_im=f(s5_b_im[l]), s5_c_re=f(s5_c_re[l]), s5_c_im=f(s5_c_im[l]), s5_d=f(s5_d[l]))
        cores = [(cid // 4, cid % 4) for cid in range(8)]
        rA = _run(_prog("A", lambda: build_attn("A")), [attn_inputs("A", plat[b], pctx[b], j, prm) for b, j in cores])
        rC = _run(_prog("C", lambda: build_attn("C")), [attn_inputs("C", plat[b], pctx[b], j, prm) for b, j in cores])
        rB = _run(_prog("rw", build_rwkv), [rwkv_inputs(plat[b], pctx[b], j, lp, cst_rw) for b, j in cores])
        rD = _run(_prog("s5", build_s5), [s5_inputs(plat[b][4000:4512], pctx[b][4000:4512], j, lp) for b, j in cores])
        olat = np.empty((2, TLAT, D), np.float32)
        octx = np.empty((2, TCTX, D), np.float32)
        for cid, (b, j) in enumerate(cores):
            cs = slice(j * 128, (j + 1) * 128)
            oa, oc_, ob, od = rA[cid]["o"], rC[cid]["o"], rB[cid]["o"], rD[cid]["yT"].T
            olat[b][:, 0:512][:, cs] = oa[:TLAT]
            octx[b][:, 0:512][:, cs] = oa[TLAT:]
            olat[b][:, 512:1024][:, cs] = ob[TCTX:]
            octx[b][:, 512:1024][:, cs] = ob[:TCTX]
            olat[b][:, 1024:1536][:, cs] = oc_[:TLAT]
            octx[b][:, 1024:1536][:, cs] = oc_[TLAT:]
            olat[b][:, 1536:2048][:, cs] = od[TCTX:]
            octx[b][:, 1536:2048][:, cs] = od[:TCTX]
        ob_ = _tok_to_blocks(np.concatenate([olat[0], olat[1], octx[0], octx[1]], axis=0))
        del pT, plat, pctx, rA, rB, rC, rD, olat, octx
        wg, wu, wd, wo = _wchunks(f(ffn2_wg[l])), _wchunks(f(ffn2_wu[l])), _wdchunks(f(ffn2_wd[l])), _wchunks(f(w_out[l]))
        gw = f(s5_glu_w[l])
        gb = np.ascontiguousarray(f(s5_glu_b[l]).reshape(4, 128).T)
        gv2 = np.ascontiguousarray(gl[2])
        ims = [dict(xT=np.ascontiguousarray(xb[bidx[i]]), oT=np.ascontiguousarray(ob_[bidx[i]]),
                    mv=np.ascontiguousarray(mv2[bidx[i]]), gv=gv2, gw=gw, gb=gb, wo=wo, wg=wg, wu=wu, wd=wd) for i in range(TC)]
        res = _run(_prog("t2", lambda: build_t2(NB)), ims)
        xb = np.concatenate([r["xo"] for r in res], axis=0)[:NBLK]
        del wg, wu, wd, wo, ims, res, ob_
    out = _blocks_to_tok(xb)[:2 * TLAT].reshape(2, TLAT, D)
    return np.ascontiguousarray(out.astype(np.float32))
```
